# Optimizing a Trainium2 kernel written in Bass

```python
import math
import jax, jax.numpy as jnp
from jax import lax
import numpy as np

D_MODEL = 2048
BATCH = 4
SEQ = 2048
DEPTH = 4

N_MEM = 256
N_MIXERS = 3
EPS = 1e-6
D_FF = ((8 * D_MODEL // 3 + 127) // 128) * 128

HG_HEAD = 128
HG_HEADS = D_MODEL // HG_HEAD
HG_WIDTH = HG_HEADS * HG_HEAD
HG_CHUNK = 64
F_FLOOR = 1e-12

LRU_WIDTH = D_MODEL
LRU_BLOCKS = 8
LRU_BLOCK = LRU_WIDTH // LRU_BLOCKS
CONV_W = 4
LRU_C = 8.0

ML_HEADS = 8
ML_DQK = D_MODEL // (2 * ML_HEADS)
ML_DV = D_MODEL // ML_HEADS
ML_HQK = ML_HEADS * ML_DQK
ML_HV = ML_HEADS * ML_DV
ML_IN = 2 * ML_HQK + 2 * ML_HV + 2 * ML_HEADS
ML_SPLITS = (ML_HQK, 2 * ML_HQK, 2 * ML_HQK + ML_HV, 2 * ML_HQK + 2 * ML_HV, 2 * ML_HQK + 2 * ML_HV + ML_HEADS)
ML_CHUNK = 64
GATE_CAP = 15.0
NEG_BIG = -1e30

XA_HEADS = 4
XA_HEAD = D_MODEL // XA_HEADS

N_A = (DEPTH + N_MIXERS - 1) // N_MIXERS
N_B = (DEPTH + N_MIXERS - 2) // N_MIXERS
N_C = DEPTH // N_MIXERS

kernel_name = "hybrid_hgrn2_rglru_mlstm_macaron"


def rmsnorm(x, g):
    xf = x.astype(jnp.float32)
    y = xf * lax.rsqrt(jnp.mean(xf * xf, axis=-1, keepdims=True) + EPS)
    return (y * g.astype(jnp.float32)).astype(x.dtype)


def head_rmsnorm(h, g):
    hf = h.astype(jnp.float32)
    return hf * lax.rsqrt(jnp.mean(hf * hf, axis=-1, keepdims=True) + EPS) * g.astype(jnp.float32)


def swiglu(x, w_gu, w_down):
    gate, up = jnp.split(x @ w_gu, 2, axis=-1)
    return (jax.nn.silu(gate) * up) @ w_down


def _chunk(t, n_chunks, heads, d):
    b = t.shape[0]
    return t.reshape(b, n_chunks, -1, heads, d).transpose(1, 0, 3, 2, 4)


def _chunk_gate(t, n_chunks, heads):
    b = t.shape[0]
    return t.reshape(b, n_chunks, -1, heads).transpose(1, 0, 3, 2)


def _unchunk(t):
    n, b, h, c, d = t.shape
    return t.transpose(1, 0, 3, 2, 4).reshape(b, n * c, h, d)


def hgrn2_mixer(x, w_in, g_norm, w_out, lb):
    bsz, seq, _ = x.shape
    n = seq // HG_CHUNK
    proj = (x @ w_in).astype(jnp.float32)
    q, f_pre, v, g = jnp.split(proj, 4, axis=-1)
    q = jax.nn.silu(q)
    f = lb + (1.0 - lb) * jax.nn.sigmoid(f_pre)
    log_f = jnp.log(jnp.maximum(f, F_FLOOR))
    k = (1.0 - lb) * jax.nn.sigmoid(-f_pre)
    qc, kc, vc, lfc = (_chunk(t, n, HG_HEADS, HG_HEAD) for t in (q, k, v, log_f))
    causal = jnp.tril(jnp.ones((HG_CHUNK, HG_CHUNK), dtype=bool))[:, :, None]

    def step(state, inp):
        q_c, k_c, v_c, lf_c = inp
        b = jnp.cumsum(lf_c, axis=2)
        diff = b[:, :, :, None, :] - b[:, :, None, :, :]
        decay = jnp.where(causal, jnp.exp(jnp.where(causal, diff, 0.0)), 0.0)
        scores = jnp.einsum('bhtd,bhsd,bhtsd->bhts', q_c, k_c, decay)
        o = (jnp.einsum('bhts,bhsv->bhtv', scores, v_c)
             + jnp.einsum('bhtd,bhdv->bhtv', q_c * jnp.exp(b), state))
        b_last = b[:, :, -1:, :]
        new_state = (jnp.exp(b_last[:, :, 0, :])[..., None] * state
                     + jnp.einsum('bhsd,bhsv->bhdv', k_c * jnp.exp(b_last - b), v_c))
        return new_state, o

    s0 = jnp.zeros((bsz, HG_HEADS, HG_HEAD, HG_HEAD), jnp.float32)
    _, o = lax.scan(step, s0, (qc, kc, vc, lfc))
    o = head_rmsnorm(_unchunk(o), g_norm).reshape(bsz, seq, HG_WIDTH) * jax.nn.silu(g)
    return o.astype(x.dtype) @ w_out


def _lin_combine(left, right):
    a_l, b_l = left
    a_r, b_r = right
    return a_l * a_r, a_r * b_l + b_r


def rglru_mixer(x, w_in, conv_w, conv_b, w_a, b_a, w_x, b_x, lam, w_out):
    bsz, seq, _ = x.shape
    gate_branch, u = jnp.split(x @ w_in, 2, axis=-1)
    u = lax.conv_general_dilated(u, conv_w[:, None, :], window_strides=(1,),
                                 padding=[(CONV_W - 1, 0)],
                                 dimension_numbers=('NWC', 'WIO', 'NWC'),
                                 feature_group_count=LRU_WIDTH) + conv_b
    u = u.astype(jnp.float32)
    ub = u.reshape(bsz, seq, LRU_BLOCKS, LRU_BLOCK)
    r = jax.nn.sigmoid(jnp.einsum('bsnc,ncd->bsnd', ub, w_a.astype(jnp.float32)).reshape(bsz, seq, LRU_WIDTH)
                       + b_a.astype(jnp.float32))
    i = jax.nn.sigmoid(jnp.einsum('bsnc,ncd->bsnd', ub, w_x.astype(jnp.float32)).reshape(bsz, seq, LRU_WIDTH)
                       + b_x.astype(jnp.float32))
    log_a = -LRU_C * r * jax.nn.softplus(-lam.astype(jnp.float32))
    a = jnp.exp(log_a)
    inp = jnp.sqrt(jnp.maximum(-jnp.expm1(2.0 * log_a), 0.0)) * (i * u)
    _, h = lax.associative_scan(_lin_combine, (a, inp), axis=1)
    y = jax.nn.gelu(gate_branch.astype(jnp.float32)) * h
    return y.astype(x.dtype) @ w_out


def _softcap(t):
    return GATE_CAP * jnp.tanh(t / GATE_CAP)


def mlstm_mixer(x, w_in, b_if, g_norm, w_out):
    bsz, seq, _ = x.shape
    n = seq // ML_CHUNK
    proj = (x @ w_in).astype(jnp.float32)
    q, k, v, o, ig, fg = jnp.split(proj, ML_SPLITS, axis=-1)
    b_if = b_if.astype(jnp.float32)
    ig = _softcap(ig + b_if[0])
    log_f = jax.nn.log_sigmoid(_softcap(fg + b_if[1]))
    k = k * (ML_DQK ** -0.5)
    qc = _chunk(q, n, ML_HEADS, ML_DQK)
    kc = _chunk(k, n, ML_HEADS, ML_DQK)
    vc = _chunk(v, n, ML_HEADS, ML_DV)
    igc = _chunk_gate(ig, n, ML_HEADS)
    lfc = _chunk_gate(log_f, n, ML_HEADS)
    causal = jnp.tril(jnp.ones((ML_CHUNK, ML_CHUNK), dtype=bool))

    def step(carry, inp):
        c_st, n_st, m_st = carry
        q_c, k_c, v_c, i_c, lf_c = inp
        b = jnp.cumsum(lf_c, axis=-1)
        d_mat = jnp.where(causal, b[..., :, None] - b[..., None, :] + i_c[..., None, :], NEG_BIG)
        inter = b + m_st[..., None]
        m_t = jnp.maximum(inter, jnp.max(d_mat, axis=-1))
        w_intra = jnp.where(causal, jnp.exp(jnp.minimum(d_mat - m_t[..., None], 0.0)), 0.0)
        w_inter = jnp.exp(inter - m_t)
        qk = jnp.einsum('bhtd,bhsd->bhts', q_c, k_c) * w_intra
        num = (jnp.einsum('bhts,bhsv->bhtv', qk, v_c)
               + w_inter[..., None] * jnp.einsum('bhtd,bhdv->bhtv', q_c, c_st))
        den = jnp.sum(qk, axis=-1) + w_inter * jnp.einsum('bhtd,bhd->bht', q_c, n_st)
        h = num / jnp.maximum(jnp.abs(den), jnp.exp(-m_t))[..., None]
        g = b[..., -1]
        upd = g[..., None] - b + i_c
        m_new = jnp.maximum(g + m_st, jnp.max(upd, axis=-1))
        w_upd = jnp.exp(upd - m_new[..., None])
        decay = jnp.exp(g + m_st - m_new)
        c_new = decay[..., None, None] * c_st + jnp.einsum('bhs,bhsd,bhsv->bhdv', w_upd, k_c, v_c)
        n_new = decay[..., None] * n_st + jnp.einsum('bhs,bhsd->bhd', w_upd, k_c)
        return (c_new, n_new, m_new), h

    carry0 = (jnp.zeros((bsz, ML_HEADS, ML_DQK, ML_DV), jnp.float32),
              jnp.zeros((bsz, ML_HEADS, ML_DQK), jnp.float32),
              jnp.zeros((bsz, ML_HEADS), jnp.float32))
    _, h = lax.scan(step, carry0, (qc, kc, vc, igc, lfc))
    h = head_rmsnorm(_unchunk(h), g_norm).reshape(bsz, seq, ML_HV) * jax.nn.sigmoid(o)
    return h.astype(x.dtype) @ w_out


def mem_cross_attention(x, mem_n, w_q, w_kv, w_o):
    bsz, seq, _ = x.shape
    q = (x @ w_q).reshape(bsz, seq, XA_HEADS, XA_HEAD)
    k, v = jnp.split(mem_n @ w_kv, 2, axis=-1)
    k = k.reshape(bsz, -1, XA_HEADS, XA_HEAD)
    v = v.reshape(bsz, -1, XA_HEADS, XA_HEAD)
    s = jnp.einsum('bshd,bmhd->bhsm', q, k).astype(jnp.float32) * (XA_HEAD ** -0.5)
    p = jax.nn.softmax(s, axis=-1).astype(v.dtype)
    o = jnp.einsum('bhsm,bmhd->bshd', p, v).reshape(bsz, seq, D_MODEL)
    return o @ w_o


def setup_inputs(seed: int = 0) -> dict:
    key = jax.random.key(seed)
    ks = iter(jax.random.split(key, 40))

    def nrm(shape, scale):
        return jax.random.normal(next(ks), shape, jnp.float32) * scale

    x = nrm((BATCH, SEQ, D_MODEL), 1.0)
    mem = nrm((BATCH, N_MEM, D_MODEL), 1.0)
    mem_norm_g = 1.0 + nrm((D_MODEL,), 0.02)
    norm_g = 1.0 + nrm((DEPTH, 4, D_MODEL), 0.02)
    final_norm_g = 1.0 + nrm((D_MODEL,), 0.02)
    ffn_w_gu = nrm((DEPTH, 2, D_MODEL, 2 * D_FF), D_MODEL ** -0.5)
    ffn_w_down = nrm((DEPTH, 2, D_FF, D_MODEL), D_FF ** -0.5)
    xa_w_q = nrm((DEPTH, D_MODEL, D_MODEL), D_MODEL ** -0.5)
    xa_w_kv = nrm((DEPTH, D_MODEL, 2 * D_MODEL), D_MODEL ** -0.5)
    xa_w_o = nrm((DEPTH, D_MODEL, D_MODEL), D_MODEL ** -0.5)
    hg_lb_param = nrm((DEPTH, HG_WIDTH), 0.1)
    hg_w_in = nrm((N_A, D_MODEL, 4 * HG_WIDTH), D_MODEL ** -0.5)
    hg_g_norm = 1.0 + nrm((N_A, HG_HEAD), 0.02)
    hg_w_out = nrm((N_A, HG_WIDTH, D_MODEL), HG_WIDTH ** -0.5)
    lru_w_in = nrm((N_B, D_MODEL, 2 * LRU_WIDTH), D_MODEL ** -0.5)
    lru_conv_w = nrm((N_B, CONV_W, LRU_WIDTH), CONV_W ** -0.5)
    lru_conv_b = nrm((N_B, LRU_WIDTH), 0.01)
    lru_w_a = nrm((N_B, LRU_BLOCKS, LRU_BLOCK, LRU_BLOCK), LRU_BLOCK ** -0.5)
    lru_b_a = nrm((N_B, LRU_WIDTH), 0.01)
    lru_w_x = nrm((N_B, LRU_BLOCKS, LRU_BLOCK, LRU_BLOCK), LRU_BLOCK ** -0.5)
    lru_b_x = nrm((N_B, LRU_WIDTH), 0.01)
    a0 = jax.random.uniform(next(ks), (N_B, LRU_WIDTH), jnp.float32, minval=0.9, maxval=0.999)
    lru_lambda = jnp.log(a0) - jnp.log1p(-a0)
    lru_w_out = nrm((N_B, LRU_WIDTH, D_MODEL), LRU_WIDTH ** -0.5)
    ml_w_in = nrm((N_C, D_MODEL, ML_IN), D_MODEL ** -0.5)
    ig_bias = nrm((N_C, ML_HEADS), 0.1)
    fg_bias = jnp.linspace(3.0, 6.0, ML_HEADS, dtype=jnp.float32)[None, :] + nrm((N_C, ML_HEADS), 0.1)
    ml_b_if = jnp.stack([ig_bias, fg_bias], axis=1)
    ml_g_norm = 1.0 + nrm((N_C, ML_DV), 0.02)
    ml_w_out = nrm((N_C, ML_HV, D_MODEL), ML_HV ** -0.5)
    return {
        "x": x, "mem": mem, "mem_norm_g": mem_norm_g, "norm_g": norm_g, "final_norm_g": final_norm_g,
        "ffn_w_gu": ffn_w_gu, "ffn_w_down": ffn_w_down,
        "xa_w_q": xa_w_q, "xa_w_kv": xa_w_kv, "xa_w_o": xa_w_o,
        "hg_lb_param": hg_lb_param, "hg_w_in": hg_w_in, "hg_g_norm": hg_g_norm, "hg_w_out": hg_w_out,
        "lru_w_in": lru_w_in, "lru_conv_w": lru_conv_w, "lru_conv_b": lru_conv_b,
        "lru_w_a": lru_w_a, "lru_b_a": lru_b_a, "lru_w_x": lru_w_x, "lru_b_x": lru_b_x,
        "lru_lambda": lru_lambda, "lru_w_out": lru_w_out,
        "ml_w_in": ml_w_in, "ml_b_if": ml_b_if, "ml_g_norm": ml_g_norm, "ml_w_out": ml_w_out,
    }


def reference(x, mem, mem_norm_g, norm_g, final_norm_g, ffn_w_gu, ffn_w_down,
              xa_w_q, xa_w_kv, xa_w_o,
              hg_lb_param, hg_w_in, hg_g_norm, hg_w_out,
              lru_w_in, lru_conv_w, lru_conv_b, lru_w_a, lru_b_a, lru_w_x, lru_b_x, lru_lambda, lru_w_out,
              ml_w_in, ml_b_if, ml_g_norm, ml_w_out):
    mem_n = rmsnorm(mem, mem_norm_g)
    lb_p = jax.nn.softmax(hg_lb_param.astype(jnp.float32), axis=0)
    lb_all = jnp.cumsum(lb_p, axis=0) - lb_p[0]

    for layer in range(DEPTH):
        kind = layer % N_MIXERS
        idx = layer // N_MIXERS
        x = x + 0.5 * swiglu(rmsnorm(x, norm_g[layer, 0]), ffn_w_gu[layer, 0], ffn_w_down[layer, 0])
        h = rmsnorm(x, norm_g[layer, 1])
        if kind == 0:
            h = hgrn2_mixer(h, hg_w_in[idx], hg_g_norm[idx], hg_w_out[idx], lb_all[layer])
        elif kind == 1:
            h = rglru_mixer(h, lru_w_in[idx], lru_conv_w[idx], lru_conv_b[idx], lru_w_a[idx], lru_b_a[idx],
                            lru_w_x[idx], lru_b_x[idx], lru_lambda[idx], lru_w_out[idx])
        else:
            h = mlstm_mixer(h, ml_w_in[idx], ml_b_if[idx], ml_g_norm[idx], ml_w_out[idx])
        x = x + h
        x = x + mem_cross_attention(rmsnorm(x, norm_g[layer, 2]), mem_n,
                                    xa_w_q[layer], xa_w_kv[layer], xa_w_o[layer])
        x = x + 0.5 * swiglu(rmsnorm(x, norm_g[layer, 3]), ffn_w_gu[layer, 1], ffn_w_down[layer, 1])
    return rmsnorm(x, final_norm_g)
```

```python
import numpy as np
import ml_dtypes
from contextlib import ExitStack
import concourse.bass as bass
import concourse.mybir as mybir
from concourse.bass_utils import run_bass_kernel_spmd

F32 = mybir.dt.float32
BF16 = mybir.dt.bfloat16
AF = mybir.ActivationFunctionType
ALU = mybir.AluOpType
AX = mybir.AxisListType
NPBF = ml_dtypes.bfloat16

SAME_ENGINE_SYNC = True

D = 2048
DC = 16
SEQ = 2048
BATCH = 4
NMEM = 256
DFF = 5504
FC = 43
EPS = 1e-6
DEPTH = 4
TA = 1024
GROUP = 8


class Buf:
    __slots__ = ("name", "last_w", "readers", "dma_readers", "dsem")

    def __init__(self, name):
        self.name = name
        self.last_w = None
        self.readers = {}
        self.dma_readers = []
        self.dsem = None


class Op:
    __slots__ = ("eng", "fn", "deps", "is_dma", "sem", "val", "milestone", "final_group", "inc")

    def __init__(self, eng, fn):
        self.eng = eng
        self.fn = fn
        self.deps = []
        self.is_dma = False
        self.sem = None
        self.val = 0
        self.milestone = False
        self.final_group = False
        self.inc = 16


class MK:
    ENGS = ("pe", "act", "dve", "pool", "sp")

    def __init__(self, nc):
        self.nc = nc
        self.es = ExitStack()
        self.ops = {e: [] for e in self.ENGS}
        self.esem = {}
        self.dsem_count = {}
        self.nsem = 0
        self.closers = []
        self.out_events = []
        self.nbuf = 0
        self.pes = None
        self.phase_sems = []
        self.free_sems = []
        self.live_async = {}
        self.nphase = 0
        self.rk = {}
        for e in self.ENGS:
            self.esem[e] = self.new_sem("eng_" + e, pooled=False)

    def new_sem(self, name, pooled=True):
        if pooled and self.free_sems:
            s = self.free_sems.pop()
        else:
            self.nsem += 1
            s = self.es.enter_context(self.nc.semaphore("%s_%d" % (name, self.nsem)))
            self.dsem_count[id(s)] = 0
        if pooled:
            self.phase_sems.append(s)
        return s

    def begin_phase(self):
        self.pes = ExitStack()
        self.nphase += 1

    def end_phase(self):
        frontier = []
        for e in self.ENGS:
            for op in reversed(self.ops[e]):
                if not op.is_dma and op.fn is not None:
                    frontier.append(op)
                    break
        frontier.extend(self.live_async.values())
        for e in self.ENGS:
            op = Op(e, None)
            op.deps = list(frontier)
            self.ops[e].append(op)
        self.live_async = {}
        self.free_sems.extend(self.phase_sems)
        self.phase_sems = []
        self.pes.close()
        self.pes = None

    def sbuf(self, name, shape, dtype):
        st = self.pes if self.pes is not None else self.es
        nm = name if self.pes is None else "%s_p%d" % (name, self.nphase)
        return st.enter_context(self.nc.sbuf_tensor(nm, list(shape), dtype))

    def psum(self, name, shape, dtype):
        return self.es.enter_context(self.nc.psum_tensor(name, list(shape), dtype))

    def buf(self, name=None):
        self.nbuf += 1
        return Buf(name or ("b%d" % self.nbuf))

    def bufs(self, name, *dims):
        if len(dims) == 1:
            return [self.buf("%s_%d" % (name, i)) for i in range(dims[0])]
        return [self.bufs("%s_%d" % (name, i), *dims[1:]) for i in range(dims[0])]

    def _record(self, op, reads, writes):
        deps = []
        for b in reads:
            if b.last_w is not None:
                deps.append(b.last_w)
        for b in writes:
            lw = b.last_w
            if lw is not None:
                if not (op.is_dma and lw.is_dma and lw.sem is op.sem and not b.readers and not b.dma_readers):
                    deps.append(lw)
            deps.extend(b.readers.values())
            deps.extend(b.dma_readers)
        op.deps = deps
        for b in reads:
            if op.is_dma:
                b.dma_readers.append(op)
            else:
                b.readers[op.eng] = op
        for b in writes:
            b.last_w = op
            b.readers = {}
            b.dma_readers = []
        self.ops[op.eng].append(op)
        return op

    def op(self, eng, meth, reads=(), writes=(), **kw):
        fn = (lambda e: getattr(e, meth)(**kw))
        return self._record(Op(eng, fn), list(reads), list(writes))

    def _async(self, op, reads, writes, sem, inc):
        if sem is None:
            b = writes[0] if writes else reads[0]
            if b.dsem is None:
                b.dsem = self.new_sem("d")
            sem = b.dsem
        op.is_dma = True
        op.sem = sem
        op.inc = inc
        self.dsem_count[id(sem)] += inc
        op.val = self.dsem_count[id(sem)]
        self.live_async[id(sem)] = op
        return self._record(op, reads, writes)

    def dma(self, eng, out_ap, in_ap, reads=(), writes=(), sem=None, final_group=False, in_fn=None):
        if in_fn is not None:
            op = Op(eng, lambda e: e.dma_start(out=out_ap, in_=in_fn(e)))
        else:
            op = Op(eng, lambda e: e.dma_start(out=out_ap, in_=in_ap))
        op.final_group = final_group
        return self._async(op, list(reads), list(writes), sem, 16)

    def cc_allgather(self, in_ap, out_ap, reads, writes, rg):
        op = Op("pool", lambda e: e.collective_compute("AllGather", ALU.bypass, replica_groups=rg, ins=[in_ap], outs=[out_ap]))
        sem = self.new_sem("cc", pooled=False)
        r = self._async(op, list(reads), list(writes), sem, 1)
        del self.live_async[id(sem)]
        return r

    def rank2(self, e, ename):
        if ename not in self.rk:
            self.rk[ename] = e.snap(e.partition_id() % 2)
        return self.rk[ename]

    def out_dma(self, eng, out_ap, in_ap, reads, sem):
        op = self.dma(eng, out_ap, in_ap, reads=reads, writes=(), sem=sem)
        self.out_events.append(op)
        return op

    def _skip(self, op, d):
        return (not d.is_dma) and d.eng == op.eng and (not op.is_dma) and (op.eng == "pe" or not SAME_ENGINE_SYNC)

    def finalize(self):
        for e in self.ENGS:
            for op in self.ops[e]:
                for d in op.deps:
                    if d.is_dma or self._skip(op, d):
                        continue
                    d.milestone = True
        for e in self.ENGS:
            n = 0
            for op in self.ops[e]:
                if op.is_dma or op.fn is None:
                    continue
                if op.milestone:
                    n += 1
                op.sem = self.esem[e]
                op.val = n
        nwaits = [0]
        with self.nc.Block() as block:
            def run(ename, eng):
                seen = {}
                for op in self.ops[ename]:
                    need = {}
                    for d in op.deps:
                        if self._skip(op, d):
                            continue
                        v = d.val
                        if d.is_dma and d.final_group:
                            v = self.dsem_count[id(d.sem)]
                        k = id(d.sem)
                        if seen.get(k, 0) >= v:
                            continue
                        if k not in need or need[k][1] < v:
                            need[k] = (d.sem, v)
                    for k, (s, v) in need.items():
                        eng.wait_ge(s, v)
                        seen[k] = v
                        nwaits[0] += 1
                    if op.fn is None:
                        continue
                    self.cur_ename = ename
                    ins = op.fn(eng)
                    if op.is_dma:
                        ins.then_inc(op.sem, op.inc)
                    elif op.milestone:
                        ins.then_inc(op.sem, 1)
                if ename == "sp":
                    fin = {}
                    for op in self.out_events:
                        fin[id(op.sem)] = (op.sem, self.dsem_count[id(op.sem)])
                    for s, v in fin.values():
                        eng.wait_ge(s, v)

            @block.sync
            def _(e):
                run("sp", e)

            @block.gpsimd
            def _(e):
                run("pool", e)

            @block.tensor
            def _(e):
                run("pe", e)

            @block.scalar
            def _(e):
                run("act", e)

            @block.vector
            def _(e):
                run("dve", e)
        self.stats = {e: len(self.ops[e]) for e in self.ENGS}
        self.stats["waits"] = nwaits[0]
        self.stats["sems"] = self.nsem
        for c in reversed(self.closers):
            c.close()
        self.es.close()


def tile_w(W):
    K, N = W.shape
    return np.ascontiguousarray(
        W.reshape(K // 128, 128, N // 128, 128).transpose(2, 1, 0, 3).reshape(N // 128, 128, K))


def fmv(v):
    n = v.shape[-1] // 128
    return np.ascontiguousarray(v.reshape(n, 128).T)


def consts_host():
    c = {}
    c["ident"] = np.eye(128, dtype=np.float32).astype(NPBF)
    s = np.arange(64)
    c["mask64"] = (s[:, None] <= s[None, :]).astype(np.float32)
    return c


class Prog:
    NSLOT = 6
    SLOT = 2048
    GLOBAL_INPUTS = ("ident", "mask64", "memg", "memT")

    def __init__(self, shared=None, pfx="", io=None):
        self.pfx = pfx
        self.io = io or {}
        if shared is None:
            self.root = self
            self.nc = bass.Bass("TRN2", target_bir_lowering=False)
            self.mk = MK(self.nc)
            mk = self.mk
            self.in_specs = {}
            self.gl_inputs = {}
            self.ps = [mk.psum("ps%d" % i, [128, 512], F32) for i in range(8)]
            self.out_sem = mk.new_sem("out", pooled=False)
            self.ones = mk.sbuf("ones", [128, 128], F32)
            self.eps_t = mk.sbuf("eps", [128, 1], F32)
            self.sq = mk.sbuf("sq", [128, 2, 512], F32)
            self.rstd = mk.sbuf("rstd", [128, 2, 512], F32)
        else:
            r = shared.root
            self.root = r
            for k in ("nc", "mk", "in_specs", "gl_inputs", "ps", "out_sem", "ones", "eps_t", "sq", "rstd"):
                setattr(self, k, getattr(r, k))

    def begin(self):
        mk = self.mk
        mk.begin_phase()
        self.wring = mk.sbuf("wring", [128, self.NSLOT, self.SLOT], BF16)
        self.wb = mk.bufs("w", self.NSLOT)
        self.psb = mk.bufs("ps", 8)
        self.onesb = mk.buf("ones")
        self.epsb = mk.buf("eps")
        self.sqb = mk.bufs("sq", 2)
        self.rstdb = mk.bufs("rstd", 2)
        self._slot = 0
        self._ps = 0
        self._rr = {}
        mk.op("dve", "memset", writes=[self.onesb], ap=self.ones[:], constant=1.0)
        mk.op("dve", "memset", writes=[self.epsb], ap=self.eps_t[:], constant=EPS)

    def din(self, name, shape, dtype=F32):
        if name in self.GLOBAL_INPUTS:
            if name not in self.gl_inputs:
                self.in_specs[name] = (tuple(shape), dtype)
                self.gl_inputs[name] = self.nc.dram_tensor(name, list(shape), dtype, kind="ExternalInput").ap()
            return self.gl_inputs[name]
        name = self.pfx + name
        self.in_specs[name] = (tuple(shape), dtype)
        return self.nc.dram_tensor(name, list(shape), dtype, kind="ExternalInput").ap()

    def dout(self, name, shape, dtype=F32):
        return self.nc.dram_tensor(self.pfx + name, list(shape), dtype, kind="ExternalOutput").ap()

    def rr(self, key, n):
        i = self._rr.get(key, 0)
        self._rr[key] = (i + 1) % n
        return i

    def next_ps(self):
        i = self._ps
        self._ps = (i + 1) % 5
        return self.ps[i], self.psb[i]

    def next_ps_long(self):
        i = 5 + self.rr("pslong", 3)
        return self.ps[i], self.psb[i]

    def wload(self, src_ap, n):
        i = self._slot
        self._slot = (i + 1) % self.NSLOT
        self.mk.dma("pool", self.wring[:, i, 0:n], src_ap, writes=[self.wb[i]])
        return self.wring[:, i, :], self.wb[i]

    def const(self, name, shape, dtype=F32, dram=None):
        t = self.mk.sbuf("c_" + name, shape, dtype)
        b = self.mk.buf("c_" + name)
        d = dram if dram is not None else self.din(name, shape, dtype)
        self.mk.dma("sp", t[:], d, writes=[b])
        return t, b

    def linear_fm(self, w_d, jlist, nk, src, srcb, ttiles, evac, k0=0):
        mk = self.mk
        for j in jlist:
            w, wbuf = self.wload(w_d[j, :, k0 * 128:(k0 + nk) * 128], nk * 128)
            for ti in ttiles:
                p, pb = self.next_ps()
                rhs0 = src(0, ti)
                ntok = rhs0.shape[-1]
                for k in range(nk):
                    mk.op("pe", "matmul", reads=[wbuf, srcb(k, ti)], writes=[pb],
                          out=p[:, 0:ntok], lhsT=w[:, k * 128:(k + 1) * 128], rhs=src(k, ti),
                          start=(k == 0), stop=(k == nk - 1))
                evac(j, ti, p, pb)

    def sumsq_bcast(self, srcs, ntok):
        mk = self.mk
        p, pb = self.next_ps()
        n = len(srcs)
        for c, (ap, b) in enumerate(srcs):
            i = self.rr("sq", 2)
            mk.op("act", "activation", reads=[b], writes=[self.sqb[i]],
                  out=self.sq[:, i, 0:ntok], in_=ap, func=AF.Square)
            mk.op("pe", "matmul", reads=[self.onesb, self.sqb[i]], writes=[pb],
                  out=p[:, 0:ntok], lhsT=self.ones[:], rhs=self.sq[:, i, 0:ntok], start=(c == 0), stop=(c == n - 1))
        return p, pb

    def rstd_from(self, p, pb, ntok, n_feat):
        mk = self.mk
        i = self.rr("rstd", 2)
        if getattr(self, "rstd_lnexp", False):
            mk.op("act", "activation", reads=[pb, self.epsb], writes=[self.rstdb[i]],
                  out=self.rstd[:, i, 0:ntok], in_=p[:, 0:ntok], func=AF.Ln, bias=self.eps_t[:, 0:1], scale=1.0 / n_feat)
            mk.op("act", "activation", reads=[self.rstdb[i]], writes=[self.rstdb[i]],
                  out=self.rstd[:, i, 0:ntok], in_=self.rstd[:, i, 0:ntok], func=AF.Exp, scale=-0.5)
            return self.rstd[:, i, 0:ntok], self.rstdb[i]
        mk.op("act", "activation", reads=[pb, self.epsb], writes=[self.rstdb[i]],
              out=self.rstd[:, i, 0:ntok], in_=p[:, 0:ntok], func=AF.Sqrt, bias=self.eps_t[:, 0:1], scale=1.0 / n_feat)
        mk.op("dve", "reciprocal", reads=[self.rstdb[i]], writes=[self.rstdb[i]],
              out=self.rstd[:, i, 0:ntok], in_=self.rstd[:, i, 0:ntok])
        return self.rstd[:, i, 0:ntok], self.rstdb[i]

    def make_eps(self):
        pass

    def rmsnorm(self, src, srcb, dst, dstb, g_ap, nchunk, ntok):
        mk = self.mk
        p, pb = self.sumsq_bcast([(src(c), srcb(c)) for c in range(nchunk)], ntok)
        r, rb = self.rstd_from(p, pb, ntok, nchunk * 128)
        for c in range(nchunk):
            mk.op("dve", "scalar_tensor_tensor", reads=[srcb(c), self.gb, rb], writes=[dstb(c)],
                  out=dst(c), in0=src(c), scalar=g_ap(c), in1=r, op0=ALU.mult, op1=ALU.mult)

    def finish(self):
        self.mk.end_phase()
        if self.root is self:
            self.mk.finalize()
        return self.nc


class ProgA(Prog):
    def __init__(self, first, last, shared=None, pfx="", io=None):
        super().__init__(shared, pfx, io)
        self.begin()
        self.first, self.last = first, last
        mk = self.mk
        io = self.io
        T = TA
        NTH = T // 512
        if "x_src" in io:
            xT_d, xsrcb = io["x_src"]
            xT_d = xT_d.rearrange("(c p) t -> p c t", p=128)
            xsrc_reads = [xsrcb]
        else:
            xT_d = self.din("xT", [D, T]).rearrange("(c p) t -> p c t", p=128)
            xsrc_reads = []
        ng_t, self.gb = self.const("ng", [128, 4, DC])
        x = mk.sbuf("x", [128, DC, T], F32)
        xb = mk.bufs("x", DC, NTH)
        xn = mk.sbuf("xn", [128, DC, T], BF16)
        xnb = mk.bufs("xn", DC, NTH)
        bigB = mk.sbuf("bigB", [128, DC, 512], BF16)
        bigBb = mk.bufs("bigB", DC)
        bigC = mk.sbuf("bigC", [128, DC, 512], BF16)
        bigCb = mk.bufs("bigC", DC)
        sg = mk.sbuf("sg", [128, 2, 512], F32)
        sgb = mk.bufs("sg", 2)
        self.x, self.xb, self.xn, self.xnb = x, xb, xn, xnb

        def load_x():
            for c in range(DC):
                for th in range(NTH):
                    mk.dma("sp", x[:, c, th * 512:(th + 1) * 512], xT_d[:, c, th * 512:(th + 1) * 512], reads=xsrc_reads, writes=[xb[c][th]])
        if first:
            load_x()

        def norm_x(gi):
            for th in range(NTH):
                tsl = slice(th * 512, (th + 1) * 512)
                self.rmsnorm(lambda c: x[:, c, tsl], lambda c: xb[c][th], lambda c: xn[:, c, tsl], lambda c: xnb[c][th],
                             lambda c: ng_t[:, gi, c:c + 1], DC, 512)

        def ffn(wgu_d, wdn_d):
            groups = [(g0, min(g0 + GROUP, FC)) for g0 in range(0, FC, GROUP)]
            for (g0, g1) in groups:
                for c in range(g0, g1):
                    cl = c - g0
                    wg, wgb = self.wload(wgu_d[c, :, :], DC * 128)
                    wu, wub = self.wload(wgu_d[FC + c, :, :], DC * 128)
                    for th in range(NTH):
                        tsl = slice(th * 512, (th + 1) * 512)
                        pg, pgb = self.next_ps()
                        for k in range(DC):
                            mk.op("pe", "matmul", reads=[wgb, xnb[k][th]], writes=[pgb], out=pg[:], lhsT=wg[:, k * 128:(k + 1) * 128],
                                  rhs=xn[:, k, tsl], start=(k == 0), stop=(k == DC - 1))
                        pu, pub = self.next_ps()
                        for k in range(DC):
                            mk.op("pe", "matmul", reads=[wub, xnb[k][th]], writes=[pub], out=pu[:], lhsT=wu[:, k * 128:(k + 1) * 128],
                                  rhs=xn[:, k, tsl], start=(k == 0), stop=(k == DC - 1))
                        i = self.rr("sg", 2)
                        mk.op("act", "activation", reads=[pgb], writes=[sgb[i]], out=sg[:, i, :], in_=pg[:], func=AF.Silu)
                        mk.op("dve", "tensor_tensor", reads=[sgb[i], pub], writes=[bigCb[cl * 2 + th]],
                              out=bigC[:, cl * 2 + th, :], in0=sg[:, i, :], in1=pu[:], op=ALU.mult)
                nk = g1 - g0
                for j in range(DC):
                    wd, wdb = self.wload(wdn_d[j, :, g0 * 128:g1 * 128], nk * 128)
                    for th in range(NTH):
                        tsl = slice(th * 512, (th + 1) * 512)
                        po, pob = self.next_ps()
                        for kk in range(nk):
                            mk.op("pe", "matmul", reads=[wdb, bigCb[kk * 2 + th]], writes=[pob], out=po[:],
                                  lhsT=wd[:, kk * 128:(kk + 1) * 128], rhs=bigC[:, kk * 2 + th, :], start=(kk == 0), stop=(kk == nk - 1))
                        mk.op("dve", "scalar_tensor_tensor", reads=[pob, xb[j][th]], writes=[xb[j][th]],
                              out=x[:, j, tsl], in0=po[:], scalar=0.5, in1=x[:, j, tsl], op0=ALU.mult, op1=ALU.add)

        def add_into_x(th):
            tsl = slice(th * 512, (th + 1) * 512)

            def evac(j, ti, p, pb):
                mk.op("dve", "tensor_tensor", reads=[pb, xb[j][th]], writes=[xb[j][th]],
                      out=x[:, j, tsl], in0=p[:], in1=x[:, j, tsl], op=ALU.add)
            return evac

        if not first:
            if "y_g" not in io:
                yT_d = self.din("yT", [D, T], BF16).rearrange("(c p) t -> p c t", p=128)
            memT_d = self.din("memT", [D, NMEM]).rearrange("(c p) t -> p c t", p=128)
            mg_t, mgb = self.const("memg", [128, DC])
            ident, identb = self.const("ident", [128, 128], BF16)
            wmo_d = self.din("w_mo", [DC, 128, D])
            wq_d = self.din("w_q", [DC, 128, D])
            wkv_d = self.din("w_kv", [2 * DC, 128, D])
            wo_d = self.din("w_o", [DC, 128, D])
            wgu2_d = self.din("w_gu2", [2 * FC, 128, D])
            wdn2_d = self.din("w_dn2", [DC, 128, DFF])
            kT = mk.sbuf("kT", [128, DC, NMEM], BF16)
            kTb = mk.bufs("kT", DC)
            vtm = mk.sbuf("vtm", [128, 2, D], BF16)
            vtmb = mk.bufs("vtm", 2, 4)
            mst = mk.sbuf("mst", [128, 2, NMEM], F32)
            mstb = mk.bufs("mst", 2)
            pT = mk.sbuf("pT", [128, 2, 2, 512], BF16)
            pTb = mk.bufs("pT", 2)
            sm = mk.sbuf("sm", [128, 2, NMEM], F32)
            smb = mk.bufs("sm", 2)
            pbf = mk.sbuf("pbf", [128, 2, NMEM], BF16)
            pbfb = mk.bufs("pbf", 2)
            st1 = mk.sbuf("st1", [128, 8], F32)
            st1b = mk.bufs("st1", 2)
            p, pb = self.next_ps()
            for c in range(DC):
                i = self.rr("mst", 2)
                mk.dma("sp", mst[:, i, :], memT_d[:, c, :], writes=[mstb[i]])
                k = self.rr("sq", 2)
                mk.op("act", "activation", reads=[mstb[i]], writes=[self.sqb[k]], out=self.sq[:, k, 0:NMEM], in_=mst[:, i, :], func=AF.Square)
                mk.op("pe", "matmul", reads=[self.onesb, self.sqb[k]], writes=[pb], out=p[:, 0:NMEM], lhsT=self.ones[:],
                      rhs=self.sq[:, k, 0:NMEM], start=(c == 0), stop=(c == DC - 1))
            r, rb = self.rstd_from(p, pb, NMEM, D)
            for c in range(DC):
                i = self.rr("mst", 2)
                mk.dma("sp", mst[:, i, :], memT_d[:, c, :], writes=[mstb[i]])
                mk.op("dve", "scalar_tensor_tensor", reads=[mstb[i], mgb, rb], writes=[bigBb[c]], out=bigB[:, c, 0:NMEM],
                      in0=mst[:, i, :], scalar=mg_t[:, c:c + 1], in1=r, op0=ALU.mult, op1=ALU.mult)

            load_x()

            def k_item(j):
                w, wbuf = self.wload(wkv_d[j, :, :], D)
                p, pb = self.next_ps()
                for k in range(DC):
                    mk.op("pe", "matmul", reads=[wbuf, bigBb[k]], writes=[pb], out=p[:, 0:NMEM], lhsT=w[:, k * 128:(k + 1) * 128],
                          rhs=bigB[:, k, 0:NMEM], start=(k == 0), stop=(k == DC - 1))
                mk.op("act", "activation", reads=[pb], writes=[kTb[j]], out=kT[:, j, :], in_=p[:, 0:NMEM], func=AF.Copy)

            def v_item(jg):
                ws = [self.wload(wkv_d[DC + jg * 4 + jj, :, :], D) for jj in range(4)]
                for mc in range(2):
                    p, pb = self.next_ps()
                    for jj in range(4):
                        w, wbuf = ws[jj]
                        for k in range(DC):
                            mk.op("pe", "matmul", reads=[wbuf, bigBb[k]], writes=[pb], out=p[:, jj * 128:(jj + 1) * 128],
                                  lhsT=bigB[:, k, mc * 128:(mc + 1) * 128], rhs=w[:, k * 128:(k + 1) * 128], start=(k == 0), stop=(k == DC - 1))
                    mk.op("act", "activation", reads=[pb], writes=[vtmb[mc][jg]], out=vtm[:, mc, jg * 512:(jg + 1) * 512], in_=p[:], func=AF.Copy)

            scale = 512.0 ** -0.5
            THS = list(range(NTH))

            def tsl_(th):
                return slice(th * 512, (th + 1) * 512)
            for th in THS:
                tsl = tsl_(th)
                for c in range(DC):
                    if "y_g" in io:
                        r_, rem = divmod(c, 8)
                        pc, i_ = divmod(rem, 4)
                        yg_ap, ygb = io["y_g"][pc]
                        row0 = r_ * 512 + i_ * 128

                        def in_fn(e, yg_ap=yg_ap, row0=row0, th=th):
                            rk = mk.rank2(e, "sp")
                            return yg_ap[row0:row0 + 128, bass.ds(rk * TA + th * 512, 512)]
                        mk.dma("sp", xn[:, c, tsl], None, reads=[ygb], writes=[xnb[c][th]], in_fn=in_fn)
                    else:
                        mk.dma("sp", xn[:, c, tsl], yT_d[:, c, tsl], writes=[xnb[c][th]])

            def evac_add(j, th, p, pb):
                mk.op("dve", "tensor_tensor", reads=[pb, xb[j][th]], writes=[xb[j][th]],
                      out=x[:, j, tsl_(th)], in0=p[:], in1=x[:, j, tsl_(th)], op=ALU.add)
            kv_items = [lambda jg=jg: v_item(jg) for jg in range(2)] + [lambda j=j: k_item(j) for j in range(DC)] + \
                       [lambda jg=jg: v_item(jg) for jg in range(2, 4)]
            for it in kv_items[:6]:
                it()
            rest = kv_items[6:]
            for j in range(DC):
                self.linear_fm(wmo_d, [j], DC, lambda k, th: xn[:, k, tsl_(th)], lambda k, th: xnb[k][th], THS, evac_add)
                if rest:
                    rest.pop(0)()
            for it in rest:
                it()
            for th in THS:
                tsl = tsl_(th)
                self.rmsnorm(lambda c: x[:, c, tsl], lambda c: xb[c][th], lambda c: xn[:, c, tsl], lambda c: xnb[c][th],
                             lambda c: ng_t[:, 0, c:c + 1], DC, 512)
            qbuf = [(bigB, bigBb), (bigC, bigCb)]

            def evac_q(j, th, p, pb):
                qd, qdb = qbuf[th]
                mk.op("act", "activation", reads=[pb], writes=[qdb[j]], out=qd[:, j, :], in_=p[:], func=AF.Copy)
            self.linear_fm(wq_d, range(DC), DC, lambda k, th: xn[:, k, tsl_(th)], lambda k, th: xnb[k][th], THS, evac_q)
            for th in THS:
                tsl = tsl_(th)
                qd, qdb = qbuf[th]
                for hd in range(4):
                    pi = self.rr("pT", 2)
                    for tt in range(4):
                        ps_, psb_ = self.next_ps()
                        for kc in range(4):
                            ch = hd * 4 + kc
                            mk.op("pe", "matmul", reads=[qdb[ch], kTb[ch]], writes=[psb_], out=ps_[:, 0:NMEM],
                                  lhsT=qd[:, ch, tt * 128:(tt + 1) * 128], rhs=kT[:, ch, :], start=(kc == 0), stop=(kc == 3))
                        si = self.rr("st1", 2)
                        mk.op("dve", "reduce_max", reads=[psb_], writes=[st1b[si]], out=st1[:, si * 4:si * 4 + 1], in_=ps_[:, 0:NMEM], axis=AX.X)
                        mk.op("dve", "tensor_scalar", reads=[st1b[si]], writes=[st1b[si]], out=st1[:, si * 4 + 1:si * 4 + 2],
                              in0=st1[:, si * 4:si * 4 + 1], scalar1=-scale, scalar2=None, op0=ALU.mult)
                        mi = self.rr("sm", 2)
                        mk.op("act", "activation", reads=[psb_, st1b[si]], writes=[smb[mi], st1b[si]], out=sm[:, mi, :], in_=ps_[:, 0:NMEM],
                              func=AF.Exp, bias=st1[:, si * 4 + 1:si * 4 + 2], scale=scale, accum_out=st1[:, si * 4 + 2:si * 4 + 3])
                        mk.op("dve", "reciprocal", reads=[st1b[si]], writes=[st1b[si]], out=st1[:, si * 4 + 3:si * 4 + 4],
                              in_=st1[:, si * 4 + 2:si * 4 + 3])
                        mk.op("dve", "tensor_scalar", reads=[smb[mi], st1b[si]], writes=[pbfb[mi]], out=pbf[:, mi, :], in0=sm[:, mi, :],
                              scalar1=st1[:, si * 4 + 3:si * 4 + 4], scalar2=None, op0=ALU.mult)
                        pt_, ptb_ = self.next_ps()
                        ptv = pt_.bitcast(BF16)
                        for mc in range(2):
                            mk.op("pe", "transpose", reads=[pbfb[mi], identb], writes=[ptb_], out=ptv[:, mc * 128:(mc + 1) * 128],
                                  in_=pbf[:, mi, mc * 128:(mc + 1) * 128], identity=ident[:])
                        for mc in range(2):
                            mk.op("act", "activation", reads=[ptb_], writes=[pTb[pi]], out=pT[:, pi, mc, tt * 128:(tt + 1) * 128],
                                  in_=ptv[:, mc * 128:(mc + 1) * 128], func=AF.Copy)
                    for dc in range(4):
                        ch = hd * 4 + dc
                        po, pob = self.next_ps()
                        for mc in range(2):
                            mk.op("pe", "matmul", reads=[vtmb[mc][hd], pTb[pi]], writes=[pob], out=po[:],
                                  lhsT=vtm[:, mc, ch * 128:(ch + 1) * 128], rhs=pT[:, pi, mc, :], start=(mc == 0), stop=(mc == 1))
                        mk.op("act", "activation", reads=[pob], writes=[xnb[ch][th]], out=xn[:, ch, tsl], in_=po[:], func=AF.Copy)
            self.linear_fm(wo_d, range(DC), DC, lambda k, th: xn[:, k, tsl_(th)], lambda k, th: xnb[k][th], THS, evac_add)
            norm_x(1)
            ffn(wgu2_d, wdn2_d)

        if not last:
            wgu1_d = self.din("w_gu1", [2 * FC, 128, D])
            wdn1_d = self.din("w_dn1", [DC, 128, DFF])
            norm_x(2)
            ffn(wgu1_d, wdn1_d)
            norm_x(3)
            if "x_dst" in io:
                for pc in range(2):
                    (hp, hpb), (hg, hgb) = io["hn_p"][pc], io["hn_g"][pc]
                    hpv = hp.rearrange("(c p) t -> p c t", p=128)
                    for c in range(DC):
                        mk.dma("sp", hpv[:, c, :], xn[:, c, pc * 512:(pc + 1) * 512], reads=[xnb[c][pc]], writes=[hpb])
                    mk.cc_allgather(hp, hg, reads=[hpb], writes=[hgb], rg=io["rg"])
                xd, xdb = io["x_dst"]
                xd = xd.rearrange("(c p) t -> p c t", p=128)
                for c in range(DC):
                    mk.dma("sp", xd[:, c, :], x[:, c, :], reads=xb[c], writes=[xdb])
            else:
                xo_d = self.dout("xT_out", [D, T]).rearrange("(c p) t -> p c t", p=128)
                hn_d = self.dout("hnT_out", [D, T], BF16).rearrange("(c p) t -> p c t", p=128)
                for c in range(DC):
                    mk.out_dma("sp", xo_d[:, c, :], x[:, c, :], reads=xb[c], sem=self.out_sem)
                    mk.out_dma("sp", hn_d[:, c, :], xn[:, c, :], reads=xnb[c], sem=self.out_sem)
        else:
            fo_d = self.dout("outT", [D, T]).rearrange("(c p) t -> p c t", p=128)
            for th in range(NTH):
                tsl = slice(th * 512, (th + 1) * 512)
                p, pb = self.sumsq_bcast([(x[:, c, tsl], xb[c][th]) for c in range(DC)], 512)
                r, rb = self.rstd_from(p, pb, 512, D)
                for c in range(DC):
                    mk.op("dve", "scalar_tensor_tensor", reads=[xb[c][th], self.gb, rb], writes=[xb[c][th]],
                          out=x[:, c, tsl], in0=x[:, c, tsl], scalar=ng_t[:, 2, c:c + 1], in1=r, op0=ALU.mult, op1=ALU.mult)
            for c in range(DC):
                mk.out_dma("sp", fo_d[:, c, :], x[:, c, :], reads=xb[c], sem=self.out_sem)
        self.finish()


def bc_chunks(t, col, nch, clen, pstep=None):
    base = t[:, 0:1]
    ps = base.ap[0][0]
    return bass.AP(t, base.offset + col, [[ps, 128], [clen, nch], [0, clen]])


def v3(ap2, nch, clen):
    return ap2.rearrange("p (c i) -> p c i", i=clen)


class ProgM(Prog):
    NSLOT = 4

    def __init__(self, shared=None, pfx="", io=None):
        super().__init__(shared, pfx, io)
        self.begin()
        mk = self.mk
        io = self.io
        self.hn = mk.sbuf("hn", [128, DC, SEQ], BF16)
        self.hnb = mk.bufs("hn", DC, 4)
        if "hn_g" in io:
            for ti in range(4):
                r_, pc = divmod(ti, 2)
                hg, hgb = io["hn_g"][pc]
                hgv = hg[r_ * D:(r_ + 1) * D, :].rearrange("(c p) t -> p c t", p=128)
                for c in range(DC):
                    mk.dma("sp", self.hn[:, c, ti * 512:(ti + 1) * 512], hgv[:, c, :], reads=[hgb], writes=[self.hnb[c][ti]])
            self.y_cnt = [0, 0]
        else:
            hnT_d = self.din("hnT", [D, SEQ], BF16).rearrange("(c p) t -> p c t", p=128)
            for c in range(DC):
                for ti in range(4):
                    mk.dma("sp", self.hn[:, c, ti * 512:(ti + 1) * 512], hnT_d[:, c, ti * 512:(ti + 1) * 512], writes=[self.hnb[c][ti]])
            self.y_d = self.dout("yT_out", [D // 2, SEQ], BF16).rearrange("(c p) t -> p c t", p=128)
        self.ybuf = mk.sbuf("ybuf", [128, 2, 512], BF16)
        self.ybufb = mk.bufs("ybuf", 2)

    def hsrc(self):
        return (lambda k, ti: self.hn[:, k, ti * 512:(ti + 1) * 512]), (lambda k, ti: self.hnb[k][ti])

    def store_y(self, ch, ti, i):
        io = self.io
        if "y_p" in io:
            pc, cl = divmod(ch, 4)
            (yp, ypb), (yg, ygb) = io["y_p"][pc], io["y_g"][pc]
            self.mk.dma("sp", yp[cl * 128:(cl + 1) * 128, ti * 512:(ti + 1) * 512], self.ybuf[:, i, :], reads=[self.ybufb[i]], writes=[ypb])
            self.y_cnt[pc] += 1
            if self.y_cnt[pc] == 16:
                self.mk.cc_allgather(yp, yg, reads=[ypb], writes=[ygb], rg=io["rg"])
        else:
            self.mk.out_dma("sp", self.y_d[:, ch, ti * 512:(ti + 1) * 512], self.ybuf[:, i, :], reads=[self.ybufb[i]], sem=self.out_sem)


class ProgHGRN(ProgM):
    NSLOT = 8

    def __init__(self, layer, shared=None, pfx="", io=None):
        super().__init__(shared, pfx, io)
        mk = self.mk
        NH = 8
        win_d = self.din("w_in", [4 * NH, 128, D])
        lbp, lbpb = self.const("lbp", [128, 4, NH])
        gn, gnb = self.const("gn", [128, 1])
        self.gb = gnb
        ident, identb = self.const("ident", [128, 128], BF16)
        mask, maskb = self.const("mask64", [64, 64])
        sm = mk.sbuf("lbs", [128, 8, NH], F32)
        smb = mk.buf("lbs")
        R_ = [lbpb, smb]

        def tt(o, a, b, op):
            mk.op("dve", "tensor_tensor", reads=R_, writes=[smb], out=o, in0=a, in1=b, op=op)
        tt(sm[:, 0, :], lbp[:, 0, :], lbp[:, 1, :], ALU.max)
        tt(sm[:, 0, :], sm[:, 0, :], lbp[:, 2, :], ALU.max)
        tt(sm[:, 0, :], sm[:, 0, :], lbp[:, 3, :], ALU.max)
        for i in range(4):
            tt(sm[:, 1 + i, :], lbp[:, i, :], sm[:, 0, :], ALU.subtract)
            mk.op("act", "activation", reads=[smb], writes=[smb], out=sm[:, 1 + i, :], in_=sm[:, 1 + i, :], func=AF.Exp)
        tt(sm[:, 5, :], sm[:, 1, :], sm[:, 2, :], ALU.add)
        tt(sm[:, 5, :], sm[:, 5, :], sm[:, 3, :], ALU.add)
        tt(sm[:, 5, :], sm[:, 5, :], sm[:, 4, :], ALU.add)
        mk.op("dve", "reciprocal", reads=[smb], writes=[smb], out=sm[:, 5, :], in_=sm[:, 5, :])
        mk.op("dve", "memset", reads=[smb], writes=[smb], ap=sm[:, 6, :], constant=0.0)
        for i in range(1, layer + 1):
            tt(sm[:, 6, :], sm[:, 6, :], sm[:, 1 + i, :], ALU.add)
        tt(sm[:, 6, :], sm[:, 6, :], sm[:, 5, :], ALU.mult)
        mk.op("dve", "tensor_scalar", reads=[smb], writes=[smb], out=sm[:, 7, :], in0=sm[:, 6, :], scalar1=-1.0, scalar2=1.0,
              op0=ALU.mult, op1=ALU.add)
        mk.op("dve", "tensor_scalar", reads=[smb], writes=[smb], out=sm[:, 0, :], in0=sm[:, 7, :], scalar1=-1.0, scalar2=None,
              op0=ALU.mult)
        LB, OML, NOML = 6, 7, 0

        def tbuf(name, dt=F32):
            return mk.sbuf(name, [128, 2, 512], dt), mk.bufs(name, 2)

        def pbuf(name, dt=BF16):
            return mk.sbuf(name, [128, SEQ], dt), mk.bufs(name, 4)
        qs, qsb = tbuf("qs")
        sg_, sgb_ = tbuf("sgm")
        kk, kkb = tbuf("kk")
        bb, bbb = tbuf("bb")
        t1, t1b = tbuf("t1")
        ex, exb = tbuf("ex")
        oo, oob = tbuf("oo")
        vf, vfb = tbuf("vf", BF16)
        qt, qtb = pbuf("qt")
        kt, ktb = pbuf("kt")
        qh, qhb = pbuf("qh")
        kh, khb = pbuf("kh")
        gs, gsb = pbuf("gs")
        vtm = mk.sbuf("vtm", [64, 32, 128], BF16)
        vtmb = mk.bufs("vtm", 4)
        ktm = mk.sbuf("ktm", [64, 32, 128], BF16)
        ktmb = mk.bufs("ktm", 4)
        scs = mk.sbuf("scs", [64, 2, 8, 64], BF16)
        scsb = mk.bufs("scs", 2)
        dec = mk.sbuf("dec", [128, 32], F32)
        decb = mk.bufs("dec", 4)
        S = mk.sbuf("S", [128, 2, 128], F32)
        Sb = mk.bufs("S", 2)
        Sall = mk.sbuf("Sall", [128, 33, 128], BF16)
        Sallb = mk.bufs("Sall", 4)
        S0b = mk.buf("Sall0")
        mk.op("dve", "memset", writes=[S0b], ap=Sall[:, 0, :], constant=0.0)
        onesr = mk.sbuf("onesr", [128, 64], F32)
        onesrb = mk.buf("onesr")
        mk.op("dve", "memset", writes=[onesrb], ap=onesr[:], constant=1.0)
        tmp = mk.sbuf("tmpy", [128, 2, 512], F32)
        tmpb = mk.bufs("tmpy", 2)
        wt = {}
        st1 = {}

        def sl(ti):
            return slice(ti * 512, (ti + 1) * 512)

        def stage1(hd, ti):
            if ti == 0:
                wt[hd] = [self.wload(win_d[kind * NH + hd, :, :], D) for kind in range(4)]
            i = self.rr("tl", 2)
            s_ = sl(ti)

            def proj(kind, evac):
                w, wbuf = wt[hd][kind]
                p, pb = self.next_ps()
                for k in range(DC):
                    mk.op("pe", "matmul", reads=[wbuf, self.hnb[k][ti]], writes=[pb], out=p[:], lhsT=w[:, k * 128:(k + 1) * 128],
                          rhs=self.hn[:, k, s_], start=(k == 0), stop=(k == DC - 1))
                evac(p, pb)
            proj(1, lambda p, pb: mk.op("act", "activation", reads=[pb], writes=[sgb_[i]], out=sg_[:, i, :], in_=p[:], func=AF.Sigmoid))
            proj(0, lambda p, pb: mk.op("act", "activation", reads=[pb], writes=[qsb[i]], out=qs[:, i, :], in_=p[:], func=AF.Silu))
            mk.op("dve", "tensor_scalar", reads=[sgb_[i], smb], writes=[kkb[i]], out=kk[:, i, :], in0=sg_[:, i, :],
                  scalar1=sm[:, NOML, hd:hd + 1], scalar2=sm[:, OML, hd:hd + 1], op0=ALU.mult, op1=ALU.add)
            mk.op("dve", "tensor_scalar", reads=[sgb_[i], smb], writes=[sgb_[i]], out=sg_[:, i, :], in0=sg_[:, i, :],
                  scalar1=sm[:, OML, hd:hd + 1], scalar2=sm[:, LB, hd:hd + 1], op0=ALU.mult, op1=ALU.add)
            mk.op("dve", "tensor_scalar", reads=[sgb_[i]], writes=[sgb_[i]], out=sg_[:, i, :], in0=sg_[:, i, :],
                  scalar1=1e-12, scalar2=None, op0=ALU.max)
            mk.op("act", "activation", reads=[sgb_[i]], writes=[sgb_[i]], out=sg_[:, i, :], in_=sg_[:, i, :], func=AF.Ln)
            proj(2, lambda p, pb: mk.op("act", "activation", reads=[pb], writes=[vfb[i]], out=vf[:, i, :], in_=p[:], func=AF.Copy))
            for c in range(8):
                cs = slice(c * 64, (c + 1) * 64)
                mk.op("dve", "tensor_tensor_scan", reads=[sgb_[i], onesrb], writes=[bbb[i]], out=bb[:, i, cs], data0=onesr[:, :],
                      data1=sg_[:, i, cs], initial=0.0, op0=ALU.mult, op1=ALU.add)
            pstep = bb[:, 0, 0:1].ap[0][0]
            b3 = v3(bb[:, i, :], 8, 64)
            bref = bass.AP(bb, bb[:, i, 31:32].offset, [[pstep, 128], [64, 8], [0, 64]])
            blast = bass.AP(bb, bb[:, i, 63:64].offset, [[pstep, 128], [64, 8], [0, 64]])
            bl2 = bass.AP(bb, bb[:, i, 63:64].offset, [[pstep, 128], [64, 8]])
            mk.op("dve", "tensor_tensor", reads=[bbb[i]], writes=[t1b[i]], out=v3(t1[:, i, :], 8, 64), in0=b3, in1=bref, op=ALU.subtract)
            mk.op("act", "activation", reads=[t1b[i]], writes=[exb[i]], out=ex[:, i, :], in_=t1[:, i, :], func=AF.Exp)
            mk.op("dve", "tensor_tensor", reads=[exb[i], qsb[i]], writes=[qtb[ti]], out=qt[:, s_], in0=ex[:, i, :], in1=qs[:, i, :], op=ALU.mult)
            mk.op("act", "activation", reads=[t1b[i]], writes=[exb[i]], out=ex[:, i, :], in_=t1[:, i, :], func=AF.Exp, scale=-1.0)
            mk.op("dve", "tensor_tensor", reads=[exb[i], kkb[i]], writes=[ktb[ti]], out=kt[:, s_], in0=ex[:, i, :], in1=kk[:, i, :], op=ALU.mult)
            proj(3, lambda p, pb: mk.op("act", "activation", reads=[pb], writes=[gsb[ti]], out=gs[:, s_], in_=p[:], func=AF.Silu))
            mk.op("act", "activation", reads=[bbb[i]], writes=[exb[i]], out=ex[:, i, :], in_=bb[:, i, :], func=AF.Exp)
            mk.op("dve", "tensor_tensor", reads=[exb[i], qsb[i]], writes=[qhb[ti]], out=qh[:, s_], in0=ex[:, i, :], in1=qs[:, i, :], op=ALU.mult)
            mk.op("dve", "tensor_tensor", reads=[bbb[i]], writes=[t1b[i]], out=v3(t1[:, i, :], 8, 64), in0=blast, in1=b3, op=ALU.subtract)
            mk.op("act", "activation", reads=[t1b[i]], writes=[exb[i]], out=ex[:, i, :], in_=t1[:, i, :], func=AF.Exp)
            mk.op("dve", "tensor_tensor", reads=[exb[i], kkb[i]], writes=[khb[ti]], out=kh[:, s_], in0=ex[:, i, :], in1=kk[:, i, :], op=ALU.mult)
            mk.op("act", "activation", reads=[bbb[i]], writes=[decb[ti]], out=dec[:, ti * 8:(ti + 1) * 8], in_=bl2, func=AF.Exp)
            st1[(hd, ti)] = i

        st2 = {}

        def stage2a(hd, ti):
            s_ = sl(ti)
            if ti == 0:
                mk.op("dve", "memset", writes=[Sb[1]], ap=S[:, 1, :], constant=0.0)
            i1 = st1.pop((hd, ti))
            for (srcf, srcfb, dstt, dsttb) in ((kh[:, s_], khb[ti], ktm, ktmb), (vf[:, i1, :], vfb[i1], vtm, vtmb)):
                p, pb = self.next_ps()
                pv = p.bitcast(BF16)
                for c in range(8):
                    mk.op("pe", "transpose", reads=[srcfb, identb], writes=[pb], out=pv[0:64, c * 128:(c + 1) * 128],
                          in_=srcf[:, c * 64:(c + 1) * 64], identity=ident[:])
                mk.op("act", "activation", reads=[pb], writes=[dsttb[ti]], out=dstt[:, ti * 8:(ti + 1) * 8, :],
                      in_=pv[0:64, :].rearrange("p (c i) -> p c i", i=128), func=AF.Copy)

            p, pb = self.next_ps()
            for c in range(8):
                cs = slice(ti * 512 + c * 64, ti * 512 + (c + 1) * 64)
                mk.op("pe", "matmul", reads=[ktb[ti], qtb[ti]], writes=[pb], out=p[0:64, c * 64:(c + 1) * 64], lhsT=kt[:, cs], rhs=qt[:, cs],
                      start=True, stop=True)
            si = self.rr("scs", 2)
            mbc = bass.AP(mask, mask[:, 0:1].offset, [[mask[:, 0:1].ap[0][0], 64], [0, 8], [1, 64]])
            mk.op("dve", "tensor_tensor", reads=[pb, maskb], writes=[scsb[si]], out=scs[:, si, :, :], in0=v3(p[0:64, :], 8, 64), in1=mbc, op=ALU.mult)
            pst = [self.next_ps(), self.next_ps()]
            for c in range(8):
                cg = ti * 8 + c
                pp, ppb = pst[c // 4]
                mk.op("pe", "matmul", reads=[ktmb[ti], vtmb[ti]], writes=[ppb], out=pp[:, (c % 4) * 128:(c % 4 + 1) * 128], lhsT=ktm[:, cg, :],
                      rhs=vtm[:, cg, :], start=True, stop=True)
            for c in range(8):
                cg = ti * 8 + c
                par = cg % 2
                pp, ppb = pst[c // 4]
                mk.op("dve", "scalar_tensor_tensor", reads=[Sb[1 - par], decb[ti], ppb], writes=[Sb[par]], out=S[:, par, :], in0=S[:, 1 - par, :],
                      scalar=dec[:, cg:cg + 1], in1=pp[:, (c % 4) * 128:(c % 4 + 1) * 128], op0=ALU.mult, op1=ALU.add)
                mk.op("act", "activation", reads=[Sb[par]], writes=[Sallb[ti]], out=Sall[:, cg + 1, :], in_=S[:, par, :], func=AF.Copy)
            st2[(hd, ti)] = si

        def stage2b(hd, ti):
            s_ = sl(ti)
            si = st2.pop((hd, ti))
            po, pob = self.next_ps_long()
            for c in range(8):
                cg = ti * 8 + c
                cs = slice(ti * 512 + c * 64, ti * 512 + (c + 1) * 64)
                prevb = S0b if cg == 0 else (Sallb[ti - 1] if c == 0 else Sallb[ti])
                mk.op("pe", "matmul", reads=[vtmb[ti], scsb[si]], writes=[pob], out=po[:, c * 64:(c + 1) * 64], lhsT=vtm[:, cg, :], rhs=scs[:, si, c, :],
                      start=True, stop=False)
                mk.op("pe", "matmul", reads=[prevb, qhb[ti]], writes=[pob], out=po[:, c * 64:(c + 1) * 64], lhsT=Sall[:, cg, :], rhs=qh[:, cs],
                      start=False, stop=True)
            oi = self.rr("oo", 2)
            mk.op("act", "activation", reads=[pob], writes=[oob[oi]], out=oo[:, oi, :], in_=po[:], func=AF.Copy)
            p2, p2b = self.sumsq_bcast([(oo[:, oi, :], oob[oi])], 512)
            r, rb = self.rstd_from(p2, p2b, 512, 128)
            i = self.rr("ybuf", 2)
            mk.op("dve", "scalar_tensor_tensor", reads=[oob[oi], gnb, rb], writes=[tmpb[i]], out=tmp[:, i, :], in0=oo[:, oi, :], scalar=gn[:, 0:1],
                  in1=r, op0=ALU.mult, op1=ALU.mult)
            mk.op("dve", "tensor_tensor", reads=[tmpb[i], gsb[ti]], writes=[self.ybufb[i]], out=self.ybuf[:, i, :], in0=tmp[:, i, :], in1=gs[:, s_], op=ALU.mult)
            self.store_y(hd, ti, i)

        units = [(hd, ti) for hd in range(NH) for ti in range(4)]
        for n in range(len(units) + 1):
            if n >= 1:
                stage2a(*units[n - 1])
            if n < len(units):
                stage1(*units[n])
            if n >= 1:
                stage2b(*units[n - 1])
        self.finish()


class ProgLRU(ProgM):
    def __init__(self, shared=None, pfx="", io=None):
        super().__init__(shared, pfx, io)
        mk = self.mk
        NCH = 8
        win_d = self.din("w_in", [2 * NCH, 128, D])
        wax_d = self.din("w_ax", [2 * NCH, 128, 256])
        cw, cwb = self.const("conv_w", [128, 4, NCH])
        vec, vecb = self.const("vec", [128, 4, NCH])
        cl = mk.sbuf("clam", [128, 2, NCH], F32)
        clb = mk.buf("clam")
        one_t = mk.sbuf("one_t", [128, 1], F32)
        oneb = mk.buf("one_t")
        mk.op("dve", "memset", writes=[oneb], ap=one_t[:], constant=1.0)
        mk.op("act", "activation", reads=[vecb], writes=[clb], out=cl[:, 0, :], in_=vec[:, 3, :], func=AF.Exp, scale=-1.0)
        mk.op("act", "activation", reads=[clb, oneb], writes=[clb], out=cl[:, 0, :], in_=cl[:, 0, :], func=AF.Ln, bias=one_t[:, 0:1])
        mk.op("dve", "tensor_scalar", reads=[clb], writes=[clb], out=cl[:, 1, :], in0=cl[:, 0, :], scalar1=-16.0, scalar2=None, op0=ALU.mult)
        mk.op("dve", "tensor_scalar", reads=[clb], writes=[clb], out=cl[:, 0, :], in0=cl[:, 0, :], scalar1=-8.0, scalar2=None, op0=ALU.mult)

        def fbuf(name, dt=F32, n=4):
            return mk.sbuf(name, [128, SEQ], dt), mk.bufs(name, n)
        u, ub = fbuf("u")
        uc = [fbuf("uc%d" % i, F32, 1) for i in range(2)]
        u16 = [fbuf("u16_%d" % i, BF16, 1) for i in range(2)]
        r_, rb_ = fbuf("r")
        ig, igb = fbuf("ig")
        a_, ab_ = fbuf("a")
        t_, tb_ = fbuf("t")
        h_, hb_ = fbuf("h")
        gx, gxb = fbuf("gx")
        src, srcb = self.hsrc()
        T4 = range(4)

        def sl(ti):
            return slice(ti * 512, (ti + 1) * 512)

        def ev_act(dst, dstb, func, bias=None, extra=()):
            def evac(j, ti, p, pb):
                kw = {}
                if bias is not None:
                    kw["bias"] = bias
                mk.op("act", "activation", reads=[pb] + list(extra), writes=[dstb[ti]], out=dst[:, sl(ti)], in_=p[:], func=func, **kw)
            return evac

        for bl in range(4):
            for q in range(2):
                cc = bl * 2 + q
                ucq, ucqb = uc[q]
                self.linear_fm(win_d, [NCH + cc], DC, src, srcb, T4, ev_act(u, ub, AF.Copy))
                mk.op("dve", "tensor_scalar", reads=ub + [cwb, vecb], writes=[ucqb[0]], out=ucq[:, :], in0=u[:, :], scalar1=cw[:, 3, cc:cc + 1],
                      scalar2=vec[:, 0, cc:cc + 1], op0=ALU.mult, op1=ALU.add)
                for sh in (1, 2, 3):
                    mk.op("dve", "scalar_tensor_tensor", reads=ub + [cwb, ucqb[0]], writes=[ucqb[0]], out=ucq[:, sh:], in0=u[:, 0:SEQ - sh],
                          scalar=cw[:, 3 - sh, cc:cc + 1], in1=ucq[:, sh:], op0=ALU.mult, op1=ALU.add)
                mk.op("act", "activation", reads=[ucqb[0]], writes=[u16[q][1][0]], out=u16[q][0][:, :], in_=ucq[:, :], func=AF.Copy)
            for q in range(2):
                cc = bl * 2 + q
                ucq, ucqb = uc[q]

                def usrc(k, ti):
                    return u16[k][0][:, sl(ti)]

                def usrcb(k, ti):
                    return u16[k][1][0]
                self.linear_fm(wax_d, [cc], 2, usrc, usrcb, T4, ev_act(r_, rb_, AF.Sigmoid, bias=vec[:, 1, cc:cc + 1], extra=[vecb]))
                self.linear_fm(wax_d, [NCH + cc], 2, usrc, usrcb, T4, ev_act(ig, igb, AF.Sigmoid, bias=vec[:, 2, cc:cc + 1], extra=[vecb]))
                self.linear_fm(win_d, [cc], DC, src, srcb, T4, ev_act(gx, gxb, AF.Copy))
                for ti in T4:
                    s_ = sl(ti)
                    mk.op("act", "activation", reads=[rb_[ti], clb], writes=[ab_[ti]], out=a_[:, s_], in_=r_[:, s_], func=AF.Exp, scale=cl[:, 0, cc:cc + 1])
                    mk.op("act", "activation", reads=[rb_[ti], clb], writes=[tb_[ti]], out=t_[:, s_], in_=r_[:, s_], func=AF.Exp, scale=cl[:, 1, cc:cc + 1])
                    mk.op("dve", "tensor_scalar", reads=[tb_[ti]], writes=[tb_[ti]], out=t_[:, s_], in0=t_[:, s_], scalar1=-1.0, scalar2=1.0,
                          op0=ALU.mult, op1=ALU.add)
                    mk.op("dve", "tensor_scalar", reads=[tb_[ti]], writes=[tb_[ti]], out=t_[:, s_], in0=t_[:, s_], scalar1=0.0, scalar2=None, op0=ALU.max)
                    mk.op("act", "activation", reads=[tb_[ti]], writes=[tb_[ti]], out=t_[:, s_], in_=t_[:, s_], func=AF.Sqrt)
                    mk.op("dve", "tensor_tensor", reads=[tb_[ti], igb[ti]], writes=[tb_[ti]], out=t_[:, s_], in0=t_[:, s_], in1=ig[:, s_], op=ALU.mult)
                    mk.op("dve", "tensor_tensor", reads=[tb_[ti], ucqb[0]], writes=[tb_[ti]], out=t_[:, s_], in0=t_[:, s_], in1=ucq[:, s_], op=ALU.mult)
                    init = 0.0 if ti == 0 else h_[:, ti * 512 - 1:ti * 512]
                    rd = [ab_[ti], tb_[ti]] + ([hb_[ti - 1]] if ti > 0 else [])
                    mk.op("dve", "tensor_tensor_scan", reads=rd, writes=[hb_[ti]], out=h_[:, s_], data0=a_[:, s_], data1=t_[:, s_], initial=init,
                          op0=ALU.mult, op1=ALU.add)
                    mk.op("dve", "tensor_tensor", reads=[gxb[ti]], writes=[ab_[ti]], out=a_[:, s_], in0=gx[:, s_], in1=gx[:, s_], op=ALU.mult)
                    mk.op("dve", "tensor_scalar", reads=[ab_[ti]], writes=[ab_[ti]], out=a_[:, s_], in0=a_[:, s_], scalar1=0.044715, scalar2=1.0,
                          op0=ALU.mult, op1=ALU.add)
                    mk.op("dve", "tensor_tensor", reads=[ab_[ti], gxb[ti]], writes=[ab_[ti]], out=a_[:, s_], in0=a_[:, s_], in1=gx[:, s_], op=ALU.mult)
                    mk.op("act", "activation", reads=[ab_[ti]], writes=[ab_[ti]], out=a_[:, s_], in_=a_[:, s_], func=AF.Sigmoid, scale=2.0 * 0.7978845608028654)
                    mk.op("dve", "tensor_tensor", reads=[ab_[ti], gxb[ti]], writes=[ab_[ti]], out=a_[:, s_], in0=a_[:, s_], in1=gx[:, s_], op=ALU.mult)
                    i = self.rr("ybuf", 2)
                    mk.op("dve", "tensor_tensor", reads=[ab_[ti], hb_[ti]], writes=[self.ybufb[i]], out=self.ybuf[:, i, :], in0=a_[:, s_], in1=h_[:, s_], op=ALU.mult)
                    self.store_y(cc, ti, i)
        self.finish()


class ProgMLSTM(ProgM):
    def __init__(self, shared=None, pfx="", io=None):
        super().__init__(shared, pfx, io)
        mk = self.mk
        NH = 4
        NCK = 32
        win_d = self.din("w_in", [24, 128, D])
        wg, wgb = self.const("w_gate", [128, DC, 8])
        bif, bifb = self.const("b_if", [4, 2])
        gn, gnb = self.const("gn", [128, 2])
        self.gb = gnb
        sel, selb = self.const("sel", [4, 4 * 128])
        ident, identb = self.const("ident", [128, 128], BF16)
        mask, maskb = self.const("mask64", [64, 64])
        wg16 = mk.sbuf("wg16", [128, DC, 8], BF16)
        wg16b = mk.buf("wg16")
        mk.op("act", "activation", reads=[wgb], writes=[wg16b], out=wg16[:], in_=wg[:], func=AF.Copy)
        one4 = mk.sbuf("one4", [4, 1], F32)
        one4b = mk.buf("one4")
        mk.op("dve", "memset", writes=[one4b], ap=one4[:], constant=1.0)
        b15 = mk.sbuf("b15", [4, 2], F32)
        b15b = mk.buf("b15")
        mk.op("dve", "tensor_scalar", reads=[bifb], writes=[b15b], out=b15[:], in0=bif[:], scalar1=1.0 / 15.0, scalar2=None, op0=ALU.mult)
        src, srcb = self.hsrc()
        T4 = range(4)

        def sl(ti):
            return slice(ti * 512, (ti + 1) * 512)

        def row(name):
            return mk.sbuf(name, [4, SEQ], F32), mk.buf(name)
        it, itb = row("g_it")
        bn, bnb = row("g_bn")
        aa, aab = row("g_a")
        AA, AAb = row("g_A")
        tr, trb = row("g_tr")
        qf, qfb = it, itb
        kf, kfb = aa, aab
        em, emb = bn, bnb
        zr4 = mk.sbuf("g_zero", [4, 1], F32)
        zrb = mk.buf("g_zero")
        mk.op("dve", "memset", writes=[zrb], ap=zr4[:], constant=0.0)
        zr_bc = bass.AP(zr4, zr4[:, 0:1].offset, [[zr4[:, 0:1].ap[0][0], 4], [0, SEQ]])
        R33 = mk.sbuf("g_R33", [4, NCK + 1], F32)
        R33b = mk.buf("g_R33")
        dec4 = mk.sbuf("g_dec", [4, NCK], F32)
        dec4b = mk.buf("g_dec")
        for gi, dst, dstb in ((0, it, itb), (1, bn, bnb)):
            for ti in T4:
                p, pb = self.next_ps()
                for k in range(DC):
                    mk.op("pe", "matmul", reads=[wg16b, self.hnb[k][ti]], writes=[pb], out=p[0:4, :], lhsT=wg16[:, k, gi * 4:(gi + 1) * 4],
                          rhs=self.hn[:, k, sl(ti)], start=(k == 0), stop=(k == DC - 1))
                mk.op("act", "activation", reads=[pb, b15b], writes=[dstb], out=dst[:, sl(ti)], in_=p[0:4, :], func=AF.Tanh,
                      bias=b15[:, gi:gi + 1], scale=1.0 / 15.0)
        mk.op("dve", "tensor_scalar", reads=[itb], writes=[itb], out=it[:], in0=it[:], scalar1=15.0, scalar2=None, op0=ALU.mult)
        mk.op("act", "activation", reads=[bnb], writes=[bnb], out=bn[:], in_=bn[:], func=AF.Exp, scale=-15.0)
        mk.op("act", "activation", reads=[bnb, one4b], writes=[bnb], out=bn[:], in_=bn[:], func=AF.Ln, bias=one4[:, 0:1])
        mk.op("dve", "tensor_tensor_scan", reads=[bnb, zrb], writes=[trb], out=tr[:], data0=bn[:], data1=zr_bc, initial=0.0, op0=ALU.add, op1=ALU.add)
        mk.op("dve", "tensor_tensor", reads=[trb, itb], writes=[aab], out=aa[:], in0=tr[:], in1=it[:], op=ALU.add)
        mk.op("dve", "tensor_tensor_scan", reads=[aab], writes=[AAb], out=AA[:], data0=aa[:], data1=aa[:], initial=0.0, op0=ALU.max, op1=ALU.max)
        pstep = AA[:, 0:1].ap[0][0]
        Rbc = bass.AP(AA, AA[:, 63:64].offset, [[pstep, 4], [64, NCK], [0, 64]])
        Rv = bass.AP(AA, AA[:, 63:64].offset, [[pstep, 4], [64, NCK]])
        mk.op("dve", "tensor_tensor", reads=[trb, AAb], writes=[emb], out=em[:], in0=tr[:], in1=AA[:], op=ALU.subtract)
        mk.op("act", "activation", reads=[emb], writes=[emb], out=em[:], in_=em[:], func=AF.Exp)
        mk.op("dve", "tensor_tensor", reads=[AAb], writes=[qfb], out=v3(qf[:], NCK, 64), in0=Rbc, in1=v3(AA[:], NCK, 64), op=ALU.subtract)
        mk.op("act", "activation", reads=[qfb], writes=[qfb], out=qf[:], in_=qf[:], func=AF.Exp)
        mk.op("dve", "tensor_tensor", reads=[AAb, aab], writes=[kfb], out=v3(kf[:], NCK, 64), in0=v3(aa[:], NCK, 64), in1=Rbc, op=ALU.subtract)
        mk.op("act", "activation", reads=[kfb], writes=[kfb], out=kf[:], in_=kf[:], func=AF.Exp)
        mk.op("dve", "tensor_scalar", reads=[kfb], writes=[kfb], out=kf[:], in0=kf[:], scalar1=128.0 ** -0.5, scalar2=None, op0=ALU.mult)
        mk.op("dve", "memset", writes=[R33b], ap=R33[:, 0:1], constant=0.0)
        mk.op("dve", "tensor_copy", reads=[AAb, R33b], writes=[R33b], out=R33[:, 1:NCK + 1], in_=Rv)
        mk.op("dve", "tensor_tensor", reads=[R33b], writes=[dec4b], out=dec4[:], in0=R33[:, 0:NCK], in1=R33[:, 1:NCK + 1], op=ALU.subtract)
        mk.op("act", "activation", reads=[dec4b], writes=[dec4b], out=dec4[:], in_=dec4[:], func=AF.Exp)

        def fbuf(name, dt=F32, n=4):
            return mk.sbuf(name, [128, SEQ], dt), mk.bufs(name, n)
        fb, fbb = fbuf("fb")
        qt, qtb = fbuf("qt", BF16)
        kt, ktb = fbuf("kt", BF16)
        vf, vfb = fbuf("vf", BF16)
        og = mk.sbuf("og", [128, 2, SEQ], BF16)
        ogb = mk.bufs("og", 2, 4)
        ktm = mk.sbuf("ktm", [64, NCK, 128], BF16)
        ktmb = mk.bufs("ktm", 4)
        vtm = mk.sbuf("vtm", [64, NCK, 384], BF16)
        vtmb = mk.bufs("vtm", 4)
        mk.op("dve", "memset", writes=vtmb, ap=vtm[:, :, 256:384], constant=1.0)
        scs = mk.sbuf("scs", [64, 2, 8, 64], BF16)
        scsb = mk.bufs("scs", 2)
        decb_ = mk.sbuf("decb", [128, NCK], F32)
        decbb = mk.buf("decb")
        C = mk.sbuf("C", [128, 2, 384], F32)
        Cb = mk.bufs("C", 2)
        Cd = mk.sbuf("Cd", [128, 2, 384], BF16)
        Cdb = mk.bufs("Cd", 2)
        nm = mk.sbuf("nm", [128, 2, 512], F32)
        nmb = mk.bufs("nm", 2)
        dn = mk.sbuf("dn", [128, 512], F32)
        dnb = mk.buf("dn")

        def bcast(rowt, rowb, hd, ti, dst_ap, dstb):
            p, pb = self.next_ps()
            mk.op("pe", "matmul", reads=[selb, rowb], writes=[pb], out=p[:], lhsT=sel[0:4, hd * 128:(hd + 1) * 128], rhs=rowt[0:4, sl(ti)],
                  start=True, stop=True)
            mk.op("act", "activation", reads=[pb], writes=[dstb], out=dst_ap, in_=p[:], func=AF.Copy)

        for hd in range(NH):
            for ti in T4:
                bcast(qf, qfb, hd, ti, fb[:, sl(ti)], fbb[ti])

            def ev_mul(dst, dstb):
                def evac(j, ti, p, pb):
                    mk.op("dve", "tensor_tensor", reads=[pb, fbb[ti]], writes=[dstb[ti]], out=dst[:, sl(ti)], in0=p[:], in1=fb[:, sl(ti)], op=ALU.mult)
                return evac
            self.linear_fm(win_d, [hd], DC, src, srcb, T4, ev_mul(qt, qtb))
            for ti in T4:
                bcast(kf, kfb, hd, ti, fb[:, sl(ti)], fbb[ti])
            self.linear_fm(win_d, [4 + hd], DC, src, srcb, T4, ev_mul(kt, ktb))
            p, pb = self.next_ps()
            mk.op("pe", "matmul", reads=[selb, dec4b], writes=[pb], out=p[:, 0:NCK], lhsT=sel[0:4, hd * 128:(hd + 1) * 128], rhs=dec4[0:4, :], start=True, stop=True)
            mk.op("act", "activation", reads=[pb], writes=[decbb], out=decb_[:], in_=p[:, 0:NCK], func=AF.Copy)
            for ti in T4:
                p, pb = self.next_ps()
                pv = p.bitcast(BF16)
                for c in range(8):
                    cs = slice(ti * 512 + c * 64, ti * 512 + (c + 1) * 64)
                    mk.op("pe", "transpose", reads=[ktb[ti], identb], writes=[pb], out=pv[0:64, c * 128:(c + 1) * 128], in_=kt[:, cs], identity=ident[:])
                mk.op("act", "activation", reads=[pb], writes=[ktmb[ti]], out=ktm[:, ti * 8:(ti + 1) * 8, :],
                      in_=pv[0:64, :].rearrange("p (c i) -> p c i", i=128), func=AF.Copy)
            for vc in range(2):
                def evac_v(j, ti, p, pb):
                    mk.op("act", "activation", reads=[pb], writes=[vfb[ti]], out=vf[:, sl(ti)], in_=p[:], func=AF.Copy)
                self.linear_fm(win_d, [8 + hd * 2 + vc], DC, src, srcb, T4, evac_v)
                for ti in T4:
                    p, pb = self.next_ps()
                    pv = p.bitcast(BF16)
                    for c in range(8):
                        cs = slice(ti * 512 + c * 64, ti * 512 + (c + 1) * 64)
                        mk.op("pe", "transpose", reads=[vfb[ti], identb], writes=[pb], out=pv[0:64, c * 128:(c + 1) * 128], in_=vf[:, cs], identity=ident[:])
                    mk.op("act", "activation", reads=[pb], writes=[vtmb[ti]], out=vtm[:, ti * 8:(ti + 1) * 8, vc * 128:(vc + 1) * 128],
                          in_=pv[0:64, :].rearrange("p (c i) -> p c i", i=128), func=AF.Copy)

                def evac_o(j, ti, p, pb, vc=vc):
                    mk.op("act", "activation", reads=[pb], writes=[ogb[vc][ti]], out=og[:, vc, sl(ti)], in_=p[:], func=AF.Sigmoid)
                self.linear_fm(win_d, [16 + hd * 2 + vc], DC, src, srcb, T4, evac_o)
            mk.op("dve", "memset", writes=[Cb[1]], ap=C[:, 1, :], constant=0.0)
            for ti in T4:
                p, pb = self.next_ps()
                for c in range(8):
                    cs = slice(ti * 512 + c * 64, ti * 512 + (c + 1) * 64)
                    mk.op("pe", "matmul", reads=[ktb[ti], qtb[ti]], writes=[pb], out=p[0:64, c * 64:(c + 1) * 64], lhsT=kt[:, cs], rhs=qt[:, cs], start=True, stop=True)
                si = self.rr("scs", 2)
                mbc = bass.AP(mask, mask[:, 0:1].offset, [[mask[:, 0:1].ap[0][0], 64], [0, 8], [1, 64]])
                mk.op("dve", "tensor_tensor", reads=[pb, maskb], writes=[scsb[si]], out=scs[:, si, :, :], in0=v3(p[0:64, :], 8, 64), in1=mbc, op=ALU.mult)
                pos = [self.next_ps_long() for _ in range(3)]
                pstq = {}

                def emit_pst(c):
                    cg = ti * 8 + c
                    pst, pstb = self.next_ps()
                    mk.op("pe", "matmul", reads=[ktmb[ti], vtmb[ti]], writes=[pstb], out=pst[:, 0:384], lhsT=ktm[:, cg, :], rhs=vtm[:, cg, :], start=True, stop=True)
                    pstq[c] = (pst, pstb)
                emit_pst(0)
                emit_pst(1)
                for c in range(8):
                    cg = ti * 8 + c
                    par = cg % 2
                    ci = c % 2
                    cs = slice(ti * 512 + c * 64, ti * 512 + (c + 1) * 64)
                    mk.op("act", "activation", reads=[Cb[1 - par], decbb], writes=[Cdb[ci]], out=Cd[:, ci, :], in_=C[:, 1 - par, :], func=AF.Copy,
                          scale=decb_[:, cg:cg + 1])
                    for oc in range(3):
                        po, pob = pos[oc]
                        mk.op("pe", "matmul", reads=[vtmb[ti], scsb[si]], writes=[pob], out=po[:, c * 64:(c + 1) * 64], lhsT=vtm[:, cg, oc * 128:(oc + 1) * 128],
                              rhs=scs[:, si, c, :], start=True, stop=False)
                        mk.op("pe", "matmul", reads=[Cdb[ci], qtb[ti]], writes=[pob], out=po[:, c * 64:(c + 1) * 64], lhsT=Cd[:, ci, oc * 128:(oc + 1) * 128],
                              rhs=qt[:, cs], start=False, stop=True)
                    if c + 2 < 8:
                        emit_pst(c + 2)
                    pst, pstb = pstq.pop(c)
                    mk.op("dve", "scalar_tensor_tensor", reads=[Cb[1 - par], decbb, pstb], writes=[Cb[par]], out=C[:, par, :], in0=C[:, 1 - par, :],
                          scalar=decb_[:, cg:cg + 1], in1=pst[:, 0:384], op0=ALU.mult, op1=ALU.add)
                pem, pemb = self.next_ps()
                mk.op("pe", "matmul", reads=[selb, emb], writes=[pemb], out=pem[:], lhsT=sel[0:4, hd * 128:(hd + 1) * 128], rhs=em[0:4, sl(ti)],
                      start=True, stop=True)
                mk.op("act", "activation", reads=[pos[2][1]], writes=[dnb], out=dn[:], in_=pos[2][0][:], func=AF.Copy)
                mk.op("dve", "scalar_tensor_tensor", reads=[dnb], writes=[dnb], out=dn[:], in0=dn[:], scalar=-1.0, in1=dn[:], op0=ALU.mult, op1=ALU.max)
                mk.op("dve", "tensor_tensor", reads=[dnb, pemb], writes=[dnb], out=dn[:], in0=dn[:], in1=pem[:], op=ALU.max)
                mk.op("dve", "reciprocal", reads=[dnb], writes=[dnb], out=dn[:], in_=dn[:])
                for oc in range(2):
                    mk.op("dve", "tensor_tensor", reads=[pos[oc][1], dnb], writes=[nmb[oc]], out=nm[:, oc, :], in0=pos[oc][0][:], in1=dn[:], op=ALU.mult)
                p2, p2b = self.sumsq_bcast([(nm[:, oc, :], nmb[oc]) for oc in range(2)], 512)
                r, rb = self.rstd_from(p2, p2b, 512, 256)
                for oc in range(2):
                    i = self.rr("ybuf", 2)
                    mk.op("dve", "scalar_tensor_tensor", reads=[nmb[oc], gnb, rb], writes=[nmb[oc]], out=nm[:, oc, :], in0=nm[:, oc, :], scalar=gn[:, oc:oc + 1],
                          in1=r, op0=ALU.mult, op1=ALU.mult)
                    mk.op("dve", "tensor_tensor", reads=[nmb[oc], ogb[oc][ti]], writes=[self.ybufb[i]], out=self.ybuf[:, i, :], in0=nm[:, oc, :], in1=og[:, oc, sl(ti)], op=ALU.mult)
                    self.store_y(hd * 2 + oc, ti, i)
        self.finish()


def mlstm_inputs(inp, idx, hh):
    W = inp["ml_w_in"][idx]
    wt = tile_w(W[:, :6144])
    h0 = hh * 4
    w_in = np.concatenate([wt[h0:h0 + 4], wt[8 + h0:8 + h0 + 4], wt[16 + h0 * 2:16 + h0 * 2 + 8], wt[32 + h0 * 2:32 + h0 * 2 + 8]], axis=0)
    gcols = np.concatenate([W[:, 6144 + h0:6144 + h0 + 4], W[:, 6152 + h0:6152 + h0 + 4]], axis=1)
    w_gate = np.ascontiguousarray(gcols.reshape(DC, 128, 8).transpose(1, 0, 2))
    b_if = np.ascontiguousarray(inp["ml_b_if"][idx][:, h0:h0 + 4].T)
    gn = fmv(inp["ml_g_norm"][idx])
    sel = np.zeros((4, 4 * 128), np.float32)
    for h in range(4):
        sel[h, h * 128:(h + 1) * 128] = 1.0
    return {"w_in": w_in, "w_gate": w_gate, "b_if": b_if, "gn": gn, "sel": sel}


def lru_inputs(inp, idx, hh):
    wt = tile_w(inp["lru_w_in"][idx])
    w_in = np.concatenate([wt[hh * 8:hh * 8 + 8], wt[16 + hh * 8:16 + hh * 8 + 8]], axis=0)
    wa = np.concatenate([tile_w(inp["lru_w_a"][idx, hh * 4 + b]) for b in range(4)], axis=0)
    wx = np.concatenate([tile_w(inp["lru_w_x"][idx, hh * 4 + b]) for b in range(4)], axis=0)
    sl = slice(hh * 1024, (hh + 1) * 1024)
    conv_w = np.ascontiguousarray(inp["lru_conv_w"][idx][:, sl].reshape(4, 8, 128).transpose(2, 0, 1))
    vec = np.stack([fmv(inp[k][idx][sl]) for k in ("lru_conv_b", "lru_b_a", "lru_b_x", "lru_lambda")], axis=1)
    return {"w_in": w_in, "w_ax": np.concatenate([wa, wx], axis=0), "conv_w": conv_w, "vec": np.ascontiguousarray(vec)}


def hgrn_inputs(inp, idx, hh):
    wt = tile_w(inp["hg_w_in"][idx])
    sel = np.concatenate([wt[kind * 16 + hh * 8: kind * 16 + hh * 8 + 8] for kind in range(4)], axis=0)
    lbp = np.ascontiguousarray(inp["hg_lb_param"][:, hh * 1024:(hh + 1) * 1024].reshape(4, 8, 128).transpose(2, 0, 1))
    return {"w_in": sel, "lbp": lbp, "gn": np.ascontiguousarray(inp["hg_g_norm"][idx].reshape(128, 1))}


class Fused(Prog):
    def __init__(self, depth=DEPTH, ncores=8):
        super().__init__()
        nc, mk = self.nc, self.mk
        rg = [[2 * i, 2 * i + 1] for i in range(ncores // 2)]

        def dbuf(name, shape, dt):
            return nc.dram_tensor(name, list(shape), dt).ap(), mk.buf(name)
        xs = [dbuf("xs%d" % l, [D, TA], F32) for l in range(depth)]
        hn_p = [[dbuf("hnp%d_%d" % (l, pc), [D, 512], BF16) for pc in range(2)] for l in range(depth)]
        hn_g = [[dbuf("hng%d_%d" % (l, pc), [2 * D, 512], BF16) for pc in range(2)] for l in range(depth)]
        y_p = [[dbuf("yp%d_%d" % (l, pc), [512, SEQ], BF16) for pc in range(2)] for l in range(depth)]
        y_g = [[dbuf("yg%d_%d" % (l, pc), [1024, SEQ], BF16) for pc in range(2)] for l in range(depth)]
        for l in range(depth + 1):
            io = {"rg": rg}
            if l > 0:
                io["x_src"] = xs[l - 1]
                io["y_g"] = y_g[l - 1]
            if l < depth:
                io["x_dst"] = xs[l]
                io["hn_p"] = hn_p[l]
                io["hn_g"] = hn_g[l]
            ProgA(first=(l == 0), last=(l == depth), shared=self, pfx="A%d_" % l, io=io)
            if l < depth:
                iom = {"rg": rg, "hn_g": hn_g[l], "y_p": y_p[l], "y_g": y_g[l]}
                kind = l % 3
                if kind == 0:
                    ProgHGRN(l, shared=self, pfx="M%d_" % l, io=iom)
                elif kind == 1:
                    ProgLRU(shared=self, pfx="M%d_" % l, io=iom)
                else:
                    ProgMLSTM(shared=self, pfx="M%d_" % l, io=iom)
        mk.finalize()


def fused_inputs(inp, depth=DEPTH, ncores=8, final_g=None):
    cst = consts_host()
    ng_all = inp["norm_g"]
    final_g = inp["final_norm_g"] if final_g is None else final_g
    mixer_wout = [inp["hg_w_out"][0], inp["lru_w_out"][0], inp["ml_w_out"][0], inp["hg_w_out"][1]]
    common = {"ident": cst["ident"], "mask64": cst["mask64"], "memg": fmv(inp["mem_norm_g"])}
    per_half = [dict(), dict()]
    for l in range(depth + 1):
        p = "A%d_" % l
        if l == 0:
            ng = np.stack([fmv(ng_all[0, i]) for i in (0, 0, 0, 1)], axis=1)
        elif l == depth:
            ng = np.stack([fmv(ng_all[l - 1, 2]), fmv(ng_all[l - 1, 3]), fmv(final_g), fmv(final_g)], axis=1)
        else:
            ng = np.stack([fmv(ng_all[l - 1, 2]), fmv(ng_all[l - 1, 3]), fmv(ng_all[l, 0]), fmv(ng_all[l, 1])], axis=1)
        common[p + "ng"] = np.ascontiguousarray(ng)
        if l > 0:
            common[p + "w_mo"] = tile_w(mixer_wout[l - 1])
            common[p + "w_q"] = tile_w(inp["xa_w_q"][l - 1])
            common[p + "w_kv"] = tile_w(inp["xa_w_kv"][l - 1])
            common[p + "w_o"] = tile_w(inp["xa_w_o"][l - 1])
            common[p + "w_gu2"] = tile_w(inp["ffn_w_gu"][l - 1, 1])
            common[p + "w_dn2"] = tile_w(inp["ffn_w_down"][l - 1, 1])
        if l < depth:
            common[p + "w_gu1"] = tile_w(inp["ffn_w_gu"][l, 0])
            common[p + "w_dn1"] = tile_w(inp["ffn_w_down"][l, 0])
            kind, idx = l % 3, l // 3
            for hh in range(2):
                if kind == 0:
                    m = hgrn_inputs(inp, idx, hh)
                elif kind == 1:
                    m = lru_inputs(inp, idx, hh)
                else:
                    m = mlstm_inputs(inp, idx, hh)
                for k, v in m.items():
                    per_half[hh]["M%d_%s" % (l, k)] = v
    in_maps = []
    for c in range(ncores):
        b, r = divmod(c, 2)
        m = dict(common)
        m.update(per_half[r])
        m["A0_xT"] = np.ascontiguousarray(inp["x"][b, r * TA:(r + 1) * TA].T)
        m["memT"] = np.ascontiguousarray(inp["mem"][b].T)
        in_maps.append(m)
    return in_maps


_PROGS = {}


def _prog(key, ctor):
    if key not in _PROGS:
        _PROGS[key] = ctor()
    return _PROGS[key]


def _run(prog, in_maps):
    for m in in_maps:
        for k, (shape, dt) in prog.in_specs.items():
            assert k in m, k
            assert tuple(m[k].shape) == shape, (k, m[k].shape, shape)
    in_maps = [{k: m[k] for k in prog.in_specs} for m in in_maps]
    res = run_bass_kernel_spmd(prog.nc, in_maps, core_ids=list(range(len(in_maps))))
    return res.results


def _run_timed(prog, in_maps):
    in_maps = [{k: m[k] for k in prog.in_specs} for m in in_maps]
    res = run_bass_kernel_spmd(prog.nc, in_maps, core_ids=list(range(len(in_maps))), trace=True)
    print("exec_time_ns", res.exec_time_ns)
    return res.results


def kernel(**inp):
    inp = {k: np.asarray(v) for k, v in inp.items()}
    prog = _prog("fused", Fused)
    res = _run(prog, fused_inputs(inp))
    out = np.empty((BATCH, SEQ, D), np.float32)
    for c in range(8):
        b, tc = divmod(c, 2)
        out[b, tc * TA:(tc + 1) * TA] = res[c]["A%d_outT" % DEPTH].T
    return out


def kernel_unfused(**inp):
    inp = {k: np.asarray(v) for k, v in inp.items()}
    cst = consts_host()
    NC = 8
    x = inp["x"]
    ng_all = inp["norm_g"]
    mixer_wout = [inp["hg_w_out"][0], inp["lru_w_out"][0], inp["ml_w_out"][0], inp["hg_w_out"][1]]

    def ffn_w(layer, f):
        return tile_w(inp["ffn_w_gu"][layer, f]), tile_w(inp["ffn_w_down"][layer, f])

    pa = _prog("A0", lambda: ProgA(first=True, last=False))
    ng = np.ascontiguousarray(np.stack([fmv(ng_all[0, i]) for i in (0, 0, 0, 1)], axis=1))
    wgu, wdn = ffn_w(0, 0)
    in_maps = []
    for c in range(NC):
        b, tc = divmod(c, 2)
        in_maps.append({"xT": np.ascontiguousarray(x[b, tc * TA:(tc + 1) * TA].T), "ng": ng, "w_gu1": wgu, "w_dn1": wdn})
    res = _run(pa, in_maps)
    xT = [r["xT_out"] for r in res]
    hnT = [r["hnT_out"] for r in res]
    del wgu, wdn
    out = None
    for layer in range(DEPTH):
        kind, idx = layer % 3, layer // 3
        in_maps = []
        for c in range(NC):
            b, hh = divmod(c, 2)
            hn_full = np.ascontiguousarray(np.concatenate([hnT[b * 2], hnT[b * 2 + 1]], axis=1))
            if kind == 0:
                m = dict(hgrn_inputs(inp, idx, hh), ident=cst["ident"], mask64=cst["mask64"])
            elif kind == 1:
                m = lru_inputs(inp, idx, hh)
            else:
                m = dict(mlstm_inputs(inp, idx, hh), ident=cst["ident"], mask64=cst["mask64"])
            m["hnT"] = hn_full
            in_maps.append(m)
        if kind == 0:
            pm = _prog("HGRN%d" % layer, lambda: ProgHGRN(layer))
        elif kind == 1:
            pm = _prog("LRU", ProgLRU)
        else:
            pm = _prog("MLSTM", ProgMLSTM)
        res = _run(pm, in_maps)
        yT = [r["yT_out"] for r in res]
        last = layer == DEPTH - 1
        pa = _prog("Alast" if last else "Amid", lambda: ProgA(first=False, last=last))
        if last:
            ng = np.stack([fmv(ng_all[layer, 2]), fmv(ng_all[layer, 3]), fmv(inp["final_norm_g"]), fmv(inp["final_norm_g"])], axis=1)
        else:
            ng = np.stack([fmv(ng_all[layer, 2]), fmv(ng_all[layer, 3]), fmv(ng_all[layer + 1, 0]), fmv(ng_all[layer + 1, 1])], axis=1)
        wgu2, wdn2 = ffn_w(layer, 1)
        common = {"ng": np.ascontiguousarray(ng), "memg": fmv(inp["mem_norm_g"]), "ident": cst["ident"],
                  "w_mo": tile_w(mixer_wout[layer]), "w_q": tile_w(inp["xa_w_q"][layer]), "w_kv": tile_w(inp["xa_w_kv"][layer]),
                  "w_o": tile_w(inp["xa_w_o"][layer]), "w_gu2": wgu2, "w_dn2": wdn2}
        if not last:
            wgu1, wdn1 = ffn_w(layer + 1, 0)
            common.update({"w_gu1": wgu1, "w_dn1": wdn1})
        in_maps = []
        for c in range(NC):
            b, tc = divmod(c, 2)
            y_c = np.ascontiguousarray(np.concatenate([yT[b * 2][:, tc * TA:(tc + 1) * TA], yT[b * 2 + 1][:, tc * TA:(tc + 1) * TA]], axis=0))
            in_maps.append(dict(common, xT=xT[c], yT=y_c, memT=np.ascontiguousarray(inp["mem"][b].T)))
        res = _run(pa, in_maps)
        if last:
            out = np.empty((BATCH, SEQ, D), np.float32)
            for c in range(NC):
                b, tc = divmod(c, 2)
                out[b, tc * TA:(tc + 1) * TA] = res[c]["outT"].T
        else:
            xT = [r["xT_out"] for r in res]
            hnT = [r["hnT_out"] for r in res]
    return out
```

```python
import numpy as np
import ml_dtypes
from contextlib import ExitStack
import concourse.bass as bass
import concourse.mybir as mybir
from concourse.bass_utils import run_bass_kernel_spmd

F32 = mybir.dt.float32
BF16 = mybir.dt.bfloat16
AF = mybir.ActivationFunctionType
ALU = mybir.AluOpType
AX = mybir.AxisListType
NPBF = ml_dtypes.bfloat16

SAME_ENGINE_SYNC = True

D = 2048
DC = 16
SEQ = 2048
BATCH = 4
NMEM = 256
DFF = 5504
FC = 43
EPS = 1e-6
DEPTH = 4
TA = 1024
GROUP = 8


class Buf:
    __slots__ = ("name", "last_w", "readers", "dma_readers", "dsem")

    def __init__(self, name):
        self.name = name
        self.last_w = None
        self.readers = {}
        self.dma_readers = []
        self.dsem = None


class Op:
    __slots__ = ("eng", "fn", "deps", "is_dma", "sem", "val", "milestone", "final_group", "inc")

    def __init__(self, eng, fn):
        self.eng = eng
        self.fn = fn
        self.deps = []
        self.is_dma = False
        self.sem = None
        self.val = 0
        self.milestone = False
        self.final_group = False
        self.inc = 16


class MK:
    ENGS = ("pe", "act", "dve", "pool", "sp")

    def __init__(self, nc):
        self.nc = nc
        self.es = ExitStack()
        self.ops = {e: [] for e in self.ENGS}
        self.esem = {}
        self.dsem_count = {}
        self.nsem = 0
        self.closers = []
        self.out_events = []
        self.nbuf = 0
        self.pes = None
        self.phase_sems = []
        self.free_sems = []
        self.live_async = {}
        self.nphase = 0
        self.rk = {}
        for e in self.ENGS:
            self.esem[e] = self.new_sem("eng_" + e, pooled=False)

    def new_sem(self, name, pooled=True):
        if pooled and self.free_sems:
            s = self.free_sems.pop()
        else:
            self.nsem += 1
            s = self.es.enter_context(self.nc.semaphore("%s_%d" % (name, self.nsem)))
            self.dsem_count[id(s)] = 0
        if pooled:
            self.phase_sems.append(s)
        return s

    def begin_phase(self):
        self.pes = ExitStack()
        self.nphase += 1

    def end_phase(self):
        frontier = []
        for e in self.ENGS:
            for op in reversed(self.ops[e]):
                if not op.is_dma and op.fn is not None:
                    frontier.append(op)
                    break
        frontier.extend(self.live_async.values())
        for e in self.ENGS:
            op = Op(e, None)
            op.deps = list(frontier)
            self.ops[e].append(op)
        self.live_async = {}
        self.free_sems.extend(self.phase_sems)
        self.phase_sems = []
        self.pes.close()
        self.pes = None

    def sbuf(self, name, shape, dtype):
        st = self.pes if self.pes is not None else self.es
        nm = name if self.pes is None else "%s_p%d" % (name, self.nphase)
        return st.enter_context(self.nc.sbuf_tensor(nm, list(shape), dtype))

    def psum(self, name, shape, dtype):
        return self.es.enter_context(self.nc.psum_tensor(name, list(shape), dtype))

    def buf(self, name=None):
        self.nbuf += 1
        return Buf(name or ("b%d" % self.nbuf))

    def bufs(self, name, *dims):
        if len(dims) == 1:
            return [self.buf("%s_%d" % (name, i)) for i in range(dims[0])]
        return [self.bufs("%s_%d" % (name, i), *dims[1:]) for i in range(dims[0])]

    def _record(self, op, reads, writes):
        deps = []
        for b in reads:
            if b.last_w is not None:
                deps.append(b.last_w)
        for b in writes:
            lw = b.last_w
            if lw is not None:
                if not (op.is_dma and lw.is_dma and lw.sem is op.sem and not b.readers and not b.dma_readers):
                    deps.append(lw)
            deps.extend(b.readers.values())
            deps.extend(b.dma_readers)
        op.deps = deps
        for b in reads:
            if op.is_dma:
                b.dma_readers.append(op)
            else:
                b.readers[op.eng] = op
        for b in writes:
            b.last_w = op
            b.readers = {}
            b.dma_readers = []
        self.ops[op.eng].append(op)
        return op

    def op(self, eng, meth, reads=(), writes=(), **kw):
        fn = (lambda e: getattr(e, meth)(**kw))
        return self._record(Op(eng, fn), list(reads), list(writes))

    def _async(self, op, reads, writes, sem, inc):
        if sem is None:
            b = writes[0] if writes else reads[0]
            if b.dsem is None:
                b.dsem = self.new_sem("d")
            sem = b.dsem
        op.is_dma = True
        op.sem = sem
        op.inc = inc
        self.dsem_count[id(sem)] += inc
        op.val = self.dsem_count[id(sem)]
        self.live_async[id(sem)] = op
        return self._record(op, reads, writes)

    def dma(self, eng, out_ap, in_ap, reads=(), writes=(), sem=None, final_group=False, in_fn=None):
        if in_fn is not None:
            op = Op(eng, lambda e: e.dma_start(out=out_ap, in_=in_fn(e)))
        else:
            op = Op(eng, lambda e: e.dma_start(out=out_ap, in_=in_ap))
        op.final_group = final_group
        return self._async(op, list(reads), list(writes), sem, 16)

    def cc_allgather(self, in_ap, out_ap, reads, writes, rg):
        op = Op("pool", lambda e: e.collective_compute("AllGather", ALU.bypass, replica_groups=rg, ins=[in_ap], outs=[out_ap]))
        sem = self.new_sem("cc", pooled=False)
        r = self._async(op, list(reads), list(writes), sem, 1)
        del self.live_async[id(sem)]
        return r

    def rank2(self, e, ename):
        if ename not in self.rk:
            self.rk[ename] = e.snap(e.partition_id() % 2)
        return self.rk[ename]

    def out_dma(self, eng, out_ap, in_ap, reads, sem):
        op = self.dma(eng, out_ap, in_ap, reads=reads, writes=(), sem=sem)
        self.out_events.append(op)
        return op

    def _skip(self, op, d):
        return (not d.is_dma) and d.eng == op.eng and (not op.is_dma) and (op.eng == "pe" or not SAME_ENGINE_SYNC)

    def finalize(self):
        for e in self.ENGS:
            for op in self.ops[e]:
                for d in op.deps:
                    if d.is_dma or self._skip(op, d):
                        continue
                    d.milestone = True
        for e in self.ENGS:
            n = 0
            for op in self.ops[e]:
                if op.is_dma or op.fn is None:
                    continue
                if op.milestone:
                    n += 1
                op.sem = self.esem[e]
                op.val = n
        nwaits = [0]
        with self.nc.Block() as block:
            def run(ename, eng):
                seen = {}
                for op in self.ops[ename]:
                    need = {}
                    for d in op.deps:
                        if self._skip(op, d):
                            continue
                        v = d.val
                        if d.is_dma and d.final_group:
                            v = self.dsem_count[id(d.sem)]
                        k = id(d.sem)
                        if seen.get(k, 0) >= v:
                            continue
                        if k not in need or need[k][1] < v:
                            need[k] = (d.sem, v)
                    for k, (s, v) in need.items():
                        eng.wait_ge(s, v)
                        seen[k] = v
                        nwaits[0] += 1
                    if op.fn is None:
                        continue
                    self.cur_ename = ename
                    ins = op.fn(eng)
                    if op.is_dma:
                        ins.then_inc(op.sem, op.inc)
                    elif op.milestone:
                        ins.then_inc(op.sem, 1)
                if ename == "sp":
                    fin = {}
                    for op in self.out_events:
                        fin[id(op.sem)] = (op.sem, self.dsem_count[id(op.sem)])
                    for s, v in fin.values():
                        eng.wait_ge(s, v)

            @block.sync
            def _(e):
                run("sp", e)

            @block.gpsimd
            def _(e):
                run("pool", e)

            @block.tensor
            def _(e):
                run("pe", e)

            @block.scalar
            def _(e):
                run("act", e)

            @block.vector
            def _(e):
                run("dve", e)
        self.stats = {e: len(self.ops[e]) for e in self.ENGS}
        self.stats["waits"] = nwaits[0]
        self.stats["sems"] = self.nsem
        for c in reversed(self.closers):
            c.close()
        self.es.close()


def tile_w(W):
    K, N = W.shape
    return np.ascontiguousarray(
        W.reshape(K // 128, 128, N // 128, 128).transpose(2, 1, 0, 3).reshape(N // 128, 128, K))


def fmv(v):
    n = v.shape[-1] // 128
    return np.ascontiguousarray(v.reshape(n, 128).T)


def consts_host():
    c = {}
    c["ident"] = np.eye(128, dtype=np.float32).astype(NPBF)
    s = np.arange(64)
    c["mask64"] = (s[:, None] <= s[None, :]).astype(np.float32)
    return c


class Prog:
    NSLOT = 6
    SLOT = 2048
    GLOBAL_INPUTS = ("ident", "mask64", "memg", "memT")

    def __init__(self, shared=None, pfx="", io=None):
        self.pfx = pfx
        self.io = io or {}
        if shared is None:
            self.root = self
            self.nc = bass.Bass("TRN2", target_bir_lowering=False)
            self.mk = MK(self.nc)
            mk = self.mk
            self.in_specs = {}
            self.gl_inputs = {}
            self.ps = [mk.psum("ps%d" % i, [128, 512], F32) for i in range(8)]
            self.out_sem = mk.new_sem("out", pooled=False)
            self.ones = mk.sbuf("ones", [128, 128], F32)
            self.eps_t = mk.sbuf("eps", [128, 1], F32)
            self.sq = mk.sbuf("sq", [128, 2, 512], F32)
            self.rstd = mk.sbuf("rstd", [128, 2, 512], F32)
        else:
            r = shared.root
            self.root = r
            for k in ("nc", "mk", "in_specs", "gl_inputs", "ps", "out_sem", "ones", "eps_t", "sq", "rstd"):
                setattr(self, k, getattr(r, k))

    def begin(self):
        mk = self.mk
        mk.begin_phase()
        self.wring = mk.sbuf("wring", [128, self.NSLOT, self.SLOT], BF16)
        self.wb = mk.bufs("w", self.NSLOT)
        self.psb = mk.bufs("ps", 8)
        self.onesb = mk.buf("ones")
        self.epsb = mk.buf("eps")
        self.sqb = mk.bufs("sq", 2)
        self.rstdb = mk.bufs("rstd", 2)
        self._slot = 0
        self._ps = 0
        self._rr = {}
        mk.op("dve", "memset", writes=[self.onesb], ap=self.ones[:], constant=1.0)
        mk.op("dve", "memset", writes=[self.epsb], ap=self.eps_t[:], constant=EPS)

    def din(self, name, shape, dtype=F32):
        if name in self.GLOBAL_INPUTS:
            if name not in self.gl_inputs:
                self.in_specs[name] = (tuple(shape), dtype)
                self.gl_inputs[name] = self.nc.dram_tensor(name, list(shape), dtype, kind="ExternalInput").ap()
            return self.gl_inputs[name]
        name = self.pfx + name
        self.in_specs[name] = (tuple(shape), dtype)
        return self.nc.dram_tensor(name, list(shape), dtype, kind="ExternalInput").ap()

    def dout(self, name, shape, dtype=F32):
        return self.nc.dram_tensor(self.pfx + name, list(shape), dtype, kind="ExternalOutput").ap()

    def rr(self, key, n):
        i = self._rr.get(key, 0)
        self._rr[key] = (i + 1) % n
        return i

    def next_ps(self):
        i = self._ps
        self._ps = (i + 1) % 5
        return self.ps[i], self.psb[i]

    def next_ps_long(self):
        i = 5 + self.rr("pslong", 3)
        return self.ps[i], self.psb[i]

    def wload(self, src_ap, n):
        i = self._slot
        self._slot = (i + 1) % self.NSLOT
        self.mk.dma("pool", self.wring[:, i, 0:n], src_ap, writes=[self.wb[i]])
        return self.wring[:, i, :], self.wb[i]

    def const(self, name, shape, dtype=F32, dram=None):
        t = self.mk.sbuf("c_" + name, shape, dtype)
        b = self.mk.buf("c_" + name)
        d = dram if dram is not None else self.din(name, shape, dtype)
        self.mk.dma("sp", t[:], d, writes=[b])
        return t, b

    def linear_fm(self, w_d, jlist, nk, src, srcb, ttiles, evac, k0=0):
        mk = self.mk
        for j in jlist:
            w, wbuf = self.wload(w_d[j, :, k0 * 128:(k0 + nk) * 128], nk * 128)
            for ti in ttiles:
                p, pb = self.next_ps()
                rhs0 = src(0, ti)
                ntok = rhs0.shape[-1]
                for k in range(nk):
                    mk.op("pe", "matmul", reads=[wbuf, srcb(k, ti)], writes=[pb],
                          out=p[:, 0:ntok], lhsT=w[:, k * 128:(k + 1) * 128], rhs=src(k, ti),
                          start=(k == 0), stop=(k == nk - 1))
                evac(j, ti, p, pb)

    def sumsq_bcast(self, srcs, ntok):
        mk = self.mk
        p, pb = self.next_ps()
        n = len(srcs)
        for c, (ap, b) in enumerate(srcs):
            i = self.rr("sq", 2)
            mk.op("act", "activation", reads=[b], writes=[self.sqb[i]],
                  out=self.sq[:, i, 0:ntok], in_=ap, func=AF.Square)
            mk.op("pe", "matmul", reads=[self.onesb, self.sqb[i]], writes=[pb],
                  out=p[:, 0:ntok], lhsT=self.ones[:], rhs=self.sq[:, i, 0:ntok], start=(c == 0), stop=(c == n - 1))
        return p, pb

    def rstd_from(self, p, pb, ntok, n_feat):
        mk = self.mk
        i = self.rr("rstd", 2)
        if getattr(self, "rstd_lnexp", False):
            mk.op("act", "activation", reads=[pb, self.epsb], writes=[self.rstdb[i]],
                  out=self.rstd[:, i, 0:ntok], in_=p[:, 0:ntok], func=AF.Ln, bias=self.eps_t[:, 0:1], scale=1.0 / n_feat)
            mk.op("act", "activation", reads=[self.rstdb[i]], writes=[self.rstdb[i]],
                  out=self.rstd[:, i, 0:ntok], in_=self.rstd[:, i, 0:ntok], func=AF.Exp, scale=-0.5)
            return self.rstd[:, i, 0:ntok], self.rstdb[i]
        mk.op("act", "activation", reads=[pb, self.epsb], writes=[self.rstdb[i]],
              out=self.rstd[:, i, 0:ntok], in_=p[:, 0:ntok], func=AF.Sqrt, bias=self.eps_t[:, 0:1], scale=1.0 / n_feat)
        mk.op("dve", "reciprocal", reads=[self.rstdb[i]], writes=[self.rstdb[i]],
              out=self.rstd[:, i, 0:ntok], in_=self.rstd[:, i, 0:ntok])
        return self.rstd[:, i, 0:ntok], self.rstdb[i]

    def make_eps(self):
        pass

    def rmsnorm(self, src, srcb, dst, dstb, g_ap, nchunk, ntok):
        mk = self.mk
        p, pb = self.sumsq_bcast([(src(c), srcb(c)) for c in range(nchunk)], ntok)
        r, rb = self.rstd_from(p, pb, ntok, nchunk * 128)
        for c in range(nchunk):
            mk.op("dve", "scalar_tensor_tensor", reads=[srcb(c), self.gb, rb], writes=[dstb(c)],
                  out=dst(c), in0=src(c), scalar=g_ap(c), in1=r, op0=ALU.mult, op1=ALU.mult)

    def finish(self):
        self.mk.end_phase()
        if self.root is self:
            self.mk.finalize()
        return self.nc


class ProgA(Prog):
    def __init__(self, first, last, shared=None, pfx="", io=None):
        super().__init__(shared, pfx, io)
        self.begin()
        self.first, self.last = first, last
        mk = self.mk
        io = self.io
        T = TA
        NTH = T // 512
        if "x_src" in io:
            xT_d, xsrcb = io["x_src"]
            xT_d = xT_d.rearrange("(c p) t -> p c t", p=128)
            xsrc_reads = [xsrcb]
        else:
            xT_d = self.din("xT", [D, T]).rearrange("(c p) t -> p c t", p=128)
            xsrc_reads = []
        ng_t, self.gb = self.const("ng", [128, 4, DC])
        x = mk.sbuf("x", [128, DC, T], F32)
        xb = mk.bufs("x", DC, NTH)
        xn = mk.sbuf("xn", [128, DC, T], BF16)
        xnb = mk.bufs("xn", DC, NTH)
        bigB = mk.sbuf("bigB", [128, DC, 512], BF16)
        bigBb = mk.bufs("bigB", DC)
        bigC = mk.sbuf("bigC", [128, DC, 512], BF16)
        bigCb = mk.bufs("bigC", DC)
        sg = mk.sbuf("sg", [128, 2, 512], F32)
        sgb = mk.bufs("sg", 2)
        self.x, self.xb, self.xn, self.xnb = x, xb, xn, xnb

        def load_x():
            for c in range(DC):
                for th in range(NTH):
                    mk.dma("sp", x[:, c, th * 512:(th + 1) * 512], xT_d[:, c, th * 512:(th + 1) * 512], reads=xsrc_reads, writes=[xb[c][th]])
        if first:
            load_x()

        def norm_x(gi):
            for th in range(NTH):
                tsl = slice(th * 512, (th + 1) * 512)
                self.rmsnorm(lambda c: x[:, c, tsl], lambda c: xb[c][th], lambda c: xn[:, c, tsl], lambda c: xnb[c][th],
                             lambda c: ng_t[:, gi, c:c + 1], DC, 512)

        def ffn(wgu_d, wdn_d):
            groups = [(g0, min(g0 + GROUP, FC)) for g0 in range(0, FC, GROUP)]
            for (g0, g1) in groups:
                for c in range(g0, g1):
                    cl = c - g0
                    wg, wgb = self.wload(wgu_d[c, :, :], DC * 128)
                    wu, wub = self.wload(wgu_d[FC + c, :, :], DC * 128)
                    for th in range(NTH):
                        tsl = slice(th * 512, (th + 1) * 512)
                        pg, pgb = self.next_ps()
                        for k in range(DC):
                            mk.op("pe", "matmul", reads=[wgb, xnb[k][th]], writes=[pgb], out=pg[:], lhsT=wg[:, k * 128:(k + 1) * 128],
                                  rhs=xn[:, k, tsl], start=(k == 0), stop=(k == DC - 1))
                        pu, pub = self.next_ps()
                        for k in range(DC):
                            mk.op("pe", "matmul", reads=[wub, xnb[k][th]], writes=[pub], out=pu[:], lhsT=wu[:, k * 128:(k + 1) * 128],
                                  rhs=xn[:, k, tsl], start=(k == 0), stop=(k == DC - 1))
                        i = self.rr("sg", 2)
                        mk.op("act", "activation", reads=[pgb], writes=[sgb[i]], out=sg[:, i, :], in_=pg[:], func=AF.Silu)
                        mk.op("dve", "tensor_tensor", reads=[sgb[i], pub], writes=[bigCb[cl * 2 + th]],
                              out=bigC[:, cl * 2 + th, :], in0=sg[:, i, :], in1=pu[:], op=ALU.mult)
                nk = g1 - g0
                for j in range(DC):
                    wd, wdb = self.wload(wdn_d[j, :, g0 * 128:g1 * 128], nk * 128)
                    for th in range(NTH):
                        tsl = slice(th * 512, (th + 1) * 512)
                        po, pob = self.next_ps()
                        for kk in range(nk):
                            mk.op("pe", "matmul", reads=[wdb, bigCb[kk * 2 + th]], writes=[pob], out=po[:],
                                  lhsT=wd[:, kk * 128:(kk + 1) * 128], rhs=bigC[:, kk * 2 + th, :], start=(kk == 0), stop=(kk == nk - 1))
                        mk.op("dve", "scalar_tensor_tensor", reads=[pob, xb[j][th]], writes=[xb[j][th]],
                              out=x[:, j, tsl], in0=po[:], scalar=0.5, in1=x[:, j, tsl], op0=ALU.mult, op1=ALU.add)

        def add_into_x(th):
            tsl = slice(th * 512, (th + 1) * 512)

            def evac(j, ti, p, pb):
                mk.op("dve", "tensor_tensor", reads=[pb, xb[j][th]], writes=[xb[j][th]],
                      out=x[:, j, tsl], in0=p[:], in1=x[:, j, tsl], op=ALU.add)
            return evac

        if not first:
            if "y_g" not in io:
                yT_d = self.din("yT", [D, T], BF16).rearrange("(c p) t -> p c t", p=128)
            memT_d = self.din("memT", [D, NMEM]).rearrange("(c p) t -> p c t", p=128)
            mg_t, mgb = self.const("memg", [128, DC])
            ident, identb = self.const("ident", [128, 128], BF16)
            wmo_d = self.din("w_mo", [DC, 128, D])
            wq_d = self.din("w_q", [DC, 128, D])
            wkv_d = self.din("w_kv", [2 * DC, 128, D])
            wo_d = self.din("w_o", [DC, 128, D])
            wgu2_d = self.din("w_gu2", [2 * FC, 128, D])
            wdn2_d = self.din("w_dn2", [DC, 128, DFF])
            kT = mk.sbuf("kT", [128, DC, NMEM], BF16)
            kTb = mk.bufs("kT", DC)
            vtm = mk.sbuf("vtm", [128, 2, D], BF16)
            vtmb = mk.bufs("vtm", 2, 4)
            mst = mk.sbuf("mst", [128, 2, NMEM], F32)
            mstb = mk.bufs("mst", 2)
            pT = mk.sbuf("pT", [128, 2, 2, 512], BF16)
            pTb = mk.bufs("pT", 2)
            sm = mk.sbuf("sm", [128, 4, NMEM], F32)
            smb = mk.bufs("sm", 4)
            pbf = mk.sbuf("pbf", [128, 4, NMEM], BF16)
            pbfb = mk.bufs("pbf", 4)
            st1 = mk.sbuf("st1", [128, 16], F32)
            st1b = mk.bufs("st1", 4)
            p, pb = self.next_ps()
            for c in range(DC):
                i = self.rr("mst", 2)
                mk.dma("sp", mst[:, i, :], memT_d[:, c, :], writes=[mstb[i]])
                k = self.rr("sq", 2)
                mk.op("act", "activation", reads=[mstb[i]], writes=[self.sqb[k]], out=self.sq[:, k, 0:NMEM], in_=mst[:, i, :], func=AF.Square)
                mk.op("pe", "matmul", reads=[self.onesb, self.sqb[k]], writes=[pb], out=p[:, 0:NMEM], lhsT=self.ones[:],
                      rhs=self.sq[:, k, 0:NMEM], start=(c == 0), stop=(c == DC - 1))
            r, rb = self.rstd_from(p, pb, NMEM, D)
            for c in range(DC):
                i = self.rr("mst", 2)
                mk.dma("sp", mst[:, i, :], memT_d[:, c, :], writes=[mstb[i]])
                mk.op("dve", "scalar_tensor_tensor", reads=[mstb[i], mgb, rb], writes=[bigBb[c]], out=bigB[:, c, 0:NMEM],
                      in0=mst[:, i, :], scalar=mg_t[:, c:c + 1], in1=r, op0=ALU.mult, op1=ALU.mult)

            load_x()

            def k_item(j):
                w, wbuf = self.wload(wkv_d[j, :, :], D)
                p, pb = self.next_ps()
                for k in range(DC):
                    mk.op("pe", "matmul", reads=[wbuf, bigBb[k]], writes=[pb], out=p[:, 0:NMEM], lhsT=w[:, k * 128:(k + 1) * 128],
                          rhs=bigB[:, k, 0:NMEM], start=(k == 0), stop=(k == DC - 1))
                mk.op("act", "activation", reads=[pb], writes=[kTb[j]], out=kT[:, j, :], in_=p[:, 0:NMEM], func=AF.Copy)

            def v_item(jg):
                ws = [self.wload(wkv_d[DC + jg * 4 + jj, :, :], D) for jj in range(4)]
                for mc in range(2):
                    p, pb = self.next_ps()
                    for jj in range(4):
                        w, wbuf = ws[jj]
                        for k in range(DC):
                            mk.op("pe", "matmul", reads=[wbuf, bigBb[k]], writes=[pb], out=p[:, jj * 128:(jj + 1) * 128],
                                  lhsT=bigB[:, k, mc * 128:(mc + 1) * 128], rhs=w[:, k * 128:(k + 1) * 128], start=(k == 0), stop=(k == DC - 1))
                    mk.op("act", "activation", reads=[pb], writes=[vtmb[mc][jg]], out=vtm[:, mc, jg * 512:(jg + 1) * 512], in_=p[:], func=AF.Copy)

            scale = 512.0 ** -0.5
            THS = list(range(NTH))

            def tsl_(th):
                return slice(th * 512, (th + 1) * 512)
            for th in THS:
                tsl = tsl_(th)
                for c in range(DC):
                    if "y_g" in io:
                        r_, rem = divmod(c, 8)
                        pc, i_ = divmod(rem, 4)
                        yg_ap, ygb = io["y_g"][pc]
                        row0 = r_ * 512 + i_ * 128

                        def in_fn(e, yg_ap=yg_ap, row0=row0, th=th):
                            rk = mk.rank2(e, "sp")
                            return yg_ap[row0:row0 + 128, bass.ds(rk * TA + th * 512, 512)]
                        mk.dma("sp", xn[:, c, tsl], None, reads=[ygb], writes=[xnb[c][th]], in_fn=in_fn)
                    else:
                        mk.dma("sp", xn[:, c, tsl], yT_d[:, c, tsl], writes=[xnb[c][th]])

            def evac_add(j, th, p, pb):
                mk.op("dve", "tensor_tensor", reads=[pb, xb[j][th]], writes=[xb[j][th]],
                      out=x[:, j, tsl_(th)], in0=p[:], in1=x[:, j, tsl_(th)], op=ALU.add)
            kv_items = [lambda jg=jg: v_item(jg) for jg in range(2)] + [lambda j=j: k_item(j) for j in range(DC)] + \
                       [lambda jg=jg: v_item(jg) for jg in range(2, 4)]
            for it in kv_items[:6]:
                it()
            rest = kv_items[6:]
            for j in range(DC):
                self.linear_fm(wmo_d, [j], DC, lambda k, th: xn[:, k, tsl_(th)], lambda k, th: xnb[k][th], THS, evac_add)
                if rest:
                    rest.pop(0)()
            for it in rest:
                it()
            for th in THS:
                tsl = tsl_(th)
                self.rmsnorm(lambda c: x[:, c, tsl], lambda c: xb[c][th], lambda c: xn[:, c, tsl], lambda c: xnb[c][th],
                             lambda c: ng_t[:, 0, c:c + 1], DC, 512)
            qbuf = [(bigB, bigBb), (bigC, bigCb)]

            def evac_q(j, th, p, pb):
                qd, qdb = qbuf[th]
                mk.op("act", "activation", reads=[pb], writes=[qdb[j]], out=qd[:, j, :], in_=p[:], func=AF.Copy)
            self.linear_fm(wq_d, range(DC), DC, lambda k, th: xn[:, k, tsl_(th)], lambda k, th: xnb[k][th], THS, evac_q)
            for th in THS:
                tsl = tsl_(th)
                qd, qdb = qbuf[th]
                for hd in range(4):
                    pi = self.rr("pT", 2)
                    def chain(tt, hd=hd, qd=qd, qdb=qdb, pi=pi):
                        ps_, psb_ = self.next_ps()
                        for kc in range(4):
                            ch = hd * 4 + kc
                            mk.op("pe", "matmul", reads=[qdb[ch], kTb[ch]], writes=[psb_], out=ps_[:, 0:NMEM],
                                  lhsT=qd[:, ch, tt * 128:(tt + 1) * 128], rhs=kT[:, ch, :], start=(kc == 0), stop=(kc == 3))
                        si = tt
                        mi = tt
                        yield
                        mk.op("dve", "reduce_max", reads=[psb_], writes=[st1b[si]], out=st1[:, si * 4:si * 4 + 1], in_=ps_[:, 0:NMEM], axis=AX.X)
                        yield
                        mk.op("dve", "tensor_scalar", reads=[st1b[si]], writes=[st1b[si]], out=st1[:, si * 4 + 1:si * 4 + 2],
                              in0=st1[:, si * 4:si * 4 + 1], scalar1=-scale, scalar2=None, op0=ALU.mult)
                        yield
                        mk.op("act", "activation", reads=[psb_, st1b[si]], writes=[smb[mi], st1b[si]], out=sm[:, mi, :], in_=ps_[:, 0:NMEM],
                              func=AF.Exp, bias=st1[:, si * 4 + 1:si * 4 + 2], scale=scale, accum_out=st1[:, si * 4 + 2:si * 4 + 3])
                        yield
                        mk.op("dve", "reciprocal", reads=[st1b[si]], writes=[st1b[si]], out=st1[:, si * 4 + 3:si * 4 + 4],
                              in_=st1[:, si * 4 + 2:si * 4 + 3])
                        yield
                        mk.op("dve", "tensor_scalar", reads=[smb[mi], st1b[si]], writes=[pbfb[mi]], out=pbf[:, mi, :], in0=sm[:, mi, :],
                              scalar1=st1[:, si * 4 + 3:si * 4 + 4], scalar2=None, op0=ALU.mult)
                        yield
                        pt_, ptb_ = self.next_ps()
                        ptv = pt_.bitcast(BF16)
                        for mc in range(2):
                            mk.op("pe", "transpose", reads=[pbfb[mi], identb], writes=[ptb_], out=ptv[:, mc * 128:(mc + 1) * 128],
                                  in_=pbf[:, mi, mc * 128:(mc + 1) * 128], identity=ident[:])
                        yield
                        for mc in range(2):
                            mk.op("act", "activation", reads=[ptb_], writes=[pTb[pi]], out=pT[:, pi, mc, tt * 128:(tt + 1) * 128],
                                  in_=ptv[:, mc * 128:(mc + 1) * 128], func=AF.Copy)
                    gens = [chain(tt) for tt in range(4)]
                    while gens:
                        for g in list(gens):
                            try:
                                next(g)
                            except StopIteration:
                                gens.remove(g)
                    for dc in range(4):
                        ch = hd * 4 + dc
                        po, pob = self.next_ps()
                        for mc in range(2):
                            mk.op("pe", "matmul", reads=[vtmb[mc][hd], pTb[pi]], writes=[pob], out=po[:],
                                  lhsT=vtm[:, mc, ch * 128:(ch + 1) * 128], rhs=pT[:, pi, mc, :], start=(mc == 0), stop=(mc == 1))
                        mk.op("act", "activation", reads=[pob], writes=[xnb[ch][th]], out=xn[:, ch, tsl], in_=po[:], func=AF.Copy)
            self.linear_fm(wo_d, range(DC), DC, lambda k, th: xn[:, k, tsl_(th)], lambda k, th: xnb[k][th], THS, evac_add)
            norm_x(1)
            ffn(wgu2_d, wdn2_d)

        if not last:
            wgu1_d = self.din("w_gu1", [2 * FC, 128, D])
            wdn1_d = self.din("w_dn1", [DC, 128, DFF])
            norm_x(2)
            ffn(wgu1_d, wdn1_d)
            norm_x(3)
            if "x_dst" in io:
                for pc in range(2):
                    (hp, hpb), (hg, hgb) = io["hn_p"][pc], io["hn_g"][pc]
                    hpv = hp.rearrange("(c p) t -> p c t", p=128)
                    for c in range(DC):
                        mk.dma("sp", hpv[:, c, :], xn[:, c, pc * 512:(pc + 1) * 512], reads=[xnb[c][pc]], writes=[hpb])
                    mk.cc_allgather(hp, hg, reads=[hpb], writes=[hgb], rg=io["rg"])
                xd, xdb = io["x_dst"]
                xd = xd.rearrange("(c p) t -> p c t", p=128)
                for c in range(DC):
                    mk.dma("sp", xd[:, c, :], x[:, c, :], reads=xb[c], writes=[xdb])
            else:
                xo_d = self.dout("xT_out", [D, T]).rearrange("(c p) t -> p c t", p=128)
                hn_d = self.dout("hnT_out", [D, T], BF16).rearrange("(c p) t -> p c t", p=128)
                for c in range(DC):
                    mk.out_dma("sp", xo_d[:, c, :], x[:, c, :], reads=xb[c], sem=self.out_sem)
                    mk.out_dma("sp", hn_d[:, c, :], xn[:, c, :], reads=xnb[c], sem=self.out_sem)
        else:
            fo_d = self.dout("outT", [D, T]).rearrange("(c p) t -> p c t", p=128)
            for th in range(NTH):
                tsl = slice(th * 512, (th + 1) * 512)
                p, pb = self.sumsq_bcast([(x[:, c, tsl], xb[c][th]) for c in range(DC)], 512)
                r, rb = self.rstd_from(p, pb, 512, D)
                for c in range(DC):
                    mk.op("dve", "scalar_tensor_tensor", reads=[xb[c][th], self.gb, rb], writes=[xb[c][th]],
                          out=x[:, c, tsl], in0=x[:, c, tsl], scalar=ng_t[:, 2, c:c + 1], in1=r, op0=ALU.mult, op1=ALU.mult)
            for c in range(DC):
                mk.out_dma("sp", fo_d[:, c, :], x[:, c, :], reads=xb[c], sem=self.out_sem)
        self.finish()


def bc_chunks(t, col, nch, clen, pstep=None):
    base = t[:, 0:1]
    ps = base.ap[0][0]
    return bass.AP(t, base.offset + col, [[ps, 128], [clen, nch], [0, clen]])


def v3(ap2, nch, clen):
    return ap2.rearrange("p (c i) -> p c i", i=clen)


class ProgM(Prog):
    NSLOT = 4

    def __init__(self, shared=None, pfx="", io=None):
        super().__init__(shared, pfx, io)
        self.begin()
        mk = self.mk
        io = self.io
        self.hn = mk.sbuf("hn", [128, DC, SEQ], BF16)
        self.hnb = mk.bufs("hn", DC, 4)
        if "hn_g" in io:
            for ti in range(4):
                r_, pc = divmod(ti, 2)
                hg, hgb = io["hn_g"][pc]
                hgv = hg[r_ * D:(r_ + 1) * D, :].rearrange("(c p) t -> p c t", p=128)
                for c in range(DC):
                    mk.dma("sp", self.hn[:, c, ti * 512:(ti + 1) * 512], hgv[:, c, :], reads=[hgb], writes=[self.hnb[c][ti]])
            self.y_cnt = [0, 0]
        else:
            hnT_d = self.din("hnT", [D, SEQ], BF16).rearrange("(c p) t -> p c t", p=128)
            for c in range(DC):
                for ti in range(4):
                    mk.dma("sp", self.hn[:, c, ti * 512:(ti + 1) * 512], hnT_d[:, c, ti * 512:(ti + 1) * 512], writes=[self.hnb[c][ti]])
            self.y_d = self.dout("yT_out", [D // 2, SEQ], BF16).rearrange("(c p) t -> p c t", p=128)
        self.ybuf = mk.sbuf("ybuf", [128, 2, 512], BF16)
        self.ybufb = mk.bufs("ybuf", 2)

    def hsrc(self):
        return (lambda k, ti: self.hn[:, k, ti * 512:(ti + 1) * 512]), (lambda k, ti: self.hnb[k][ti])

    def store_y(self, ch, ti, i):
        io = self.io
        if "y_p" in io:
            pc, cl = divmod(ch, 4)
            (yp, ypb), (yg, ygb) = io["y_p"][pc], io["y_g"][pc]
            self.mk.dma("sp", yp[cl * 128:(cl + 1) * 128, ti * 512:(ti + 1) * 512], self.ybuf[:, i, :], reads=[self.ybufb[i]], writes=[ypb])
            self.y_cnt[pc] += 1
            if self.y_cnt[pc] == 16:
                self.mk.cc_allgather(yp, yg, reads=[ypb], writes=[ygb], rg=io["rg"])
        else:
            self.mk.out_dma("sp", self.y_d[:, ch, ti * 512:(ti + 1) * 512], self.ybuf[:, i, :], reads=[self.ybufb[i]], sem=self.out_sem)


class ProgHGRN(ProgM):
    NSLOT = 8

    def __init__(self, layer, shared=None, pfx="", io=None):
        super().__init__(shared, pfx, io)
        mk = self.mk
        NH = 8
        win_d = self.din("w_in", [4 * NH, 128, D])
        lbp, lbpb = self.const("lbp", [128, 4, NH])
        gn, gnb = self.const("gn", [128, 1])
        self.gb = gnb
        ident, identb = self.const("ident", [128, 128], BF16)
        mask, maskb = self.const("mask64", [64, 64])
        sm = mk.sbuf("lbs", [128, 8, NH], F32)
        smb = mk.buf("lbs")
        R_ = [lbpb, smb]

        def tt(o, a, b, op):
            mk.op("dve", "tensor_tensor", reads=R_, writes=[smb], out=o, in0=a, in1=b, op=op)
        tt(sm[:, 0, :], lbp[:, 0, :], lbp[:, 1, :], ALU.max)
        tt(sm[:, 0, :], sm[:, 0, :], lbp[:, 2, :], ALU.max)
        tt(sm[:, 0, :], sm[:, 0, :], lbp[:, 3, :], ALU.max)
        for i in range(4):
            tt(sm[:, 1 + i, :], lbp[:, i, :], sm[:, 0, :], ALU.subtract)
            mk.op("act", "activation", reads=[smb], writes=[smb], out=sm[:, 1 + i, :], in_=sm[:, 1 + i, :], func=AF.Exp)
        tt(sm[:, 5, :], sm[:, 1, :], sm[:, 2, :], ALU.add)
        tt(sm[:, 5, :], sm[:, 5, :], sm[:, 3, :], ALU.add)
        tt(sm[:, 5, :], sm[:, 5, :], sm[:, 4, :], ALU.add)
        mk.op("dve", "reciprocal", reads=[smb], writes=[smb], out=sm[:, 5, :], in_=sm[:, 5, :])
        mk.op("dve", "memset", reads=[smb], writes=[smb], ap=sm[:, 6, :], constant=0.0)
        for i in range(1, layer + 1):
            tt(sm[:, 6, :], sm[:, 6, :], sm[:, 1 + i, :], ALU.add)
        tt(sm[:, 6, :], sm[:, 6, :], sm[:, 5, :], ALU.mult)
        mk.op("dve", "tensor_scalar", reads=[smb], writes=[smb], out=sm[:, 7, :], in0=sm[:, 6, :], scalar1=-1.0, scalar2=1.0,
              op0=ALU.mult, op1=ALU.add)
        mk.op("dve", "tensor_scalar", reads=[smb], writes=[smb], out=sm[:, 0, :], in0=sm[:, 7, :], scalar1=-1.0, scalar2=None,
              op0=ALU.mult)
        LB, OML, NOML = 6, 7, 0

        def tbuf(name, dt=F32):
            return mk.sbuf(name, [128, 2, 512], dt), mk.bufs(name, 2)

        def pbuf(name, dt=BF16):
            return mk.sbuf(name, [128, SEQ], dt), mk.bufs(name, 4)
        qs, qsb = tbuf("qs")
        sg_, sgb_ = tbuf("sgm")
        kk, kkb = tbuf("kk")
        bb, bbb = tbuf("bb")
        t1, t1b = tbuf("t1")
        ex, exb = tbuf("ex")
        oo, oob = tbuf("oo")
        vf, vfb = tbuf("vf", BF16)
        qt, qtb = pbuf("qt")
        kt, ktb = pbuf("kt")
        qh, qhb = pbuf("qh")
        kh, khb = pbuf("kh")
        gs, gsb = pbuf("gs")
        vtm = mk.sbuf("vtm", [64, 32, 128], BF16)
        vtmb = mk.bufs("vtm", 4)
        ktm = mk.sbuf("ktm", [64, 32, 128], BF16)
        ktmb = mk.bufs("ktm", 4)
        scs = mk.sbuf("scs", [64, 2, 8, 64], BF16)
        scsb = mk.bufs("scs", 2)
        dec = mk.sbuf("dec", [128, 32], F32)
        decb = mk.bufs("dec", 4)
        S = mk.sbuf("S", [128, 2, 128], F32)
        Sb = mk.bufs("S", 2)
        Sall = mk.sbuf("Sall", [128, 33, 128], BF16)
        Sallb = mk.bufs("Sall", 4)
        S0b = mk.buf("Sall0")
        mk.op("dve", "memset", writes=[S0b], ap=Sall[:, 0, :], constant=0.0)
        onesr = mk.sbuf("onesr", [128, 64], F32)
        onesrb = mk.buf("onesr")
        mk.op("dve", "memset", writes=[onesrb], ap=onesr[:], constant=1.0)
        tmp = mk.sbuf("tmpy", [128, 2, 512], F32)
        tmpb = mk.bufs("tmpy", 2)
        wt = {}
        st1 = {}

        def sl(ti):
            return slice(ti * 512, (ti + 1) * 512)

        def stage1(hd, ti):
            if ti == 0:
                wt[hd] = [self.wload(win_d[kind * NH + hd, :, :], D) for kind in range(4)]
            i = self.rr("tl", 2)
            s_ = sl(ti)

            def proj(kind, evac):
                w, wbuf = wt[hd][kind]
                p, pb = self.next_ps()
                for k in range(DC):
                    mk.op("pe", "matmul", reads=[wbuf, self.hnb[k][ti]], writes=[pb], out=p[:], lhsT=w[:, k * 128:(k + 1) * 128],
                          rhs=self.hn[:, k, s_], start=(k == 0), stop=(k == DC - 1))
                evac(p, pb)
            proj(1, lambda p, pb: mk.op("act", "activation", reads=[pb], writes=[sgb_[i]], out=sg_[:, i, :], in_=p[:], func=AF.Sigmoid))
            proj(0, lambda p, pb: mk.op("act", "activation", reads=[pb], writes=[qsb[i]], out=qs[:, i, :], in_=p[:], func=AF.Silu))
            mk.op("dve", "tensor_scalar", reads=[sgb_[i], smb], writes=[kkb[i]], out=kk[:, i, :], in0=sg_[:, i, :],
                  scalar1=sm[:, NOML, hd:hd + 1], scalar2=sm[:, OML, hd:hd + 1], op0=ALU.mult, op1=ALU.add)
            mk.op("dve", "tensor_scalar", reads=[sgb_[i], smb], writes=[sgb_[i]], out=sg_[:, i, :], in0=sg_[:, i, :],
                  scalar1=sm[:, OML, hd:hd + 1], scalar2=sm[:, LB, hd:hd + 1], op0=ALU.mult, op1=ALU.add)
            mk.op("dve", "tensor_scalar", reads=[sgb_[i]], writes=[sgb_[i]], out=sg_[:, i, :], in0=sg_[:, i, :],
                  scalar1=1e-12, scalar2=None, op0=ALU.max)
            mk.op("act", "activation", reads=[sgb_[i]], writes=[sgb_[i]], out=sg_[:, i, :], in_=sg_[:, i, :], func=AF.Ln)
            proj(2, lambda p, pb: mk.op("act", "activation", reads=[pb], writes=[vfb[i]], out=vf[:, i, :], in_=p[:], func=AF.Copy))
            for c in range(8):
                cs = slice(c * 64, (c + 1) * 64)
                mk.op("dve", "tensor_tensor_scan", reads=[sgb_[i], onesrb], writes=[bbb[i]], out=bb[:, i, cs], data0=onesr[:, :],
                      data1=sg_[:, i, cs], initial=0.0, op0=ALU.mult, op1=ALU.add)
            pstep = bb[:, 0, 0:1].ap[0][0]
            b3 = v3(bb[:, i, :], 8, 64)
            bref = bass.AP(bb, bb[:, i, 31:32].offset, [[pstep, 128], [64, 8], [0, 64]])
            blast = bass.AP(bb, bb[:, i, 63:64].offset, [[pstep, 128], [64, 8], [0, 64]])
            bl2 = bass.AP(bb, bb[:, i, 63:64].offset, [[pstep, 128], [64, 8]])
            mk.op("dve", "tensor_tensor", reads=[bbb[i]], writes=[t1b[i]], out=v3(t1[:, i, :], 8, 64), in0=b3, in1=bref, op=ALU.subtract)
            mk.op("act", "activation", reads=[t1b[i]], writes=[exb[i]], out=ex[:, i, :], in_=t1[:, i, :], func=AF.Exp)
            mk.op("dve", "tensor_tensor", reads=[exb[i], qsb[i]], writes=[qtb[ti]], out=qt[:, s_], in0=ex[:, i, :], in1=qs[:, i, :], op=ALU.mult)
            mk.op("act", "activation", reads=[t1b[i]], writes=[exb[i]], out=ex[:, i, :], in_=t1[:, i, :], func=AF.Exp, scale=-1.0)
            mk.op("dve", "tensor_tensor", reads=[exb[i], kkb[i]], writes=[ktb[ti]], out=kt[:, s_], in0=ex[:, i, :], in1=kk[:, i, :], op=ALU.mult)
            proj(3, lambda p, pb: mk.op("act", "activation", reads=[pb], writes=[gsb[ti]], out=gs[:, s_], in_=p[:], func=AF.Silu))
            mk.op("act", "activation", reads=[bbb[i]], writes=[exb[i]], out=ex[:, i, :], in_=bb[:, i, :], func=AF.Exp)
            mk.op("dve", "tensor_tensor", reads=[exb[i], qsb[i]], writes=[qhb[ti]], out=qh[:, s_], in0=ex[:, i, :], in1=qs[:, i, :], op=ALU.mult)
            mk.op("dve", "tensor_tensor", reads=[bbb[i]], writes=[t1b[i]], out=v3(t1[:, i, :], 8, 64), in0=blast, in1=b3, op=ALU.subtract)
            mk.op("act", "activation", reads=[t1b[i]], writes=[exb[i]], out=ex[:, i, :], in_=t1[:, i, :], func=AF.Exp)
            mk.op("dve", "tensor_tensor", reads=[exb[i], kkb[i]], writes=[khb[ti]], out=kh[:, s_], in0=ex[:, i, :], in1=kk[:, i, :], op=ALU.mult)
            mk.op("act", "activation", reads=[bbb[i]], writes=[decb[ti]], out=dec[:, ti * 8:(ti + 1) * 8], in_=bl2, func=AF.Exp)
            st1[(hd, ti)] = i

        st2 = {}

        def stage2a(hd, ti):
            s_ = sl(ti)
            if ti == 0:
                mk.op("dve", "memset", writes=[Sb[1]], ap=S[:, 1, :], constant=0.0)
            i1 = st1.pop((hd, ti))
            for (srcf, srcfb, dstt, dsttb) in ((kh[:, s_], khb[ti], ktm, ktmb), (vf[:, i1, :], vfb[i1], vtm, vtmb)):
                p, pb = self.next_ps()
                pv = p.bitcast(BF16)
                for c in range(8):
                    mk.op("pe", "transpose", reads=[srcfb, identb], writes=[pb], out=pv[0:64, c * 128:(c + 1) * 128],
                          in_=srcf[:, c * 64:(c + 1) * 64], identity=ident[:])
                mk.op("act", "activation", reads=[pb], writes=[dsttb[ti]], out=dstt[:, ti * 8:(ti + 1) * 8, :],
                      in_=pv[0:64, :].rearrange("p (c i) -> p c i", i=128), func=AF.Copy)

            p, pb = self.next_ps()
            for c in range(8):
                cs = slice(ti * 512 + c * 64, ti * 512 + (c + 1) * 64)
                mk.op("pe", "matmul", reads=[ktb[ti], qtb[ti]], writes=[pb], out=p[0:64, c * 64:(c + 1) * 64], lhsT=kt[:, cs], rhs=qt[:, cs],
                      start=True, stop=True)
            si = self.rr("scs", 2)
            mbc = bass.AP(mask, mask[:, 0:1].offset, [[mask[:, 0:1].ap[0][0], 64], [0, 8], [1, 64]])
            mk.op("dve", "tensor_tensor", reads=[pb, maskb], writes=[scsb[si]], out=scs[:, si, :, :], in0=v3(p[0:64, :], 8, 64), in1=mbc, op=ALU.mult)
            pst = [self.next_ps(), self.next_ps()]
            for c in range(8):
                cg = ti * 8 + c
                pp, ppb = pst[c // 4]
                mk.op("pe", "matmul", reads=[ktmb[ti], vtmb[ti]], writes=[ppb], out=pp[:, (c % 4) * 128:(c % 4 + 1) * 128], lhsT=ktm[:, cg, :],
                      rhs=vtm[:, cg, :], start=True, stop=True)
            for c in range(8):
                cg = ti * 8 + c
                par = cg % 2
                pp, ppb = pst[c // 4]
                mk.op("dve", "scalar_tensor_tensor", reads=[Sb[1 - par], decb[ti], ppb], writes=[Sb[par]], out=S[:, par, :], in0=S[:, 1 - par, :],
                      scalar=dec[:, cg:cg + 1], in1=pp[:, (c % 4) * 128:(c % 4 + 1) * 128], op0=ALU.mult, op1=ALU.add)
                mk.op("act", "activation", reads=[Sb[par]], writes=[Sallb[ti]], out=Sall[:, cg + 1, :], in_=S[:, par, :], func=AF.Copy)
            st2[(hd, ti)] = si

        def stage2b(hd, ti):
            s_ = sl(ti)
            si = st2.pop((hd, ti))
            po, pob = self.next_ps_long()
            for c in range(8):
                cg = ti * 8 + c
                cs = slice(ti * 512 + c * 64, ti * 512 + (c + 1) * 64)
                prevb = S0b if cg == 0 else (Sallb[ti - 1] if c == 0 else Sallb[ti])
                mk.op("pe", "matmul", reads=[vtmb[ti], scsb[si]], writes=[pob], out=po[:, c * 64:(c + 1) * 64], lhsT=vtm[:, cg, :], rhs=scs[:, si, c, :],
                      start=True, stop=False)
                mk.op("pe", "matmul", reads=[prevb, qhb[ti]], writes=[pob], out=po[:, c * 64:(c + 1) * 64], lhsT=Sall[:, cg, :], rhs=qh[:, cs],
                      start=False, stop=True)
            oi = self.rr("oo", 2)
            mk.op("act", "activation", reads=[pob], writes=[oob[oi]], out=oo[:, oi, :], in_=po[:], func=AF.Copy)
            p2, p2b = self.sumsq_bcast([(oo[:, oi, :], oob[oi])], 512)
            r, rb = self.rstd_from(p2, p2b, 512, 128)
            i = self.rr("ybuf", 2)
            mk.op("dve", "scalar_tensor_tensor", reads=[oob[oi], gnb, rb], writes=[tmpb[i]], out=tmp[:, i, :], in0=oo[:, oi, :], scalar=gn[:, 0:1],
                  in1=r, op0=ALU.mult, op1=ALU.mult)
            mk.op("dve", "tensor_tensor", reads=[tmpb[i], gsb[ti]], writes=[self.ybufb[i]], out=self.ybuf[:, i, :], in0=tmp[:, i, :], in1=gs[:, s_], op=ALU.mult)
            self.store_y(hd, ti, i)

        units = [(hd, ti) for hd in range(NH) for ti in range(4)]
        for n in range(len(units) + 1):
            if n >= 1:
                stage2a(*units[n - 1])
            if n < len(units):
                stage1(*units[n])
            if n >= 1:
                stage2b(*units[n - 1])
        self.finish()


class ProgLRU(ProgM):
    def __init__(self, shared=None, pfx="", io=None):
        super().__init__(shared, pfx, io)
        mk = self.mk
        NCH = 8
        win_d = self.din("w_in", [2 * NCH, 128, D])
        wax_d = self.din("w_ax", [2 * NCH, 128, 256])
        cw, cwb = self.const("conv_w", [128, 4, NCH])
        vec, vecb = self.const("vec", [128, 4, NCH])
        cl = mk.sbuf("clam", [128, 2, NCH], F32)
        clb = mk.buf("clam")
        one_t = mk.sbuf("one_t", [128, 1], F32)
        oneb = mk.buf("one_t")
        mk.op("dve", "memset", writes=[oneb], ap=one_t[:], constant=1.0)
        mk.op("act", "activation", reads=[vecb], writes=[clb], out=cl[:, 0, :], in_=vec[:, 3, :], func=AF.Exp, scale=-1.0)
        mk.op("act", "activation", reads=[clb, oneb], writes=[clb], out=cl[:, 0, :], in_=cl[:, 0, :], func=AF.Ln, bias=one_t[:, 0:1])
        mk.op("dve", "tensor_scalar", reads=[clb], writes=[clb], out=cl[:, 1, :], in0=cl[:, 0, :], scalar1=-16.0, scalar2=None, op0=ALU.mult)
        mk.op("dve", "tensor_scalar", reads=[clb], writes=[clb], out=cl[:, 0, :], in0=cl[:, 0, :], scalar1=-8.0, scalar2=None, op0=ALU.mult)

        def fbuf(name, dt=F32, n=4):
            return mk.sbuf(name, [128, SEQ], dt), mk.bufs(name, n)
        u, ub = fbuf("u")
        uc = [fbuf("uc%d" % i, F32, 1) for i in range(2)]
        u16 = [fbuf("u16_%d" % i, BF16, 1) for i in range(2)]
        r_, rb_ = fbuf("r")
        ig, igb = fbuf("ig")
        a_, ab_ = fbuf("a")
        t_, tb_ = fbuf("t")
        h_, hb_ = fbuf("h")
        gx, gxb = fbuf("gx")
        src, srcb = self.hsrc()
        T4 = range(4)

        def sl(ti):
            return slice(ti * 512, (ti + 1) * 512)

        def ev_act(dst, dstb, func, bias=None, extra=()):
            def evac(j, ti, p, pb):
                kw = {}
                if bias is not None:
                    kw["bias"] = bias
                mk.op("act", "activation", reads=[pb] + list(extra), writes=[dstb[ti]], out=dst[:, sl(ti)], in_=p[:], func=func, **kw)
            return evac

        for bl in range(4):
            for q in range(2):
                cc = bl * 2 + q
                ucq, ucqb = uc[q]
                self.linear_fm(win_d, [NCH + cc], DC, src, srcb, T4, ev_act(u, ub, AF.Copy))
                mk.op("dve", "tensor_scalar", reads=ub + [cwb, vecb], writes=[ucqb[0]], out=ucq[:, :], in0=u[:, :], scalar1=cw[:, 3, cc:cc + 1],
                      scalar2=vec[:, 0, cc:cc + 1], op0=ALU.mult, op1=ALU.add)
                for sh in (1, 2, 3):
                    mk.op("dve", "scalar_tensor_tensor", reads=ub + [cwb, ucqb[0]], writes=[ucqb[0]], out=ucq[:, sh:], in0=u[:, 0:SEQ - sh],
                          scalar=cw[:, 3 - sh, cc:cc + 1], in1=ucq[:, sh:], op0=ALU.mult, op1=ALU.add)
                mk.op("act", "activation", reads=[ucqb[0]], writes=[u16[q][1][0]], out=u16[q][0][:, :], in_=ucq[:, :], func=AF.Copy)
            for q in range(2):
                cc = bl * 2 + q
                ucq, ucqb = uc[q]

                def usrc(k, ti):
                    return u16[k][0][:, sl(ti)]

                def usrcb(k, ti):
                    return u16[k][1][0]
                self.linear_fm(wax_d, [cc], 2, usrc, usrcb, T4, ev_act(r_, rb_, AF.Sigmoid, bias=vec[:, 1, cc:cc + 1], extra=[vecb]))
                self.linear_fm(wax_d, [NCH + cc], 2, usrc, usrcb, T4, ev_act(ig, igb, AF.Sigmoid, bias=vec[:, 2, cc:cc + 1], extra=[vecb]))
                self.linear_fm(win_d, [cc], DC, src, srcb, T4, ev_act(gx, gxb, AF.Copy))
                def chain(ti, cc=cc, ucq=ucq, ucqb=ucqb):
                    s_ = sl(ti)
                    mk.op("act", "activation", reads=[rb_[ti], clb], writes=[ab_[ti]], out=a_[:, s_], in_=r_[:, s_], func=AF.Exp, scale=cl[:, 0, cc:cc + 1])
                    yield
                    mk.op("act", "activation", reads=[rb_[ti], clb], writes=[tb_[ti]], out=t_[:, s_], in_=r_[:, s_], func=AF.Exp, scale=cl[:, 1, cc:cc + 1])
                    yield
                    mk.op("dve", "tensor_scalar", reads=[tb_[ti]], writes=[tb_[ti]], out=t_[:, s_], in0=t_[:, s_], scalar1=-1.0, scalar2=1.0,
                          op0=ALU.mult, op1=ALU.add)
                    yield
                    mk.op("dve", "tensor_scalar", reads=[tb_[ti]], writes=[tb_[ti]], out=t_[:, s_], in0=t_[:, s_], scalar1=0.0, scalar2=None, op0=ALU.max)
                    yield
                    mk.op("act", "activation", reads=[tb_[ti]], writes=[tb_[ti]], out=t_[:, s_], in_=t_[:, s_], func=AF.Sqrt)
                    yield
                    mk.op("dve", "tensor_tensor", reads=[tb_[ti], igb[ti]], writes=[tb_[ti]], out=t_[:, s_], in0=t_[:, s_], in1=ig[:, s_], op=ALU.mult)
                    yield
                    mk.op("dve", "tensor_tensor", reads=[tb_[ti], ucqb[0]], writes=[tb_[ti]], out=t_[:, s_], in0=t_[:, s_], in1=ucq[:, s_], op=ALU.mult)
                    yield
                    init = 0.0 if ti == 0 else h_[:, ti * 512 - 1:ti * 512]
                    rd = [ab_[ti], tb_[ti]] + ([hb_[ti - 1]] if ti > 0 else [])
                    yield
                    mk.op("dve", "tensor_tensor_scan", reads=rd, writes=[hb_[ti]], out=h_[:, s_], data0=a_[:, s_], data1=t_[:, s_], initial=init,
                          op0=ALU.mult, op1=ALU.add)
                    yield
                    yield
                    mk.op("dve", "tensor_tensor", reads=[gxb[ti]], writes=[ab_[ti]], out=a_[:, s_], in0=gx[:, s_], in1=gx[:, s_], op=ALU.mult)
                    yield
                    mk.op("dve", "tensor_scalar", reads=[ab_[ti]], writes=[ab_[ti]], out=a_[:, s_], in0=a_[:, s_], scalar1=0.044715, scalar2=1.0,
                          op0=ALU.mult, op1=ALU.add)
                    yield
                    mk.op("dve", "tensor_tensor", reads=[ab_[ti], gxb[ti]], writes=[ab_[ti]], out=a_[:, s_], in0=a_[:, s_], in1=gx[:, s_], op=ALU.mult)
                    yield
                    mk.op("act", "activation", reads=[ab_[ti]], writes=[ab_[ti]], out=a_[:, s_], in_=a_[:, s_], func=AF.Sigmoid, scale=2.0 * 0.7978845608028654)
                    yield
                    mk.op("dve", "tensor_tensor", reads=[ab_[ti], gxb[ti]], writes=[ab_[ti]], out=a_[:, s_], in0=a_[:, s_], in1=gx[:, s_], op=ALU.mult)
                    yield
                    i = self.rr("ybuf", 2)
                    yield
                    mk.op("dve", "tensor_tensor", reads=[ab_[ti], hb_[ti]], writes=[self.ybufb[i]], out=self.ybuf[:, i, :], in0=a_[:, s_], in1=h_[:, s_], op=ALU.mult)
                    self.store_y(cc, ti, i)
                    yield
                gens = [chain(ti) for ti in T4]
                while gens:
                    for g in list(gens):
                        try:
                            next(g)
                        except StopIteration:
                            gens.remove(g)
        self.finish()


class ProgMLSTM(ProgM):
    def __init__(self, shared=None, pfx="", io=None):
        super().__init__(shared, pfx, io)
        mk = self.mk
        NH = 4
        NCK = 32
        win_d = self.din("w_in", [24, 128, D])
        wg, wgb = self.const("w_gate", [128, DC, 8])
        bif, bifb = self.const("b_if", [4, 2])
        gn, gnb = self.const("gn", [128, 2])
        self.gb = gnb
        sel, selb = self.const("sel", [4, 4 * 128])
        ident, identb = self.const("ident", [128, 128], BF16)
        mask, maskb = self.const("mask64", [64, 64])
        wg16 = mk.sbuf("wg16", [128, DC, 8], BF16)
        wg16b = mk.buf("wg16")
        mk.op("act", "activation", reads=[wgb], writes=[wg16b], out=wg16[:], in_=wg[:], func=AF.Copy)
        one4 = mk.sbuf("one4", [4, 1], F32)
        one4b = mk.buf("one4")
        mk.op("dve", "memset", writes=[one4b], ap=one4[:], constant=1.0)
        b15 = mk.sbuf("b15", [4, 2], F32)
        b15b = mk.buf("b15")
        mk.op("dve", "tensor_scalar", reads=[bifb], writes=[b15b], out=b15[:], in0=bif[:], scalar1=1.0 / 15.0, scalar2=None, op0=ALU.mult)
        src, srcb = self.hsrc()
        T4 = range(4)

        def sl(ti):
            return slice(ti * 512, (ti + 1) * 512)

        def row(name):
            return mk.sbuf(name, [4, SEQ], F32), mk.buf(name)
        it, itb = row("g_it")
        bn, bnb = row("g_bn")
        aa, aab = row("g_a")
        AA, AAb = row("g_A")
        tr, trb = row("g_tr")
        qf, qfb = it, itb
        kf, kfb = aa, aab
        em, emb = bn, bnb
        zr4 = mk.sbuf("g_zero", [4, 1], F32)
        zrb = mk.buf("g_zero")
        mk.op("dve", "memset", writes=[zrb], ap=zr4[:], constant=0.0)
        zr_bc = bass.AP(zr4, zr4[:, 0:1].offset, [[zr4[:, 0:1].ap[0][0], 4], [0, SEQ]])
        R33 = mk.sbuf("g_R33", [4, NCK + 1], F32)
        R33b = mk.buf("g_R33")
        dec4 = mk.sbuf("g_dec", [4, NCK], F32)
        dec4b = mk.buf("g_dec")
        for gi, dst, dstb in ((0, it, itb), (1, bn, bnb)):
            for ti in T4:
                p, pb = self.next_ps()
                for k in range(DC):
                    mk.op("pe", "matmul", reads=[wg16b, self.hnb[k][ti]], writes=[pb], out=p[0:4, :], lhsT=wg16[:, k, gi * 4:(gi + 1) * 4],
                          rhs=self.hn[:, k, sl(ti)], start=(k == 0), stop=(k == DC - 1))
                mk.op("act", "activation", reads=[pb, b15b], writes=[dstb], out=dst[:, sl(ti)], in_=p[0:4, :], func=AF.Tanh,
                      bias=b15[:, gi:gi + 1], scale=1.0 / 15.0)
        mk.op("dve", "tensor_scalar", reads=[itb], writes=[itb], out=it[:], in0=it[:], scalar1=15.0, scalar2=None, op0=ALU.mult)
        mk.op("act", "activation", reads=[bnb], writes=[bnb], out=bn[:], in_=bn[:], func=AF.Exp, scale=-15.0)
        mk.op("act", "activation", reads=[bnb, one4b], writes=[bnb], out=bn[:], in_=bn[:], func=AF.Ln, bias=one4[:, 0:1])
        mk.op("dve", "tensor_tensor_scan", reads=[bnb, zrb], writes=[trb], out=tr[:], data0=bn[:], data1=zr_bc, initial=0.0, op0=ALU.add, op1=ALU.add)
        mk.op("dve", "tensor_tensor", reads=[trb, itb], writes=[aab], out=aa[:], in0=tr[:], in1=it[:], op=ALU.add)
        mk.op("dve", "tensor_tensor_scan", reads=[aab], writes=[AAb], out=AA[:], data0=aa[:], data1=aa[:], initial=0.0, op0=ALU.max, op1=ALU.max)
        pstep = AA[:, 0:1].ap[0][0]
        Rbc = bass.AP(AA, AA[:, 63:64].offset, [[pstep, 4], [64, NCK], [0, 64]])
        Rv = bass.AP(AA, AA[:, 63:64].offset, [[pstep, 4], [64, NCK]])
        mk.op("dve", "tensor_tensor", reads=[trb, AAb], writes=[emb], out=em[:], in0=tr[:], in1=AA[:], op=ALU.subtract)
        mk.op("act", "activation", reads=[emb], writes=[emb], out=em[:], in_=em[:], func=AF.Exp)
        mk.op("dve", "tensor_tensor", reads=[AAb], writes=[qfb], out=v3(qf[:], NCK, 64), in0=Rbc, in1=v3(AA[:], NCK, 64), op=ALU.subtract)
        mk.op("act", "activation", reads=[qfb], writes=[qfb], out=qf[:], in_=qf[:], func=AF.Exp)
        mk.op("dve", "tensor_tensor", reads=[AAb, aab], writes=[kfb], out=v3(kf[:], NCK, 64), in0=v3(aa[:], NCK, 64), in1=Rbc, op=ALU.subtract)
        mk.op("act", "activation", reads=[kfb], writes=[kfb], out=kf[:], in_=kf[:], func=AF.Exp)
        mk.op("dve", "tensor_scalar", reads=[kfb], writes=[kfb], out=kf[:], in0=kf[:], scalar1=128.0 ** -0.5, scalar2=None, op0=ALU.mult)
        mk.op("dve", "memset", writes=[R33b], ap=R33[:, 0:1], constant=0.0)
        mk.op("dve", "tensor_copy", reads=[AAb, R33b], writes=[R33b], out=R33[:, 1:NCK + 1], in_=Rv)
        mk.op("dve", "tensor_tensor", reads=[R33b], writes=[dec4b], out=dec4[:], in0=R33[:, 0:NCK], in1=R33[:, 1:NCK + 1], op=ALU.subtract)
        mk.op("act", "activation", reads=[dec4b], writes=[dec4b], out=dec4[:], in_=dec4[:], func=AF.Exp)

        def fbuf(name, dt=F32, n=4):
            return mk.sbuf(name, [128, SEQ], dt), mk.bufs(name, n)
        fb, fbb = fbuf("fb")
        qt, qtb = fbuf("qt", BF16)
        kt, ktb = fbuf("kt", BF16)
        vf, vfb = fbuf("vf", BF16)
        og = mk.sbuf("og", [128, 2, SEQ], BF16)
        ogb = mk.bufs("og", 2, 4)
        ktm = mk.sbuf("ktm", [64, NCK, 128], BF16)
        ktmb = mk.bufs("ktm", 4)
        vtm = mk.sbuf("vtm", [64, NCK, 384], BF16)
        vtmb = mk.bufs("vtm", 4)
        mk.op("dve", "memset", writes=vtmb, ap=vtm[:, :, 256:384], constant=1.0)
        scs = mk.sbuf("scs", [64, 2, 8, 64], BF16)
        scsb = mk.bufs("scs", 2)
        decb_ = mk.sbuf("decb", [128, NCK], F32)
        decbb = mk.buf("decb")
        C = mk.sbuf("C", [128, 2, 384], F32)
        Cb = mk.bufs("C", 2)
        Cd = mk.sbuf("Cd", [128, 2, 384], BF16)
        Cdb = mk.bufs("Cd", 2)
        nm = mk.sbuf("nm", [128, 2, 512], F32)
        nmb = mk.bufs("nm", 2)
        dn = mk.sbuf("dn", [128, 512], F32)
        dnb = mk.buf("dn")

        def bcast(rowt, rowb, hd, ti, dst_ap, dstb):
            p, pb = self.next_ps()
            mk.op("pe", "matmul", reads=[selb, rowb], writes=[pb], out=p[:], lhsT=sel[0:4, hd * 128:(hd + 1) * 128], rhs=rowt[0:4, sl(ti)],
                  start=True, stop=True)
            mk.op("act", "activation", reads=[pb], writes=[dstb], out=dst_ap, in_=p[:], func=AF.Copy)

        for hd in range(NH):
            for ti in T4:
                bcast(qf, qfb, hd, ti, fb[:, sl(ti)], fbb[ti])

            def ev_mul(dst, dstb):
                def evac(j, ti, p, pb):
                    mk.op("dve", "tensor_tensor", reads=[pb, fbb[ti]], writes=[dstb[ti]], out=dst[:, sl(ti)], in0=p[:], in1=fb[:, sl(ti)], op=ALU.mult)
                return evac
            self.linear_fm(win_d, [hd], DC, src, srcb, T4, ev_mul(qt, qtb))
            for ti in T4:
                bcast(kf, kfb, hd, ti, fb[:, sl(ti)], fbb[ti])
            self.linear_fm(win_d, [4 + hd], DC, src, srcb, T4, ev_mul(kt, ktb))
            p, pb = self.next_ps()
            mk.op("pe", "matmul", reads=[selb, dec4b], writes=[pb], out=p[:, 0:NCK], lhsT=sel[0:4, hd * 128:(hd + 1) * 128], rhs=dec4[0:4, :], start=True, stop=True)
            mk.op("act", "activation", reads=[pb], writes=[decbb], out=decb_[:], in_=p[:, 0:NCK], func=AF.Copy)
            for ti in T4:
                p, pb = self.next_ps()
                pv = p.bitcast(BF16)
                for c in range(8):
                    cs = slice(ti * 512 + c * 64, ti * 512 + (c + 1) * 64)
                    mk.op("pe", "transpose", reads=[ktb[ti], identb], writes=[pb], out=pv[0:64, c * 128:(c + 1) * 128], in_=kt[:, cs], identity=ident[:])
                mk.op("act", "activation", reads=[pb], writes=[ktmb[ti]], out=ktm[:, ti * 8:(ti + 1) * 8, :],
                      in_=pv[0:64, :].rearrange("p (c i) -> p c i", i=128), func=AF.Copy)
            for vc in range(2):
                def evac_v(j, ti, p, pb):
                    mk.op("act", "activation", reads=[pb], writes=[vfb[ti]], out=vf[:, sl(ti)], in_=p[:], func=AF.Copy)
                self.linear_fm(win_d, [8 + hd * 2 + vc], DC, src, srcb, T4, evac_v)
                for ti in T4:
                    p, pb = self.next_ps()
                    pv = p.bitcast(BF16)
                    for c in range(8):
                        cs = slice(ti * 512 + c * 64, ti * 512 + (c + 1) * 64)
                        mk.op("pe", "transpose", reads=[vfb[ti], identb], writes=[pb], out=pv[0:64, c * 128:(c + 1) * 128], in_=vf[:, cs], identity=ident[:])
                    mk.op("act", "activation", reads=[pb], writes=[vtmb[ti]], out=vtm[:, ti * 8:(ti + 1) * 8, vc * 128:(vc + 1) * 128],
                          in_=pv[0:64, :].rearrange("p (c i) -> p c i", i=128), func=AF.Copy)

                def evac_o(j, ti, p, pb, vc=vc):
                    mk.op("act", "activation", reads=[pb], writes=[ogb[vc][ti]], out=og[:, vc, sl(ti)], in_=p[:], func=AF.Sigmoid)
                self.linear_fm(win_d, [16 + hd * 2 + vc], DC, src, srcb, T4, evac_o)
            mk.op("dve", "memset", writes=[Cb[1]], ap=C[:, 1, :], constant=0.0)
            for ti in T4:
                p, pb = self.next_ps()
                for c in range(8):
                    cs = slice(ti * 512 + c * 64, ti * 512 + (c + 1) * 64)
                    mk.op("pe", "matmul", reads=[ktb[ti], qtb[ti]], writes=[pb], out=p[0:64, c * 64:(c + 1) * 64], lhsT=kt[:, cs], rhs=qt[:, cs], start=True, stop=True)
                si = self.rr("scs", 2)
                mbc = bass.AP(mask, mask[:, 0:1].offset, [[mask[:, 0:1].ap[0][0], 64], [0, 8], [1, 64]])
                mk.op("dve", "tensor_tensor", reads=[pb, maskb], writes=[scsb[si]], out=scs[:, si, :, :], in0=v3(p[0:64, :], 8, 64), in1=mbc, op=ALU.mult)
                pos = [self.next_ps_long() for _ in range(3)]
                pstq = {}

                def emit_pst(c):
                    cg = ti * 8 + c
                    pst, pstb = self.next_ps()
                    mk.op("pe", "matmul", reads=[ktmb[ti], vtmb[ti]], writes=[pstb], out=pst[:, 0:384], lhsT=ktm[:, cg, :], rhs=vtm[:, cg, :], start=True, stop=True)
                    pstq[c] = (pst, pstb)
                emit_pst(0)
                emit_pst(1)
                for c in range(8):
                    cg = ti * 8 + c
                    par = cg % 2
                    ci = c % 2
                    cs = slice(ti * 512 + c * 64, ti * 512 + (c + 1) * 64)
                    mk.op("act", "activation", reads=[Cb[1 - par], decbb], writes=[Cdb[ci]], out=Cd[:, ci, :], in_=C[:, 1 - par, :], func=AF.Copy,
                          scale=decb_[:, cg:cg + 1])
                    for oc in range(3):
                        po, pob = pos[oc]
                        mk.op("pe", "matmul", reads=[vtmb[ti], scsb[si]], writes=[pob], out=po[:, c * 64:(c + 1) * 64], lhsT=vtm[:, cg, oc * 128:(oc + 1) * 128],
                              rhs=scs[:, si, c, :], start=True, stop=False)
                        mk.op("pe", "matmul", reads=[Cdb[ci], qtb[ti]], writes=[pob], out=po[:, c * 64:(c + 1) * 64], lhsT=Cd[:, ci, oc * 128:(oc + 1) * 128],
                              rhs=qt[:, cs], start=False, stop=True)
                    if c + 2 < 8:
                        emit_pst(c + 2)
                    pst, pstb = pstq.pop(c)
                    mk.op("dve", "scalar_tensor_tensor", reads=[Cb[1 - par], decbb, pstb], writes=[Cb[par]], out=C[:, par, :], in0=C[:, 1 - par, :],
                          scalar=decb_[:, cg:cg + 1], in1=pst[:, 0:384], op0=ALU.mult, op1=ALU.add)
                pem, pemb = self.next_ps()
                mk.op("pe", "matmul", reads=[selb, emb], writes=[pemb], out=pem[:], lhsT=sel[0:4, hd * 128:(hd + 1) * 128], rhs=em[0:4, sl(ti)],
                      start=True, stop=True)
                mk.op("act", "activation", reads=[pos[2][1]], writes=[dnb], out=dn[:], in_=pos[2][0][:], func=AF.Copy)
                mk.op("dve", "scalar_tensor_tensor", reads=[dnb], writes=[dnb], out=dn[:], in0=dn[:], scalar=-1.0, in1=dn[:], op0=ALU.mult, op1=ALU.max)
                mk.op("dve", "tensor_tensor", reads=[dnb, pemb], writes=[dnb], out=dn[:], in0=dn[:], in1=pem[:], op=ALU.max)
                mk.op("dve", "reciprocal", reads=[dnb], writes=[dnb], out=dn[:], in_=dn[:])
                for oc in range(2):
                    mk.op("dve", "tensor_tensor", reads=[pos[oc][1], dnb], writes=[nmb[oc]], out=nm[:, oc, :], in0=pos[oc][0][:], in1=dn[:], op=ALU.mult)
                p2, p2b = self.sumsq_bcast([(nm[:, oc, :], nmb[oc]) for oc in range(2)], 512)
                r, rb = self.rstd_from(p2, p2b, 512, 256)
                for oc in range(2):
                    i = self.rr("ybuf", 2)
                    mk.op("dve", "scalar_tensor_tensor", reads=[nmb[oc], gnb, rb], writes=[nmb[oc]], out=nm[:, oc, :], in0=nm[:, oc, :], scalar=gn[:, oc:oc + 1],
                          in1=r, op0=ALU.mult, op1=ALU.mult)
                    mk.op("dve", "tensor_tensor", reads=[nmb[oc], ogb[oc][ti]], writes=[self.ybufb[i]], out=self.ybuf[:, i, :], in0=nm[:, oc, :], in1=og[:, oc, sl(ti)], op=ALU.mult)
                    self.store_y(hd * 2 + oc, ti, i)
        self.finish()


def mlstm_inputs(inp, idx, hh):
    W = inp["ml_w_in"][idx]
    wt = tile_w(W[:, :6144])
    h0 = hh * 4
    w_in = np.concatenate([wt[h0:h0 + 4], wt[8 + h0:8 + h0 + 4], wt[16 + h0 * 2:16 + h0 * 2 + 8], wt[32 + h0 * 2:32 + h0 * 2 + 8]], axis=0)
    gcols = np.concatenate([W[:, 6144 + h0:6144 + h0 + 4], W[:, 6152 + h0:6152 + h0 + 4]], axis=1)
    w_gate = np.ascontiguousarray(gcols.reshape(DC, 128, 8).transpose(1, 0, 2))
    b_if = np.ascontiguousarray(inp["ml_b_if"][idx][:, h0:h0 + 4].T)
    gn = fmv(inp["ml_g_norm"][idx])
    sel = np.zeros((4, 4 * 128), np.float32)
    for h in range(4):
        sel[h, h * 128:(h + 1) * 128] = 1.0
    return {"w_in": w_in, "w_gate": w_gate, "b_if": b_if, "gn": gn, "sel": sel}


def lru_inputs(inp, idx, hh):
    wt = tile_w(inp["lru_w_in"][idx])
    w_in = np.concatenate([wt[hh * 8:hh * 8 + 8], wt[16 + hh * 8:16 + hh * 8 + 8]], axis=0)
    wa = np.concatenate([tile_w(inp["lru_w_a"][idx, hh * 4 + b]) for b in range(4)], axis=0)
    wx = np.concatenate([tile_w(inp["lru_w_x"][idx, hh * 4 + b]) for b in range(4)], axis=0)
    sl = slice(hh * 1024, (hh + 1) * 1024)
    conv_w = np.ascontiguousarray(inp["lru_conv_w"][idx][:, sl].reshape(4, 8, 128).transpose(2, 0, 1))
    vec = np.stack([fmv(inp[k][idx][sl]) for k in ("lru_conv_b", "lru_b_a", "lru_b_x", "lru_lambda")], axis=1)
    return {"w_in": w_in, "w_ax": np.concatenate([wa, wx], axis=0), "conv_w": conv_w, "vec": np.ascontiguousarray(vec)}


def hgrn_inputs(inp, idx, hh):
    wt = tile_w(inp["hg_w_in"][idx])
    sel = np.concatenate([wt[kind * 16 + hh * 8: kind * 16 + hh * 8 + 8] for kind in range(4)], axis=0)
    lbp = np.ascontiguousarray(inp["hg_lb_param"][:, hh * 1024:(hh + 1) * 1024].reshape(4, 8, 128).transpose(2, 0, 1))
    return {"w_in": sel, "lbp": lbp, "gn": np.ascontiguousarray(inp["hg_g_norm"][idx].reshape(128, 1))}


class Fused(Prog):
    def __init__(self, depth=DEPTH, ncores=8):
        super().__init__()
        nc, mk = self.nc, self.mk
        rg = [[2 * i, 2 * i + 1] for i in range(ncores // 2)]

        def dbuf(name, shape, dt):
            return nc.dram_tensor(name, list(shape), dt).ap(), mk.buf(name)
        xs = [dbuf("xs%d" % l, [D, TA], F32) for l in range(depth)]
        hn_p = [[dbuf("hnp%d_%d" % (l, pc), [D, 512], BF16) for pc in range(2)] for l in range(depth)]
        hn_g = [[dbuf("hng%d_%d" % (l, pc), [2 * D, 512], BF16) for pc in range(2)] for l in range(depth)]
        y_p = [[dbuf("yp%d_%d" % (l, pc), [512, SEQ], BF16) for pc in range(2)] for l in range(depth)]
        y_g = [[dbuf("yg%d_%d" % (l, pc), [1024, SEQ], BF16) for pc in range(2)] for l in range(depth)]
        for l in range(depth + 1):
            io = {"rg": rg}
            if l > 0:
                io["x_src"] = xs[l - 1]
                io["y_g"] = y_g[l - 1]
            if l < depth:
                io["x_dst"] = xs[l]
                io["hn_p"] = hn_p[l]
                io["hn_g"] = hn_g[l]
            ProgA(first=(l == 0), last=(l == depth), shared=self, pfx="A%d_" % l, io=io)
            if l < depth:
                iom = {"rg": rg, "hn_g": hn_g[l], "y_p": y_p[l], "y_g": y_g[l]}
                kind = l % 3
                if kind == 0:
                    ProgHGRN(l, shared=self, pfx="M%d_" % l, io=iom)
                elif kind == 1:
                    ProgLRU(shared=self, pfx="M%d_" % l, io=iom)
                else:
                    ProgMLSTM(shared=self, pfx="M%d_" % l, io=iom)
        mk.finalize()


def fused_inputs(inp, depth=DEPTH, ncores=8, final_g=None):
    cst = consts_host()
    ng_all = inp["norm_g"]
    final_g = inp["final_norm_g"] if final_g is None else final_g
    mixer_wout = [inp["hg_w_out"][0], inp["lru_w_out"][0], inp["ml_w_out"][0], inp["hg_w_out"][1]]
    common = {"ident": cst["ident"], "mask64": cst["mask64"], "memg": fmv(inp["mem_norm_g"])}
    per_half = [dict(), dict()]
    for l in range(depth + 1):
        p = "A%d_" % l
        if l == 0:
            ng = np.stack([fmv(ng_all[0, i]) for i in (0, 0, 0, 1)], axis=1)
        elif l == depth:
            ng = np.stack([fmv(ng_all[l - 1, 2]), fmv(ng_all[l - 1, 3]), fmv(final_g), fmv(final_g)], axis=1)
        else:
            ng = np.stack([fmv(ng_all[l - 1, 2]), fmv(ng_all[l - 1, 3]), fmv(ng_all[l, 0]), fmv(ng_all[l, 1])], axis=1)
        common[p + "ng"] = np.ascontiguousarray(ng)
        if l > 0:
            common[p + "w_mo"] = tile_w(mixer_wout[l - 1])
            common[p + "w_q"] = tile_w(inp["xa_w_q"][l - 1])
            common[p + "w_kv"] = tile_w(inp["xa_w_kv"][l - 1])
            common[p + "w_o"] = tile_w(inp["xa_w_o"][l - 1])
            common[p + "w_gu2"] = tile_w(inp["ffn_w_gu"][l - 1, 1])
            common[p + "w_dn2"] = tile_w(inp["ffn_w_down"][l - 1, 1])
        if l < depth:
            common[p + "w_gu1"] = tile_w(inp["ffn_w_gu"][l, 0])
            common[p + "w_dn1"] = tile_w(inp["ffn_w_down"][l, 0])
            kind, idx = l % 3, l // 3
            for hh in range(2):
                if kind == 0:
                    m = hgrn_inputs(inp, idx, hh)
                elif kind == 1:
                    m = lru_inputs(inp, idx, hh)
                else:
                    m = mlstm_inputs(inp, idx, hh)
                for k, v in m.items():
                    per_half[hh]["M%d_%s" % (l, k)] = v
    in_maps = []
    for c in range(ncores):
        b, r = divmod(c, 2)
        m = dict(common)
        m.update(per_half[r])
        m["A0_xT"] = np.ascontiguousarray(inp["x"][b, r * TA:(r + 1) * TA].T)
        m["memT"] = np.ascontiguousarray(inp["mem"][b].T)
        in_maps.append(m)
    return in_maps


_PROGS = {}


def _prog(key, ctor):
    if key not in _PROGS:
        _PROGS[key] = ctor()
    return _PROGS[key]


def _run(prog, in_maps):
    for m in in_maps:
        for k, (shape, dt) in prog.in_specs.items():
            assert k in m, k
            assert tuple(m[k].shape) == shape, (k, m[k].shape, shape)
    in_maps = [{k: m[k] for k in prog.in_specs} for m in in_maps]
    res = run_bass_kernel_spmd(prog.nc, in_maps, core_ids=list(range(len(in_maps))))
    return res.results


def _run_timed(prog, in_maps):
    in_maps = [{k: m[k] for k in prog.in_specs} for m in in_maps]
    res = run_bass_kernel_spmd(prog.nc, in_maps, core_ids=list(range(len(in_maps))), trace=True)
    print("exec_time_ns", res.exec_time_ns)
    return res.results


def kernel(**inp):
    inp = {k: np.asarray(v) for k, v in inp.items()}
    prog = _prog("fused", Fused)
    res = _run(prog, fused_inputs(inp))
    out = np.empty((BATCH, SEQ, D), np.float32)
    for c in range(8):
        b, tc = divmod(c, 2)
        out[b, tc * TA:(tc + 1) * TA] = res[c]["A%d_outT" % DEPTH].T
    return out


def kernel_unfused(**inp):
    inp = {k: np.asarray(v) for k, v in inp.items()}
    cst = consts_host()
    NC = 8
    x = inp["x"]
    ng_all = inp["norm_g"]
    mixer_wout = [inp["hg_w_out"][0], inp["lru_w_out"][0], inp["ml_w_out"][0], inp["hg_w_out"][1]]

    def ffn_w(layer, f):
        return tile_w(inp["ffn_w_gu"][layer, f]), tile_w(inp["ffn_w_down"][layer, f])

    pa = _prog("A0", lambda: ProgA(first=True, last=False))
    ng = np.ascontiguousarray(np.stack([fmv(ng_all[0, i]) for i in (0, 0, 0, 1)], axis=1))
    wgu, wdn = ffn_w(0, 0)
    in_maps = []
    for c in range(NC):
        b, tc = divmod(c, 2)
        in_maps.append({"xT": np.ascontiguousarray(x[b, tc * TA:(tc + 1) * TA].T), "ng": ng, "w_gu1": wgu, "w_dn1": wdn})
    res = _run(pa, in_maps)
    xT = [r["xT_out"] for r in res]
    hnT = [r["hnT_out"] for r in res]
    del wgu, wdn
    out = None
    for layer in range(DEPTH):
        kind, idx = layer % 3, layer // 3
        in_maps = []
        for c in range(NC):
            b, hh = divmod(c, 2)
            hn_full = np.ascontiguousarray(np.concatenate([hnT[b * 2], hnT[b * 2 + 1]], axis=1))
            if kind == 0:
                m = dict(hgrn_inputs(inp, idx, hh), ident=cst["ident"], mask64=cst["mask64"])
            elif kind == 1:
                m = lru_inputs(inp, idx, hh)
            else:
                m = dict(mlstm_inputs(inp, idx, hh), ident=cst["ident"], mask64=cst["mask64"])
            m["hnT"] = hn_full
            in_maps.append(m)
        if kind == 0:
            pm = _prog("HGRN%d" % layer, lambda: ProgHGRN(layer))
        elif kind == 1:
            pm = _prog("LRU", ProgLRU)
        else:
            pm = _prog("MLSTM", ProgMLSTM)
        res = _run(pm, in_maps)
        yT = [r["yT_out"] for r in res]
        last = layer == DEPTH - 1
        pa = _prog("Alast" if last else "Amid", lambda: ProgA(first=False, last=last))
        if last:
            ng = np.stack([fmv(ng_all[layer, 2]), fmv(ng_all[layer, 3]), fmv(inp["final_norm_g"]), fmv(inp["final_norm_g"])], axis=1)
        else:
            ng = np.stack([fmv(ng_all[layer, 2]), fmv(ng_all[layer, 3]), fmv(ng_all[layer + 1, 0]), fmv(ng_all[layer + 1, 1])], axis=1)
        wgu2, wdn2 = ffn_w(layer, 1)
        common = {"ng": np.ascontiguousarray(ng), "memg": fmv(inp["mem_norm_g"]), "ident": cst["ident"],
                  "w_mo": tile_w(mixer_wout[layer]), "w_q": tile_w(inp["xa_w_q"][layer]), "w_kv": tile_w(inp["xa_w_kv"][layer]),
                  "w_o": tile_w(inp["xa_w_o"][layer]), "w_gu2": wgu2, "w_dn2": wdn2}
        if not last:
            wgu1, wdn1 = ffn_w(layer + 1, 0)
            common.update({"w_gu1": wgu1, "w_dn1": wdn1})
        in_maps = []
        for c in range(NC):
            b, tc = divmod(c, 2)
            y_c = np.ascontiguousarray(np.concatenate([yT[b * 2][:, tc * TA:(tc + 1) * TA], yT[b * 2 + 1][:, tc * TA:(tc + 1) * TA]], axis=0))
            in_maps.append(dict(common, xT=xT[c], yT=y_c, memT=np.ascontiguousarray(inp["mem"][b].T)))
        res = _run(pa, in_maps)
        if last:
            out = np.empty((BATCH, SEQ, D), np.float32)
            for c in range(NC):
                b, tc = divmod(c, 2)
                out[b, tc * TA:(tc + 1) * TA] = res[c]["outT"].T
        else:
            xT = [r["xT_out"] for r in res]
            hnT = [r["hnT_out"] for r in res]
    return out
```

```python
import numpy as np
import ml_dtypes
from contextlib import ExitStack
import concourse.bass as bass
import concourse.mybir as mybir
from concourse.bass_utils import run_bass_kernel_spmd

F32 = mybir.dt.float32
BF16 = mybir.dt.bfloat16
AF = mybir.ActivationFunctionType
ALU = mybir.AluOpType
AX = mybir.AxisListType
NPBF = ml_dtypes.bfloat16

SAME_ENGINE_SYNC = True

D = 2048
DC = 16
SEQ = 2048
BATCH = 4
NMEM = 256
DFF = 5504
FC = 43
EPS = 1e-6
DEPTH = 4
TA = 1024
GROUP = 8


class Buf:
    __slots__ = ("name", "last_w", "readers", "dma_readers", "dsem")

    def __init__(self, name):
        self.name = name
        self.last_w = None
        self.readers = {}
        self.dma_readers = []
        self.dsem = None


class Op:
    __slots__ = ("eng", "fn", "deps", "is_dma", "sem", "val", "milestone", "final_group", "inc")

    def __init__(self, eng, fn):
        self.eng = eng
        self.fn = fn
        self.deps = []
        self.is_dma = False
        self.sem = None
        self.val = 0
        self.milestone = False
        self.final_group = False
        self.inc = 16


class MK:
    ENGS = ("pe", "act", "dve", "pool", "sp")

    def __init__(self, nc):
        self.nc = nc
        self.es = ExitStack()
        self.ops = {e: [] for e in self.ENGS}
        self.esem = {}
        self.dsem_count = {}
        self.nsem = 0
        self.closers = []
        self.out_events = []
        self.nbuf = 0
        self.pes = None
        self.phase_sems = []
        self.free_sems = []
        self.live_async = {}
        self.nphase = 0
        self.rk = {}
        for e in self.ENGS:
            self.esem[e] = self.new_sem("eng_" + e, pooled=False)

    def new_sem(self, name, pooled=True):
        if pooled and self.free_sems:
            s = self.free_sems.pop()
        else:
            self.nsem += 1
            s = self.es.enter_context(self.nc.semaphore("%s_%d" % (name, self.nsem)))
            self.dsem_count[id(s)] = 0
        if pooled:
            self.phase_sems.append(s)
        return s

    def begin_phase(self):
        self.pes = ExitStack()
        self.nphase += 1

    def end_phase(self):
        frontier = []
        for e in self.ENGS:
            for op in reversed(self.ops[e]):
                if not op.is_dma and op.fn is not None:
                    frontier.append(op)
                    break
        frontier.extend(self.live_async.values())
        for e in self.ENGS:
            op = Op(e, None)
            op.deps = list(frontier)
            self.ops[e].append(op)
        self.live_async = {}
        self.free_sems.extend(self.phase_sems)
        self.phase_sems = []
        self.pes.close()
        self.pes = None

    def sbuf(self, name, shape, dtype):
        st = self.pes if self.pes is not None else self.es
        nm = name if self.pes is None else "%s_p%d" % (name, self.nphase)
        return st.enter_context(self.nc.sbuf_tensor(nm, list(shape), dtype))

    def psum(self, name, shape, dtype):
        return self.es.enter_context(self.nc.psum_tensor(name, list(shape), dtype))

    def buf(self, name=None):
        self.nbuf += 1
        return Buf(name or ("b%d" % self.nbuf))

    def bufs(self, name, *dims):
        if len(dims) == 1:
            return [self.buf("%s_%d" % (name, i)) for i in range(dims[0])]
        return [self.bufs("%s_%d" % (name, i), *dims[1:]) for i in range(dims[0])]

    def _record(self, op, reads, writes):
        deps = []
        for b in reads:
            if b.last_w is not None:
                deps.append(b.last_w)
        for b in writes:
            lw = b.last_w
            if lw is not None:
                if not (op.is_dma and lw.is_dma and lw.sem is op.sem and not b.readers and not b.dma_readers):
                    deps.append(lw)
            deps.extend(b.readers.values())
            deps.extend(b.dma_readers)
        op.deps = deps
        for b in reads:
            if op.is_dma:
                b.dma_readers.append(op)
            else:
                b.readers[op.eng] = op
        for b in writes:
            b.last_w = op
            b.readers = {}
            b.dma_readers = []
        self.ops[op.eng].append(op)
        return op

    def op(self, eng, meth, reads=(), writes=(), **kw):
        fn = (lambda e: getattr(e, meth)(**kw))
        return self._record(Op(eng, fn), list(reads), list(writes))

    def _async(self, op, reads, writes, sem, inc):
        if sem is None:
            b = writes[0] if writes else reads[0]
            if b.dsem is None:
                b.dsem = self.new_sem("d")
            sem = b.dsem
        op.is_dma = True
        op.sem = sem
        op.inc = inc
        self.dsem_count[id(sem)] += inc
        op.val = self.dsem_count[id(sem)]
        self.live_async[id(sem)] = op
        return self._record(op, reads, writes)

    def dma(self, eng, out_ap, in_ap, reads=(), writes=(), sem=None, final_group=False, in_fn=None):
        if in_fn is not None:
            op = Op(eng, lambda e: e.dma_start(out=out_ap, in_=in_fn(e)))
        else:
            op = Op(eng, lambda e: e.dma_start(out=out_ap, in_=in_ap))
        op.final_group = final_group
        return self._async(op, list(reads), list(writes), sem, 16)

    def cc_allgather(self, in_ap, out_ap, reads, writes, rg):
        op = Op("pool", lambda e: e.collective_compute("AllGather", ALU.bypass, replica_groups=rg, ins=[in_ap], outs=[out_ap]))
        sem = self.new_sem("cc", pooled=False)
        r = self._async(op, list(reads), list(writes), sem, 1)
        del self.live_async[id(sem)]
        return r

    def rank2(self, e, ename):
        if ename not in self.rk:
            self.rk[ename] = e.snap(e.partition_id() % 2)
        return self.rk[ename]

    def out_dma(self, eng, out_ap, in_ap, reads, sem):
        op = self.dma(eng, out_ap, in_ap, reads=reads, writes=(), sem=sem)
        self.out_events.append(op)
        return op

    def _skip(self, op, d):
        return (not d.is_dma) and d.eng == op.eng and (not op.is_dma) and (op.eng == "pe" or not SAME_ENGINE_SYNC)

    def finalize(self):
        for e in self.ENGS:
            for op in self.ops[e]:
                for d in op.deps:
                    if d.is_dma or self._skip(op, d):
                        continue
                    d.milestone = True
        for e in self.ENGS:
            n = 0
            for op in self.ops[e]:
                if op.is_dma or op.fn is None:
                    continue
                if op.milestone:
                    n += 1
                op.sem = self.esem[e]
                op.val = n
        nwaits = [0]
        with self.nc.Block() as block:
            def run(ename, eng):
                seen = {}
                for op in self.ops[ename]:
                    need = {}
                    for d in op.deps:
                        if self._skip(op, d):
                            continue
                        v = d.val
                        if d.is_dma and d.final_group:
                            v = self.dsem_count[id(d.sem)]
                        k = id(d.sem)
                        if seen.get(k, 0) >= v:
                            continue
                        if k not in need or need[k][1] < v:
                            need[k] = (d.sem, v)
                    for k, (s, v) in need.items():
                        eng.wait_ge(s, v)
                        seen[k] = v
                        nwaits[0] += 1
                    if op.fn is None:
                        continue
                    self.cur_ename = ename
                    ins = op.fn(eng)
                    if op.is_dma:
                        ins.then_inc(op.sem, op.inc)
                    elif op.milestone:
                        ins.then_inc(op.sem, 1)
                if ename == "sp":
                    fin = {}
                    for op in self.out_events:
                        fin[id(op.sem)] = (op.sem, self.dsem_count[id(op.sem)])
                    for s, v in fin.values():
                        eng.wait_ge(s, v)

            @block.sync
            def _(e):
                run("sp", e)

            @block.gpsimd
            def _(e):
                run("pool", e)

            @block.tensor
            def _(e):
                run("pe", e)

            @block.scalar
            def _(e):
                run("act", e)

            @block.vector
            def _(e):
                run("dve", e)
        self.stats = {e: len(self.ops[e]) for e in self.ENGS}
        self.stats["waits"] = nwaits[0]
        self.stats["sems"] = self.nsem
        for c in reversed(self.closers):
            c.close()
        self.es.close()


def tile_w(W):
    K, N = W.shape
    return np.ascontiguousarray(
        W.reshape(K // 128, 128, N // 128, 128).transpose(2, 1, 0, 3).reshape(N // 128, 128, K))


def fmv(v):
    n = v.shape[-1] // 128
    return np.ascontiguousarray(v.reshape(n, 128).T)


def consts_host():
    c = {}
    c["ident"] = np.eye(128, dtype=np.float32).astype(NPBF)
    s = np.arange(64)
    c["mask64"] = (s[:, None] <= s[None, :]).astype(np.float32)
    return c


class Prog:
    NSLOT = 6
    SLOT = 2048
    GLOBAL_INPUTS = ("ident", "mask64", "memg", "memT")

    def __init__(self, shared=None, pfx="", io=None):
        self.pfx = pfx
        self.io = io or {}
        if shared is None:
            self.root = self
            self.nc = bass.Bass("TRN2", target_bir_lowering=False)
            self.mk = MK(self.nc)
            mk = self.mk
            self.in_specs = {}
            self.gl_inputs = {}
            self.ps = [mk.psum("ps%d" % i, [128, 512], F32) for i in range(8)]
            self.out_sem = mk.new_sem("out", pooled=False)
            self.ones = mk.sbuf("ones", [128, 128], F32)
            self.eps_t = mk.sbuf("eps", [128, 1], F32)
            self.sq = mk.sbuf("sq", [128, 2, 512], F32)
            self.rstd = mk.sbuf("rstd", [128, 2, 512], F32)
        else:
            r = shared.root
            self.root = r
            for k in ("nc", "mk", "in_specs", "gl_inputs", "ps", "out_sem", "ones", "eps_t", "sq", "rstd"):
                setattr(self, k, getattr(r, k))

    def begin(self):
        mk = self.mk
        mk.begin_phase()
        self.wring = mk.sbuf("wring", [128, self.NSLOT, self.SLOT], BF16)
        self.wb = mk.bufs("w", self.NSLOT)
        self.psb = mk.bufs("ps", 8)
        self.onesb = mk.buf("ones")
        self.epsb = mk.buf("eps")
        self.sqb = mk.bufs("sq", 2)
        self.rstdb = mk.bufs("rstd", 2)
        self._slot = 0
        self._ps = 0
        self._rr = {}
        mk.op("dve", "memset", writes=[self.onesb], ap=self.ones[:], constant=1.0)
        mk.op("dve", "memset", writes=[self.epsb], ap=self.eps_t[:], constant=EPS)

    def din(self, name, shape, dtype=F32):
        if name in self.GLOBAL_INPUTS:
            if name not in self.gl_inputs:
                self.in_specs[name] = (tuple(shape), dtype)
                self.gl_inputs[name] = self.nc.dram_tensor(name, list(shape), dtype, kind="ExternalInput").ap()
            return self.gl_inputs[name]
        name = self.pfx + name
        self.in_specs[name] = (tuple(shape), dtype)
        return self.nc.dram_tensor(name, list(shape), dtype, kind="ExternalInput").ap()

    def dout(self, name, shape, dtype=F32):
        return self.nc.dram_tensor(self.pfx + name, list(shape), dtype, kind="ExternalOutput").ap()

    def rr(self, key, n):
        i = self._rr.get(key, 0)
        self._rr[key] = (i + 1) % n
        return i

    def next_ps(self):
        i = self._ps
        self._ps = (i + 1) % 5
        return self.ps[i], self.psb[i]

    def next_ps_long(self):
        i = 5 + self.rr("pslong", 3)
        return self.ps[i], self.psb[i]

    def wload(self, src_ap, n):
        i = self._slot
        self._slot = (i + 1) % self.NSLOT
        self.mk.dma("pool", self.wring[:, i, 0:n], src_ap, writes=[self.wb[i]])
        return self.wring[:, i, :], self.wb[i]

    def const(self, name, shape, dtype=F32, dram=None):
        t = self.mk.sbuf("c_" + name, shape, dtype)
        b = self.mk.buf("c_" + name)
        d = dram if dram is not None else self.din(name, shape, dtype)
        self.mk.dma("sp", t[:], d, writes=[b])
        return t, b

    def linear_fm(self, w_d, jlist, nk, src, srcb, ttiles, evac, k0=0):
        mk = self.mk
        for j in jlist:
            w, wbuf = self.wload(w_d[j, :, k0 * 128:(k0 + nk) * 128], nk * 128)
            for ti in ttiles:
                p, pb = self.next_ps()
                rhs0 = src(0, ti)
                ntok = rhs0.shape[-1]
                for k in range(nk):
                    mk.op("pe", "matmul", reads=[wbuf, srcb(k, ti)], writes=[pb],
                          out=p[:, 0:ntok], lhsT=w[:, k * 128:(k + 1) * 128], rhs=src(k, ti),
                          start=(k == 0), stop=(k == nk - 1))
                evac(j, ti, p, pb)

    def sumsq_bcast(self, srcs, ntok):
        mk = self.mk
        p, pb = self.next_ps()
        n = len(srcs)
        for c, (ap, b) in enumerate(srcs):
            i = self.rr("sq", 2)
            mk.op("act", "activation", reads=[b], writes=[self.sqb[i]],
                  out=self.sq[:, i, 0:ntok], in_=ap, func=AF.Square)
            mk.op("pe", "matmul", reads=[self.onesb, self.sqb[i]], writes=[pb],
                  out=p[:, 0:ntok], lhsT=self.ones[:], rhs=self.sq[:, i, 0:ntok], start=(c == 0), stop=(c == n - 1))
        return p, pb

    def rstd_from(self, p, pb, ntok, n_feat):
        mk = self.mk
        i = self.rr("rstd", 2)
        if getattr(self, "rstd_lnexp", False):
            mk.op("act", "activation", reads=[pb, self.epsb], writes=[self.rstdb[i]],
                  out=self.rstd[:, i, 0:ntok], in_=p[:, 0:ntok], func=AF.Ln, bias=self.eps_t[:, 0:1], scale=1.0 / n_feat)
            mk.op("act", "activation", reads=[self.rstdb[i]], writes=[self.rstdb[i]],
                  out=self.rstd[:, i, 0:ntok], in_=self.rstd[:, i, 0:ntok], func=AF.Exp, scale=-0.5)
            return self.rstd[:, i, 0:ntok], self.rstdb[i]
        mk.op("act", "activation", reads=[pb, self.epsb], writes=[self.rstdb[i]],
              out=self.rstd[:, i, 0:ntok], in_=p[:, 0:ntok], func=AF.Sqrt, bias=self.eps_t[:, 0:1], scale=1.0 / n_feat)
        mk.op("dve", "reciprocal", reads=[self.rstdb[i]], writes=[self.rstdb[i]],
              out=self.rstd[:, i, 0:ntok], in_=self.rstd[:, i, 0:ntok])
        return self.rstd[:, i, 0:ntok], self.rstdb[i]

    def make_eps(self):
        pass

    def rmsnorm(self, src, srcb, dst, dstb, g_ap, nchunk, ntok):
        mk = self.mk
        p, pb = self.sumsq_bcast([(src(c), srcb(c)) for c in range(nchunk)], ntok)
        r, rb = self.rstd_from(p, pb, ntok, nchunk * 128)
        for c in range(nchunk):
            mk.op("dve", "scalar_tensor_tensor", reads=[srcb(c), self.gb, rb], writes=[dstb(c)],
                  out=dst(c), in0=src(c), scalar=g_ap(c), in1=r, op0=ALU.mult, op1=ALU.mult)

    def finish(self):
        self.mk.end_phase()
        if self.root is self:
            self.mk.finalize()
        return self.nc


class ProgA(Prog):
    def __init__(self, first, last, shared=None, pfx="", io=None):
        super().__init__(shared, pfx, io)
        self.begin()
        self.first, self.last = first, last
        mk = self.mk
        io = self.io
        T = TA
        NTH = T // 512
        if "x_src" in io:
            xT_d, xsrcb = io["x_src"]
            xT_d = xT_d.rearrange("(c p) t -> p c t", p=128)
            xsrc_reads = [xsrcb]
        else:
            xT_d = self.din("xT", [D, T]).rearrange("(c p) t -> p c t", p=128)
            xsrc_reads = []
        ng_t, self.gb = self.const("ng", [128, 4, DC])
        x = mk.sbuf("x", [128, DC, T], F32)
        xb = mk.bufs("x", DC, NTH)
        xn = mk.sbuf("xn", [128, DC, T], BF16)
        xnb = mk.bufs("xn", DC, NTH)
        bigB = mk.sbuf("bigB", [128, DC, 512], BF16)
        bigBb = mk.bufs("bigB", DC)
        bigC = mk.sbuf("bigC", [128, DC, 512], BF16)
        bigCb = mk.bufs("bigC", DC)
        sg = mk.sbuf("sg", [128, 2, 512], F32)
        sgb = mk.bufs("sg", 2)
        self.x, self.xb, self.xn, self.xnb = x, xb, xn, xnb

        def load_x():
            for c in range(DC):
                for th in range(NTH):
                    mk.dma("sp", x[:, c, th * 512:(th + 1) * 512], xT_d[:, c, th * 512:(th + 1) * 512], reads=xsrc_reads, writes=[xb[c][th]])
        if first:
            load_x()

        def norm_x(gi):
            for th in range(NTH):
                tsl = slice(th * 512, (th + 1) * 512)
                self.rmsnorm(lambda c: x[:, c, tsl], lambda c: xb[c][th], lambda c: xn[:, c, tsl], lambda c: xnb[c][th],
                             lambda c: ng_t[:, gi, c:c + 1], DC, 512)

        def ffn(wgu_d, wdn_d):
            groups = [(g0, min(g0 + GROUP, FC)) for g0 in range(0, FC, GROUP)]
            for (g0, g1) in groups:
                for c in range(g0, g1):
                    cl = c - g0
                    wg, wgb = self.wload(wgu_d[c, :, :], DC * 128)
                    wu, wub = self.wload(wgu_d[FC + c, :, :], DC * 128)
                    for th in range(NTH):
                        tsl = slice(th * 512, (th + 1) * 512)
                        pg, pgb = self.next_ps()
                        for k in range(DC):
                            mk.op("pe", "matmul", reads=[wgb, xnb[k][th]], writes=[pgb], out=pg[:], lhsT=wg[:, k * 128:(k + 1) * 128],
                                  rhs=xn[:, k, tsl], start=(k == 0), stop=(k == DC - 1))
                        pu, pub = self.next_ps()
                        for k in range(DC):
                            mk.op("pe", "matmul", reads=[wub, xnb[k][th]], writes=[pub], out=pu[:], lhsT=wu[:, k * 128:(k + 1) * 128],
                                  rhs=xn[:, k, tsl], start=(k == 0), stop=(k == DC - 1))
                        i = self.rr("sg", 2)
                        mk.op("act", "activation", reads=[pgb], writes=[sgb[i]], out=sg[:, i, :], in_=pg[:], func=AF.Silu)
                        mk.op("dve", "tensor_tensor", reads=[sgb[i], pub], writes=[bigCb[cl * 2 + th]],
                              out=bigC[:, cl * 2 + th, :], in0=sg[:, i, :], in1=pu[:], op=ALU.mult)
                nk = g1 - g0
                for j in range(DC):
                    wd, wdb = self.wload(wdn_d[j, :, g0 * 128:g1 * 128], nk * 128)
                    for th in range(NTH):
                        tsl = slice(th * 512, (th + 1) * 512)
                        po, pob = self.next_ps()
                        for kk in range(nk):
                            mk.op("pe", "matmul", reads=[wdb, bigCb[kk * 2 + th]], writes=[pob], out=po[:],
                                  lhsT=wd[:, kk * 128:(kk + 1) * 128], rhs=bigC[:, kk * 2 + th, :], start=(kk == 0), stop=(kk == nk - 1))
                        mk.op("dve", "scalar_tensor_tensor", reads=[pob, xb[j][th]], writes=[xb[j][th]],
                              out=x[:, j, tsl], in0=po[:], scalar=0.5, in1=x[:, j, tsl], op0=ALU.mult, op1=ALU.add)

        def add_into_x(th):
            tsl = slice(th * 512, (th + 1) * 512)

            def evac(j, ti, p, pb):
                mk.op("dve", "tensor_tensor", reads=[pb, xb[j][th]], writes=[xb[j][th]],
                      out=x[:, j, tsl], in0=p[:], in1=x[:, j, tsl], op=ALU.add)
            return evac

        if not first:
            if "y_g" not in io:
                yT_d = self.din("yT", [D, T], BF16).rearrange("(c p) t -> p c t", p=128)
            memT_d = self.din("memT", [D, NMEM]).rearrange("(c p) t -> p c t", p=128)
            mg_t, mgb = self.const("memg", [128, DC])
            ident, identb = self.const("ident", [128, 128], BF16)
            wmo_d = self.din("w_mo", [DC, 128, D])
            wq_d = self.din("w_q", [DC, 128, D])
            wkv_d = self.din("w_kv", [2 * DC, 128, D])
            wo_d = self.din("w_o", [DC, 128, D])
            wgu2_d = self.din("w_gu2", [2 * FC, 128, D])
            wdn2_d = self.din("w_dn2", [DC, 128, DFF])
            kT = mk.sbuf("kT", [128, DC, NMEM], BF16)
            kTb = mk.bufs("kT", DC)
            vtm = mk.sbuf("vtm", [128, 2, D], BF16)
            vtmb = mk.bufs("vtm", 2, 4)
            mst = mk.sbuf("mst", [128, 2, NMEM], F32)
            mstb = mk.bufs("mst", 2)
            pT = mk.sbuf("pT", [128, 2, 2, 512], BF16)
            pTb = mk.bufs("pT", 2)
            sm = mk.sbuf("sm", [128, 4, NMEM], F32)
            smb = mk.bufs("sm", 4)
            pbf = mk.sbuf("pbf", [128, 4, NMEM], BF16)
            pbfb = mk.bufs("pbf", 4)
            st1 = mk.sbuf("st1", [128, 16], F32)
            st1b = mk.bufs("st1", 4)
            p, pb = self.next_ps()
            for c in range(DC):
                i = self.rr("mst", 2)
                mk.dma("sp", mst[:, i, :], memT_d[:, c, :], writes=[mstb[i]])
                k = self.rr("sq", 2)
                mk.op("act", "activation", reads=[mstb[i]], writes=[self.sqb[k]], out=self.sq[:, k, 0:NMEM], in_=mst[:, i, :], func=AF.Square)
                mk.op("pe", "matmul", reads=[self.onesb, self.sqb[k]], writes=[pb], out=p[:, 0:NMEM], lhsT=self.ones[:],
                      rhs=self.sq[:, k, 0:NMEM], start=(c == 0), stop=(c == DC - 1))
            r, rb = self.rstd_from(p, pb, NMEM, D)
            for c in range(DC):
                i = self.rr("mst", 2)
                mk.dma("sp", mst[:, i, :], memT_d[:, c, :], writes=[mstb[i]])
                mk.op("dve", "scalar_tensor_tensor", reads=[mstb[i], mgb, rb], writes=[bigBb[c]], out=bigB[:, c, 0:NMEM],
                      in0=mst[:, i, :], scalar=mg_t[:, c:c + 1], in1=r, op0=ALU.mult, op1=ALU.mult)

            load_x()

            def k_item(j):
                w, wbuf = self.wload(wkv_d[j, :, :], D)
                p, pb = self.next_ps()
                for k in range(DC):
                    mk.op("pe", "matmul", reads=[wbuf, bigBb[k]], writes=[pb], out=p[:, 0:NMEM], lhsT=w[:, k * 128:(k + 1) * 128],
                          rhs=bigB[:, k, 0:NMEM], start=(k == 0), stop=(k == DC - 1))
                mk.op("act", "activation", reads=[pb], writes=[kTb[j]], out=kT[:, j, :], in_=p[:, 0:NMEM], func=AF.Copy)

            def v_item(jg):
                ws = [self.wload(wkv_d[DC + jg * 4 + jj, :, :], D) for jj in range(4)]
                for mc in range(2):
                    p, pb = self.next_ps()
                    for jj in range(4):
                        w, wbuf = ws[jj]
                        for k in range(DC):
                            mk.op("pe", "matmul", reads=[wbuf, bigBb[k]], writes=[pb], out=p[:, jj * 128:(jj + 1) * 128],
                                  lhsT=bigB[:, k, mc * 128:(mc + 1) * 128], rhs=w[:, k * 128:(k + 1) * 128], start=(k == 0), stop=(k == DC - 1))
                    mk.op("act", "activation", reads=[pb], writes=[vtmb[mc][jg]], out=vtm[:, mc, jg * 512:(jg + 1) * 512], in_=p[:], func=AF.Copy)

            scale = 512.0 ** -0.5
            THS = list(range(NTH))

            def tsl_(th):
                return slice(th * 512, (th + 1) * 512)
            for th in THS:
                tsl = tsl_(th)
                for c in range(DC):
                    if "y_g" in io:
                        r_, rem = divmod(c, 8)
                        pc, i_ = divmod(rem, 4)
                        yg_ap, ygb = io["y_g"][pc]
                        row0 = r_ * 512 + i_ * 128

                        def in_fn(e, yg_ap=yg_ap, row0=row0, th=th):
                            rk = mk.rank2(e, "sp")
                            return yg_ap[row0:row0 + 128, bass.ds(rk * TA + th * 512, 512)]
                        mk.dma("sp", xn[:, c, tsl], None, reads=[ygb], writes=[xnb[c][th]], in_fn=in_fn)
                    else:
                        mk.dma("sp", xn[:, c, tsl], yT_d[:, c, tsl], writes=[xnb[c][th]])

            def evac_add(j, th, p, pb):
                mk.op("dve", "tensor_tensor", reads=[pb, xb[j][th]], writes=[xb[j][th]],
                      out=x[:, j, tsl_(th)], in0=p[:], in1=x[:, j, tsl_(th)], op=ALU.add)
            kv_items = [lambda jg=jg: v_item(jg) for jg in range(2)] + [lambda j=j: k_item(j) for j in range(DC)] + \
                       [lambda jg=jg: v_item(jg) for jg in range(2, 4)]
            for it in kv_items[:6]:
                it()
            rest = kv_items[6:]
            for j in range(DC):
                self.linear_fm(wmo_d, [j], DC, lambda k, th: xn[:, k, tsl_(th)], lambda k, th: xnb[k][th], THS, evac_add)
                if rest:
                    rest.pop(0)()
            for it in rest:
                it()
            for th in THS:
                tsl = tsl_(th)
                self.rmsnorm(lambda c: x[:, c, tsl], lambda c: xb[c][th], lambda c: xn[:, c, tsl], lambda c: xnb[c][th],
                             lambda c: ng_t[:, 0, c:c + 1], DC, 512)
            qbuf = [(bigB, bigBb), (bigC, bigCb)]

            def evac_q(j, th, p, pb):
                qd, qdb = qbuf[th]
                mk.op("act", "activation", reads=[pb], writes=[qdb[j]], out=qd[:, j, :], in_=p[:], func=AF.Copy)
            self.linear_fm(wq_d, range(DC), DC, lambda k, th: xn[:, k, tsl_(th)], lambda k, th: xnb[k][th], THS, evac_q)
            for th in THS:
                tsl = tsl_(th)
                qd, qdb = qbuf[th]
                for hd in range(4):
                    pi = self.rr("pT", 2)
                    def chain(tt, hd=hd, qd=qd, qdb=qdb, pi=pi):
                        ps_, psb_ = self.next_ps()
                        for kc in range(4):
                            ch = hd * 4 + kc
                            mk.op("pe", "matmul", reads=[qdb[ch], kTb[ch]], writes=[psb_], out=ps_[:, 0:NMEM],
                                  lhsT=qd[:, ch, tt * 128:(tt + 1) * 128], rhs=kT[:, ch, :], start=(kc == 0), stop=(kc == 3))
                        si = tt
                        mi = tt
                        yield
                        mk.op("dve", "reduce_max", reads=[psb_], writes=[st1b[si]], out=st1[:, si * 4:si * 4 + 1], in_=ps_[:, 0:NMEM], axis=AX.X)
                        yield
                        mk.op("dve", "tensor_scalar", reads=[st1b[si]], writes=[st1b[si]], out=st1[:, si * 4 + 1:si * 4 + 2],
                              in0=st1[:, si * 4:si * 4 + 1], scalar1=-scale, scalar2=None, op0=ALU.mult)
                        yield
                        mk.op("act", "activation", reads=[psb_, st1b[si]], writes=[smb[mi], st1b[si]], out=sm[:, mi, :], in_=ps_[:, 0:NMEM],
                              func=AF.Exp, bias=st1[:, si * 4 + 1:si * 4 + 2], scale=scale, accum_out=st1[:, si * 4 + 2:si * 4 + 3])
                        yield
                        mk.op("dve", "reciprocal", reads=[st1b[si]], writes=[st1b[si]], out=st1[:, si * 4 + 3:si * 4 + 4],
                              in_=st1[:, si * 4 + 2:si * 4 + 3])
                        yield
                        mk.op("dve", "tensor_scalar", reads=[smb[mi], st1b[si]], writes=[pbfb[mi]], out=pbf[:, mi, :], in0=sm[:, mi, :],
                              scalar1=st1[:, si * 4 + 3:si * 4 + 4], scalar2=None, op0=ALU.mult)
                        yield
                        pt_, ptb_ = self.next_ps()
                        ptv = pt_.bitcast(BF16)
                        for mc in range(2):
                            mk.op("pe", "transpose", reads=[pbfb[mi], identb], writes=[ptb_], out=ptv[:, mc * 128:(mc + 1) * 128],
                                  in_=pbf[:, mi, mc * 128:(mc + 1) * 128], identity=ident[:])
                        yield
                        for mc in range(2):
                            mk.op("act", "activation", reads=[ptb_], writes=[pTb[pi]], out=pT[:, pi, mc, tt * 128:(tt + 1) * 128],
                                  in_=ptv[:, mc * 128:(mc + 1) * 128], func=AF.Copy)
                    gens = [chain(tt) for tt in range(4)]
                    while gens:
                        for g in list(gens):
                            try:
                                next(g)
                            except StopIteration:
                                gens.remove(g)
                    for dc in range(4):
                        ch = hd * 4 + dc
                        po, pob = self.next_ps()
                        for mc in range(2):
                            mk.op("pe", "matmul", reads=[vtmb[mc][hd], pTb[pi]], writes=[pob], out=po[:],
                                  lhsT=vtm[:, mc, ch * 128:(ch + 1) * 128], rhs=pT[:, pi, mc, :], start=(mc == 0), stop=(mc == 1))
                        mk.op("act", "activation", reads=[pob], writes=[xnb[ch][th]], out=xn[:, ch, tsl], in_=po[:], func=AF.Copy)
            self.linear_fm(wo_d, range(DC), DC, lambda k, th: xn[:, k, tsl_(th)], lambda k, th: xnb[k][th], THS, evac_add)
            norm_x(1)
            ffn(wgu2_d, wdn2_d)

        if not last:
            wgu1_d = self.din("w_gu1", [2 * FC, 128, D])
            wdn1_d = self.din("w_dn1", [DC, 128, DFF])
            norm_x(2)
            ffn(wgu1_d, wdn1_d)
            norm_x(3)
            if "x_dst" in io:
                for pc in range(2):
                    (hp, hpb), (hg, hgb) = io["hn_p"][pc], io["hn_g"][pc]
                    hpv = hp.rearrange("(c p) t -> p c t", p=128)
                    for c in range(DC):
                        mk.dma("sp", hpv[:, c, :], xn[:, c, pc * 512:(pc + 1) * 512], reads=[xnb[c][pc]], writes=[hpb])
                    mk.cc_allgather(hp, hg, reads=[hpb], writes=[hgb], rg=io["rg"])
                xd, xdb = io["x_dst"]
                xd = xd.rearrange("(c p) t -> p c t", p=128)
                for c in range(DC):
                    mk.dma("sp", xd[:, c, :], x[:, c, :], reads=xb[c], writes=[xdb])
            else:
                xo_d = self.dout("xT_out", [D, T]).rearrange("(c p) t -> p c t", p=128)
                hn_d = self.dout("hnT_out", [D, T], BF16).rearrange("(c p) t -> p c t", p=128)
                for c in range(DC):
                    mk.out_dma("sp", xo_d[:, c, :], x[:, c, :], reads=xb[c], sem=self.out_sem)
                    mk.out_dma("sp", hn_d[:, c, :], xn[:, c, :], reads=xnb[c], sem=self.out_sem)
        else:
            fo_d = self.dout("outT", [D, T]).rearrange("(c p) t -> p c t", p=128)
            for th in range(NTH):
                tsl = slice(th * 512, (th + 1) * 512)
                p, pb = self.sumsq_bcast([(x[:, c, tsl], xb[c][th]) for c in range(DC)], 512)
                r, rb = self.rstd_from(p, pb, 512, D)
                for c in range(DC):
                    mk.op("dve", "scalar_tensor_tensor", reads=[xb[c][th], self.gb, rb], writes=[xb[c][th]],
                          out=x[:, c, tsl], in0=x[:, c, tsl], scalar=ng_t[:, 2, c:c + 1], in1=r, op0=ALU.mult, op1=ALU.mult)
            for c in range(DC):
                mk.out_dma("sp", fo_d[:, c, :], x[:, c, :], reads=xb[c], sem=self.out_sem)
        self.finish()


def bc_chunks(t, col, nch, clen, pstep=None):
    base = t[:, 0:1]
    ps = base.ap[0][0]
    return bass.AP(t, base.offset + col, [[ps, 128], [clen, nch], [0, clen]])


def v3(ap2, nch, clen):
    return ap2.rearrange("p (c i) -> p c i", i=clen)


class ProgM(Prog):
    NSLOT = 4

    def __init__(self, shared=None, pfx="", io=None):
        super().__init__(shared, pfx, io)
        self.begin()
        mk = self.mk
        io = self.io
        self.hn = mk.sbuf("hn", [128, DC, SEQ], BF16)
        self.hnb = mk.bufs("hn", DC, 4)
        if "hn_g" in io:
            for ti in range(4):
                r_, pc = divmod(ti, 2)
                hg, hgb = io["hn_g"][pc]
                hgv = hg[r_ * D:(r_ + 1) * D, :].rearrange("(c p) t -> p c t", p=128)
                for c in range(DC):
                    mk.dma("sp", self.hn[:, c, ti * 512:(ti + 1) * 512], hgv[:, c, :], reads=[hgb], writes=[self.hnb[c][ti]])
            self.y_cnt = [0, 0]
        else:
            hnT_d = self.din("hnT", [D, SEQ], BF16).rearrange("(c p) t -> p c t", p=128)
            for c in range(DC):
                for ti in range(4):
                    mk.dma("sp", self.hn[:, c, ti * 512:(ti + 1) * 512], hnT_d[:, c, ti * 512:(ti + 1) * 512], writes=[self.hnb[c][ti]])
            self.y_d = self.dout("yT_out", [D // 2, SEQ], BF16).rearrange("(c p) t -> p c t", p=128)
        self.ybuf = mk.sbuf("ybuf", [128, 2, 512], BF16)
        self.ybufb = mk.bufs("ybuf", 2)

    def hsrc(self):
        return (lambda k, ti: self.hn[:, k, ti * 512:(ti + 1) * 512]), (lambda k, ti: self.hnb[k][ti])

    def store_y(self, ch, ti, i):
        io = self.io
        if "y_p" in io:
            pc, cl = divmod(ch, 4)
            (yp, ypb), (yg, ygb) = io["y_p"][pc], io["y_g"][pc]
            self.mk.dma("sp", yp[cl * 128:(cl + 1) * 128, ti * 512:(ti + 1) * 512], self.ybuf[:, i, :], reads=[self.ybufb[i]], writes=[ypb])
            self.y_cnt[pc] += 1
            if self.y_cnt[pc] == 16:
                self.mk.cc_allgather(yp, yg, reads=[ypb], writes=[ygb], rg=io["rg"])
        else:
            self.mk.out_dma("sp", self.y_d[:, ch, ti * 512:(ti + 1) * 512], self.ybuf[:, i, :], reads=[self.ybufb[i]], sem=self.out_sem)


class ProgHGRN(ProgM):
    NSLOT = 8

    def __init__(self, layer, shared=None, pfx="", io=None):
        super().__init__(shared, pfx, io)
        mk = self.mk
        NH = 8
        win_d = self.din("w_in", [4 * NH, 128, D])
        lbp, lbpb = self.const("lbp", [128, 4, NH])
        gn, gnb = self.const("gn", [128, 1])
        self.gb = gnb
        ident, identb = self.const("ident", [128, 128], BF16)
        mask, maskb = self.const("mask64", [64, 64])
        sm = mk.sbuf("lbs", [128, 8, NH], F32)
        smb = mk.buf("lbs")
        R_ = [lbpb, smb]

        def tt(o, a, b, op):
            mk.op("dve", "tensor_tensor", reads=R_, writes=[smb], out=o, in0=a, in1=b, op=op)
        tt(sm[:, 0, :], lbp[:, 0, :], lbp[:, 1, :], ALU.max)
        tt(sm[:, 0, :], sm[:, 0, :], lbp[:, 2, :], ALU.max)
        tt(sm[:, 0, :], sm[:, 0, :], lbp[:, 3, :], ALU.max)
        for i in range(4):
            tt(sm[:, 1 + i, :], lbp[:, i, :], sm[:, 0, :], ALU.subtract)
            mk.op("act", "activation", reads=[smb], writes=[smb], out=sm[:, 1 + i, :], in_=sm[:, 1 + i, :], func=AF.Exp)
        tt(sm[:, 5, :], sm[:, 1, :], sm[:, 2, :], ALU.add)
        tt(sm[:, 5, :], sm[:, 5, :], sm[:, 3, :], ALU.add)
        tt(sm[:, 5, :], sm[:, 5, :], sm[:, 4, :], ALU.add)
        mk.op("dve", "reciprocal", reads=[smb], writes=[smb], out=sm[:, 5, :], in_=sm[:, 5, :])
        mk.op("dve", "memset", reads=[smb], writes=[smb], ap=sm[:, 6, :], constant=0.0)
        for i in range(1, layer + 1):
            tt(sm[:, 6, :], sm[:, 6, :], sm[:, 1 + i, :], ALU.add)
        tt(sm[:, 6, :], sm[:, 6, :], sm[:, 5, :], ALU.mult)
        mk.op("dve", "tensor_scalar", reads=[smb], writes=[smb], out=sm[:, 7, :], in0=sm[:, 6, :], scalar1=-1.0, scalar2=1.0,
              op0=ALU.mult, op1=ALU.add)
        mk.op("dve", "tensor_scalar", reads=[smb], writes=[smb], out=sm[:, 0, :], in0=sm[:, 7, :], scalar1=-1.0, scalar2=None,
              op0=ALU.mult)
        LB, OML, NOML = 6, 7, 0

        def tbuf(name, dt=F32):
            return mk.sbuf(name, [128, 2, 512], dt), mk.bufs(name, 2)

        def pbuf(name, dt=BF16):
            return mk.sbuf(name, [128, SEQ], dt), mk.bufs(name, 4)
        qs, qsb = tbuf("qs")
        sg_, sgb_ = tbuf("sgm")
        kk, kkb = tbuf("kk")
        bb, bbb = tbuf("bb")
        t1, t1b = tbuf("t1")
        ex, exb = tbuf("ex")
        oo, oob = tbuf("oo")
        vf, vfb = tbuf("vf", BF16)
        qt, qtb = pbuf("qt")
        kt, ktb = pbuf("kt")
        qh, qhb = pbuf("qh")
        kh, khb = pbuf("kh")
        gs, gsb = pbuf("gs")
        vtm = mk.sbuf("vtm", [64, 32, 128], BF16)
        vtmb = mk.bufs("vtm", 4)
        ktm = mk.sbuf("ktm", [64, 32, 128], BF16)
        ktmb = mk.bufs("ktm", 4)
        scs = mk.sbuf("scs", [64, 2, 8, 64], BF16)
        scsb = mk.bufs("scs", 2)
        dec = mk.sbuf("dec", [128, 32], F32)
        decb = mk.bufs("dec", 4)
        S = mk.sbuf("S", [128, 2, 128], F32)
        Sb = mk.bufs("S", 2)
        Sall = mk.sbuf("Sall", [128, 33, 128], BF16)
        Sallb = mk.bufs("Sall", 4)
        S0b = mk.buf("Sall0")
        mk.op("dve", "memset", writes=[S0b], ap=Sall[:, 0, :], constant=0.0)
        onesr = mk.sbuf("onesr", [128, 64], F32)
        onesrb = mk.buf("onesr")
        mk.op("dve", "memset", writes=[onesrb], ap=onesr[:], constant=1.0)
        tmp = mk.sbuf("tmpy", [128, 2, 512], F32)
        tmpb = mk.bufs("tmpy", 2)
        wt = {}
        st1 = {}

        def sl(ti):
            return slice(ti * 512, (ti + 1) * 512)

        def stage1(hd, ti):
            if ti == 0:
                wt[hd] = [self.wload(win_d[kind * NH + hd, :, :], D) for kind in range(4)]
            i = self.rr("tl", 2)
            s_ = sl(ti)

            def proj(kind, evac):
                w, wbuf = wt[hd][kind]
                p, pb = self.next_ps()
                for k in range(DC):
                    mk.op("pe", "matmul", reads=[wbuf, self.hnb[k][ti]], writes=[pb], out=p[:], lhsT=w[:, k * 128:(k + 1) * 128],
                          rhs=self.hn[:, k, s_], start=(k == 0), stop=(k == DC - 1))
                evac(p, pb)
            proj(1, lambda p, pb: mk.op("act", "activation", reads=[pb], writes=[sgb_[i]], out=sg_[:, i, :], in_=p[:], func=AF.Sigmoid))
            yield
            proj(0, lambda p, pb: mk.op("act", "activation", reads=[pb], writes=[qsb[i]], out=qs[:, i, :], in_=p[:], func=AF.Silu))
            yield
            mk.op("dve", "tensor_scalar", reads=[sgb_[i], smb], writes=[kkb[i]], out=kk[:, i, :], in0=sg_[:, i, :],
                  scalar1=sm[:, NOML, hd:hd + 1], scalar2=sm[:, OML, hd:hd + 1], op0=ALU.mult, op1=ALU.add)
            yield
            mk.op("dve", "tensor_scalar", reads=[sgb_[i], smb], writes=[sgb_[i]], out=sg_[:, i, :], in0=sg_[:, i, :],
                  scalar1=sm[:, OML, hd:hd + 1], scalar2=sm[:, LB, hd:hd + 1], op0=ALU.mult, op1=ALU.add)
            yield
            mk.op("dve", "tensor_scalar", reads=[sgb_[i]], writes=[sgb_[i]], out=sg_[:, i, :], in0=sg_[:, i, :],
                  scalar1=1e-12, scalar2=None, op0=ALU.max)
            yield
            mk.op("act", "activation", reads=[sgb_[i]], writes=[sgb_[i]], out=sg_[:, i, :], in_=sg_[:, i, :], func=AF.Ln)
            yield
            proj(2, lambda p, pb: mk.op("act", "activation", reads=[pb], writes=[vfb[i]], out=vf[:, i, :], in_=p[:], func=AF.Copy))
            yield
            for c in range(8):
                cs = slice(c * 64, (c + 1) * 64)
                mk.op("dve", "tensor_tensor_scan", reads=[sgb_[i], onesrb], writes=[bbb[i]], out=bb[:, i, cs], data0=onesr[:, :],
                      data1=sg_[:, i, cs], initial=0.0, op0=ALU.mult, op1=ALU.add)
            yield
            pstep = bb[:, 0, 0:1].ap[0][0]
            b3 = v3(bb[:, i, :], 8, 64)
            bref = bass.AP(bb, bb[:, i, 31:32].offset, [[pstep, 128], [64, 8], [0, 64]])
            blast = bass.AP(bb, bb[:, i, 63:64].offset, [[pstep, 128], [64, 8], [0, 64]])
            bl2 = bass.AP(bb, bb[:, i, 63:64].offset, [[pstep, 128], [64, 8]])
            mk.op("dve", "tensor_tensor", reads=[bbb[i]], writes=[t1b[i]], out=v3(t1[:, i, :], 8, 64), in0=b3, in1=bref, op=ALU.subtract)
            yield
            mk.op("act", "activation", reads=[t1b[i]], writes=[exb[i]], out=ex[:, i, :], in_=t1[:, i, :], func=AF.Exp)
            yield
            mk.op("dve", "tensor_tensor", reads=[exb[i], qsb[i]], writes=[qtb[ti]], out=qt[:, s_], in0=ex[:, i, :], in1=qs[:, i, :], op=ALU.mult)
            yield
            mk.op("act", "activation", reads=[t1b[i]], writes=[exb[i]], out=ex[:, i, :], in_=t1[:, i, :], func=AF.Exp, scale=-1.0)
            yield
            mk.op("dve", "tensor_tensor", reads=[exb[i], kkb[i]], writes=[ktb[ti]], out=kt[:, s_], in0=ex[:, i, :], in1=kk[:, i, :], op=ALU.mult)
            yield
            proj(3, lambda p, pb: mk.op("act", "activation", reads=[pb], writes=[gsb[ti]], out=gs[:, s_], in_=p[:], func=AF.Silu))
            yield
            mk.op("act", "activation", reads=[bbb[i]], writes=[exb[i]], out=ex[:, i, :], in_=bb[:, i, :], func=AF.Exp)
            yield
            mk.op("dve", "tensor_tensor", reads=[exb[i], qsb[i]], writes=[qhb[ti]], out=qh[:, s_], in0=ex[:, i, :], in1=qs[:, i, :], op=ALU.mult)
            yield
            mk.op("dve", "tensor_tensor", reads=[bbb[i]], writes=[t1b[i]], out=v3(t1[:, i, :], 8, 64), in0=blast, in1=b3, op=ALU.subtract)
            yield
            mk.op("act", "activation", reads=[t1b[i]], writes=[exb[i]], out=ex[:, i, :], in_=t1[:, i, :], func=AF.Exp)
            yield
            mk.op("dve", "tensor_tensor", reads=[exb[i], kkb[i]], writes=[khb[ti]], out=kh[:, s_], in0=ex[:, i, :], in1=kk[:, i, :], op=ALU.mult)
            yield
            mk.op("act", "activation", reads=[bbb[i]], writes=[decb[ti]], out=dec[:, ti * 8:(ti + 1) * 8], in_=bl2, func=AF.Exp)
            yield
            st1[(hd, ti)] = i

        st2 = {}

        def stage2a(hd, ti):
            s_ = sl(ti)
            if ti == 0:
                mk.op("dve", "memset", writes=[Sb[1]], ap=S[:, 1, :], constant=0.0)
            i1 = st1.pop((hd, ti))
            for (srcf, srcfb, dstt, dsttb) in ((kh[:, s_], khb[ti], ktm, ktmb), (vf[:, i1, :], vfb[i1], vtm, vtmb)):
                p, pb = self.next_ps()
                pv = p.bitcast(BF16)
                for c in range(8):
                    mk.op("pe", "transpose", reads=[srcfb, identb], writes=[pb], out=pv[0:64, c * 128:(c + 1) * 128],
                          in_=srcf[:, c * 64:(c + 1) * 64], identity=ident[:])
                mk.op("act", "activation", reads=[pb], writes=[dsttb[ti]], out=dstt[:, ti * 8:(ti + 1) * 8, :],
                      in_=pv[0:64, :].rearrange("p (c i) -> p c i", i=128), func=AF.Copy)

            p, pb = self.next_ps()
            for c in range(8):
                cs = slice(ti * 512 + c * 64, ti * 512 + (c + 1) * 64)
                mk.op("pe", "matmul", reads=[ktb[ti], qtb[ti]], writes=[pb], out=p[0:64, c * 64:(c + 1) * 64], lhsT=kt[:, cs], rhs=qt[:, cs],
                      start=True, stop=True)
            si = self.rr("scs", 2)
            mbc = bass.AP(mask, mask[:, 0:1].offset, [[mask[:, 0:1].ap[0][0], 64], [0, 8], [1, 64]])
            mk.op("dve", "tensor_tensor", reads=[pb, maskb], writes=[scsb[si]], out=scs[:, si, :, :], in0=v3(p[0:64, :], 8, 64), in1=mbc, op=ALU.mult)
            pst = [self.next_ps(), self.next_ps()]
            for c in range(8):
                cg = ti * 8 + c
                pp, ppb = pst[c // 4]
                mk.op("pe", "matmul", reads=[ktmb[ti], vtmb[ti]], writes=[ppb], out=pp[:, (c % 4) * 128:(c % 4 + 1) * 128], lhsT=ktm[:, cg, :],
                      rhs=vtm[:, cg, :], start=True, stop=True)
            for c in range(8):
                cg = ti * 8 + c
                par = cg % 2
                pp, ppb = pst[c // 4]
                mk.op("dve", "scalar_tensor_tensor", reads=[Sb[1 - par], decb[ti], ppb], writes=[Sb[par]], out=S[:, par, :], in0=S[:, 1 - par, :],
                      scalar=dec[:, cg:cg + 1], in1=pp[:, (c % 4) * 128:(c % 4 + 1) * 128], op0=ALU.mult, op1=ALU.add)
                mk.op("act", "activation", reads=[Sb[par]], writes=[Sallb[ti]], out=Sall[:, cg + 1, :], in_=S[:, par, :], func=AF.Copy)
            st2[(hd, ti)] = si

        def stage2b(hd, ti):
            s_ = sl(ti)
            si = st2.pop((hd, ti))
            po, pob = self.next_ps_long()
            for c in range(8):
                cg = ti * 8 + c
                cs = slice(ti * 512 + c * 64, ti * 512 + (c + 1) * 64)
                prevb = S0b if cg == 0 else (Sallb[ti - 1] if c == 0 else Sallb[ti])
                mk.op("pe", "matmul", reads=[vtmb[ti], scsb[si]], writes=[pob], out=po[:, c * 64:(c + 1) * 64], lhsT=vtm[:, cg, :], rhs=scs[:, si, c, :],
                      start=True, stop=False)
                mk.op("pe", "matmul", reads=[prevb, qhb[ti]], writes=[pob], out=po[:, c * 64:(c + 1) * 64], lhsT=Sall[:, cg, :], rhs=qh[:, cs],
                      start=False, stop=True)
            oi = self.rr("oo", 2)
            mk.op("act", "activation", reads=[pob], writes=[oob[oi]], out=oo[:, oi, :], in_=po[:], func=AF.Copy)
            p2, p2b = self.sumsq_bcast([(oo[:, oi, :], oob[oi])], 512)
            r, rb = self.rstd_from(p2, p2b, 512, 128)
            i = self.rr("ybuf", 2)
            mk.op("dve", "scalar_tensor_tensor", reads=[oob[oi], gnb, rb], writes=[tmpb[i]], out=tmp[:, i, :], in0=oo[:, oi, :], scalar=gn[:, 0:1],
                  in1=r, op0=ALU.mult, op1=ALU.mult)
            mk.op("dve", "tensor_tensor", reads=[tmpb[i], gsb[ti]], writes=[self.ybufb[i]], out=self.ybuf[:, i, :], in0=tmp[:, i, :], in1=gs[:, s_], op=ALU.mult)
            self.store_y(hd, ti, i)

        units = [(hd, ti) for hd in range(NH) for ti in range(4)]
        pairs = [(units[2 * k], units[2 * k + 1]) for k in range(len(units) // 2)]
        prev = None
        for pr in pairs + [None]:
            if prev is not None:
                stage2a(*prev[0])
                stage2a(*prev[1])
            if pr is not None:
                gens = [stage1(*pr[0]), stage1(*pr[1])]
                while gens:
                    for g in list(gens):
                        try:
                            next(g)
                        except StopIteration:
                            gens.remove(g)
            if prev is not None:
                stage2b(*prev[0])
                stage2b(*prev[1])
            prev = pr
        self.finish()


class ProgLRU(ProgM):
    def __init__(self, shared=None, pfx="", io=None):
        super().__init__(shared, pfx, io)
        mk = self.mk
        NCH = 8
        win_d = self.din("w_in", [2 * NCH, 128, D])
        wax_d = self.din("w_ax", [2 * NCH, 128, 256])
        cw, cwb = self.const("conv_w", [128, 4, NCH])
        vec, vecb = self.const("vec", [128, 4, NCH])
        cl = mk.sbuf("clam", [128, 2, NCH], F32)
        clb = mk.buf("clam")
        one_t = mk.sbuf("one_t", [128, 1], F32)
        oneb = mk.buf("one_t")
        mk.op("dve", "memset", writes=[oneb], ap=one_t[:], constant=1.0)
        mk.op("act", "activation", reads=[vecb], writes=[clb], out=cl[:, 0, :], in_=vec[:, 3, :], func=AF.Exp, scale=-1.0)
        mk.op("act", "activation", reads=[clb, oneb], writes=[clb], out=cl[:, 0, :], in_=cl[:, 0, :], func=AF.Ln, bias=one_t[:, 0:1])
        mk.op("dve", "tensor_scalar", reads=[clb], writes=[clb], out=cl[:, 1, :], in0=cl[:, 0, :], scalar1=-16.0, scalar2=None, op0=ALU.mult)
        mk.op("dve", "tensor_scalar", reads=[clb], writes=[clb], out=cl[:, 0, :], in0=cl[:, 0, :], scalar1=-8.0, scalar2=None, op0=ALU.mult)

        def fbuf(name, dt=F32, n=4):
            return mk.sbuf(name, [128, SEQ], dt), mk.bufs(name, n)
        u, ub = fbuf("u")
        uc = [fbuf("uc%d" % i, F32, 1) for i in range(2)]
        u16 = [fbuf("u16_%d" % i, BF16, 1) for i in range(2)]
        r_, rb_ = fbuf("r")
        ig, igb = fbuf("ig")
        a_, ab_ = fbuf("a")
        t_, tb_ = fbuf("t")
        h_, hb_ = fbuf("h")
        gx, gxb = fbuf("gx")
        src, srcb = self.hsrc()
        T4 = range(4)

        def sl(ti):
            return slice(ti * 512, (ti + 1) * 512)

        def ev_act(dst, dstb, func, bias=None, extra=()):
            def evac(j, ti, p, pb):
                kw = {}
                if bias is not None:
                    kw["bias"] = bias
                mk.op("act", "activation", reads=[pb] + list(extra), writes=[dstb[ti]], out=dst[:, sl(ti)], in_=p[:], func=func, **kw)
            return evac

        for bl in range(4):
            for q in range(2):
                cc = bl * 2 + q
                ucq, ucqb = uc[q]
                self.linear_fm(win_d, [NCH + cc], DC, src, srcb, T4, ev_act(u, ub, AF.Copy))
                mk.op("dve", "tensor_scalar", reads=ub + [cwb, vecb], writes=[ucqb[0]], out=ucq[:, :], in0=u[:, :], scalar1=cw[:, 3, cc:cc + 1],
                      scalar2=vec[:, 0, cc:cc + 1], op0=ALU.mult, op1=ALU.add)
                for sh in (1, 2, 3):
                    mk.op("dve", "scalar_tensor_tensor", reads=ub + [cwb, ucqb[0]], writes=[ucqb[0]], out=ucq[:, sh:], in0=u[:, 0:SEQ - sh],
                          scalar=cw[:, 3 - sh, cc:cc + 1], in1=ucq[:, sh:], op0=ALU.mult, op1=ALU.add)
                mk.op("act", "activation", reads=[ucqb[0]], writes=[u16[q][1][0]], out=u16[q][0][:, :], in_=ucq[:, :], func=AF.Copy)
            for q in range(2):
                cc = bl * 2 + q
                ucq, ucqb = uc[q]

                def usrc(k, ti):
                    return u16[k][0][:, sl(ti)]

                def usrcb(k, ti):
                    return u16[k][1][0]
                self.linear_fm(wax_d, [cc], 2, usrc, usrcb, T4, ev_act(r_, rb_, AF.Sigmoid, bias=vec[:, 1, cc:cc + 1], extra=[vecb]))
                self.linear_fm(wax_d, [NCH + cc], 2, usrc, usrcb, T4, ev_act(ig, igb, AF.Sigmoid, bias=vec[:, 2, cc:cc + 1], extra=[vecb]))
                self.linear_fm(win_d, [cc], DC, src, srcb, T4, ev_act(gx, gxb, AF.Copy))
                def chain(ti, cc=cc, ucq=ucq, ucqb=ucqb):
                    s_ = sl(ti)
                    mk.op("act", "activation", reads=[rb_[ti], clb], writes=[ab_[ti]], out=a_[:, s_], in_=r_[:, s_], func=AF.Exp, scale=cl[:, 0, cc:cc + 1])
                    yield
                    mk.op("act", "activation", reads=[rb_[ti], clb], writes=[tb_[ti]], out=t_[:, s_], in_=r_[:, s_], func=AF.Exp, scale=cl[:, 1, cc:cc + 1])
                    yield
                    mk.op("dve", "tensor_scalar", reads=[tb_[ti]], writes=[tb_[ti]], out=t_[:, s_], in0=t_[:, s_], scalar1=-1.0, scalar2=1.0,
                          op0=ALU.mult, op1=ALU.add)
                    yield
                    mk.op("dve", "tensor_scalar", reads=[tb_[ti]], writes=[tb_[ti]], out=t_[:, s_], in0=t_[:, s_], scalar1=0.0, scalar2=None, op0=ALU.max)
                    yield
                    mk.op("act", "activation", reads=[tb_[ti]], writes=[tb_[ti]], out=t_[:, s_], in_=t_[:, s_], func=AF.Sqrt)
                    yield
                    mk.op("dve", "tensor_tensor", reads=[tb_[ti], igb[ti]], writes=[tb_[ti]], out=t_[:, s_], in0=t_[:, s_], in1=ig[:, s_], op=ALU.mult)
                    yield
                    mk.op("dve", "tensor_tensor", reads=[tb_[ti], ucqb[0]], writes=[tb_[ti]], out=t_[:, s_], in0=t_[:, s_], in1=ucq[:, s_], op=ALU.mult)
                    yield
                    init = 0.0 if ti == 0 else h_[:, ti * 512 - 1:ti * 512]
                    rd = [ab_[ti], tb_[ti]] + ([hb_[ti - 1]] if ti > 0 else [])
                    yield
                    mk.op("dve", "tensor_tensor_scan", reads=rd, writes=[hb_[ti]], out=h_[:, s_], data0=a_[:, s_], data1=t_[:, s_], initial=init,
                          op0=ALU.mult, op1=ALU.add)
                    yield
                    yield
                    mk.op("dve", "tensor_tensor", reads=[gxb[ti]], writes=[ab_[ti]], out=a_[:, s_], in0=gx[:, s_], in1=gx[:, s_], op=ALU.mult)
                    yield
                    mk.op("dve", "tensor_scalar", reads=[ab_[ti]], writes=[ab_[ti]], out=a_[:, s_], in0=a_[:, s_], scalar1=0.044715, scalar2=1.0,
                          op0=ALU.mult, op1=ALU.add)
                    yield
                    mk.op("dve", "tensor_tensor", reads=[ab_[ti], gxb[ti]], writes=[ab_[ti]], out=a_[:, s_], in0=a_[:, s_], in1=gx[:, s_], op=ALU.mult)
                    yield
                    mk.op("act", "activation", reads=[ab_[ti]], writes=[ab_[ti]], out=a_[:, s_], in_=a_[:, s_], func=AF.Sigmoid, scale=2.0 * 0.7978845608028654)
                    yield
                    mk.op("dve", "tensor_tensor", reads=[ab_[ti], gxb[ti]], writes=[ab_[ti]], out=a_[:, s_], in0=a_[:, s_], in1=gx[:, s_], op=ALU.mult)
                    yield
                    i = self.rr("ybuf", 2)
                    yield
                    mk.op("dve", "tensor_tensor", reads=[ab_[ti], hb_[ti]], writes=[self.ybufb[i]], out=self.ybuf[:, i, :], in0=a_[:, s_], in1=h_[:, s_], op=ALU.mult)
                    self.store_y(cc, ti, i)
                    yield
                gens = [chain(ti) for ti in T4]
                while gens:
                    for g in list(gens):
                        try:
                            next(g)
                        except StopIteration:
                            gens.remove(g)
        self.finish()


class ProgMLSTM(ProgM):
    def __init__(self, shared=None, pfx="", io=None):
        super().__init__(shared, pfx, io)
        mk = self.mk
        NH = 4
        NCK = 32
        win_d = self.din("w_in", [24, 128, D])
        wg, wgb = self.const("w_gate", [128, DC, 8])
        bif, bifb = self.const("b_if", [4, 2])
        gn, gnb = self.const("gn", [128, 2])
        self.gb = gnb
        sel, selb = self.const("sel", [4, 4 * 128])
        ident, identb = self.const("ident", [128, 128], BF16)
        mask, maskb = self.const("mask64", [64, 64])
        wg16 = mk.sbuf("wg16", [128, DC, 8], BF16)
        wg16b = mk.buf("wg16")
        mk.op("act", "activation", reads=[wgb], writes=[wg16b], out=wg16[:], in_=wg[:], func=AF.Copy)
        one4 = mk.sbuf("one4", [4, 1], F32)
        one4b = mk.buf("one4")
        mk.op("dve", "memset", writes=[one4b], ap=one4[:], constant=1.0)
        b15 = mk.sbuf("b15", [4, 2], F32)
        b15b = mk.buf("b15")
        mk.op("dve", "tensor_scalar", reads=[bifb], writes=[b15b], out=b15[:], in0=bif[:], scalar1=1.0 / 15.0, scalar2=None, op0=ALU.mult)
        src, srcb = self.hsrc()
        T4 = range(4)

        def sl(ti):
            return slice(ti * 512, (ti + 1) * 512)

        def row(name):
            return mk.sbuf(name, [4, SEQ], F32), mk.buf(name)
        it, itb = row("g_it")
        bn, bnb = row("g_bn")
        aa, aab = row("g_a")
        AA, AAb = row("g_A")
        tr, trb = row("g_tr")
        qf, qfb = it, itb
        kf, kfb = aa, aab
        em, emb = bn, bnb
        zr4 = mk.sbuf("g_zero", [4, 1], F32)
        zrb = mk.buf("g_zero")
        mk.op("dve", "memset", writes=[zrb], ap=zr4[:], constant=0.0)
        zr_bc = bass.AP(zr4, zr4[:, 0:1].offset, [[zr4[:, 0:1].ap[0][0], 4], [0, SEQ]])
        R33 = mk.sbuf("g_R33", [4, NCK + 1], F32)
        R33b = mk.buf("g_R33")
        dec4 = mk.sbuf("g_dec", [4, NCK], F32)
        dec4b = mk.buf("g_dec")
        for gi, dst, dstb in ((0, it, itb), (1, bn, bnb)):
            for ti in T4:
                p, pb = self.next_ps()
                for k in range(DC):
                    mk.op("pe", "matmul", reads=[wg16b, self.hnb[k][ti]], writes=[pb], out=p[0:4, :], lhsT=wg16[:, k, gi * 4:(gi + 1) * 4],
                          rhs=self.hn[:, k, sl(ti)], start=(k == 0), stop=(k == DC - 1))
                mk.op("act", "activation", reads=[pb, b15b], writes=[dstb], out=dst[:, sl(ti)], in_=p[0:4, :], func=AF.Tanh,
                      bias=b15[:, gi:gi + 1], scale=1.0 / 15.0)
        mk.op("dve", "tensor_scalar", reads=[itb], writes=[itb], out=it[:], in0=it[:], scalar1=15.0, scalar2=None, op0=ALU.mult)
        mk.op("act", "activation", reads=[bnb], writes=[bnb], out=bn[:], in_=bn[:], func=AF.Exp, scale=-15.0)
        mk.op("act", "activation", reads=[bnb, one4b], writes=[bnb], out=bn[:], in_=bn[:], func=AF.Ln, bias=one4[:, 0:1])
        mk.op("dve", "tensor_tensor_scan", reads=[bnb, zrb], writes=[trb], out=tr[:], data0=bn[:], data1=zr_bc, initial=0.0, op0=ALU.add, op1=ALU.add)
        mk.op("dve", "tensor_tensor", reads=[trb, itb], writes=[aab], out=aa[:], in0=tr[:], in1=it[:], op=ALU.add)
        mk.op("dve", "tensor_tensor_scan", reads=[aab], writes=[AAb], out=AA[:], data0=aa[:], data1=aa[:], initial=0.0, op0=ALU.max, op1=ALU.max)
        pstep = AA[:, 0:1].ap[0][0]
        Rbc = bass.AP(AA, AA[:, 63:64].offset, [[pstep, 4], [64, NCK], [0, 64]])
        Rv = bass.AP(AA, AA[:, 63:64].offset, [[pstep, 4], [64, NCK]])
        mk.op("dve", "tensor_tensor", reads=[trb, AAb], writes=[emb], out=em[:], in0=tr[:], in1=AA[:], op=ALU.subtract)
        mk.op("act", "activation", reads=[emb], writes=[emb], out=em[:], in_=em[:], func=AF.Exp)
        mk.op("dve", "tensor_tensor", reads=[AAb], writes=[qfb], out=v3(qf[:], NCK, 64), in0=Rbc, in1=v3(AA[:], NCK, 64), op=ALU.subtract)
        mk.op("act", "activation", reads=[qfb], writes=[qfb], out=qf[:], in_=qf[:], func=AF.Exp)
        mk.op("dve", "tensor_tensor", reads=[AAb, aab], writes=[kfb], out=v3(kf[:], NCK, 64), in0=v3(aa[:], NCK, 64), in1=Rbc, op=ALU.subtract)
        mk.op("act", "activation", reads=[kfb], writes=[kfb], out=kf[:], in_=kf[:], func=AF.Exp)
        mk.op("dve", "tensor_scalar", reads=[kfb], writes=[kfb], out=kf[:], in0=kf[:], scalar1=128.0 ** -0.5, scalar2=None, op0=ALU.mult)
        mk.op("dve", "memset", writes=[R33b], ap=R33[:, 0:1], constant=0.0)
        mk.op("dve", "tensor_copy", reads=[AAb, R33b], writes=[R33b], out=R33[:, 1:NCK + 1], in_=Rv)
        mk.op("dve", "tensor_tensor", reads=[R33b], writes=[dec4b], out=dec4[:], in0=R33[:, 0:NCK], in1=R33[:, 1:NCK + 1], op=ALU.subtract)
        mk.op("act", "activation", reads=[dec4b], writes=[dec4b], out=dec4[:], in_=dec4[:], func=AF.Exp)

        def fbuf(name, dt=F32, n=4):
            return mk.sbuf(name, [128, SEQ], dt), mk.bufs(name, n)
        fb, fbb = fbuf("fb")
        qt, qtb = fbuf("qt", BF16)
        kt, ktb = fbuf("kt", BF16)
        vf, vfb = fbuf("vf", BF16)
        og = mk.sbuf("og", [128, 2, SEQ], BF16)
        ogb = mk.bufs("og", 2, 4)
        ktm = mk.sbuf("ktm", [64, NCK, 128], BF16)
        ktmb = mk.bufs("ktm", 4)
        vtm = mk.sbuf("vtm", [64, NCK, 384], BF16)
        vtmb = mk.bufs("vtm", 4)
        mk.op("dve", "memset", writes=vtmb, ap=vtm[:, :, 256:384], constant=1.0)
        scs = mk.sbuf("scs", [64, 2, 8, 64], BF16)
        scsb = mk.bufs("scs", 2)
        decb_ = mk.sbuf("decb", [128, NCK], F32)
        decbb = mk.buf("decb")
        C = mk.sbuf("C", [128, 2, 384], F32)
        Cb = mk.bufs("C", 2)
        Cd = mk.sbuf("Cd", [128, 2, 384], BF16)
        Cdb = mk.bufs("Cd", 2)
        nm = mk.sbuf("nm", [128, 2, 512], F32)
        nmb = mk.bufs("nm", 2)
        dn = mk.sbuf("dn", [128, 512], F32)
        dnb = mk.buf("dn")

        def bcast(rowt, rowb, hd, ti, dst_ap, dstb):
            p, pb = self.next_ps()
            mk.op("pe", "matmul", reads=[selb, rowb], writes=[pb], out=p[:], lhsT=sel[0:4, hd * 128:(hd + 1) * 128], rhs=rowt[0:4, sl(ti)],
                  start=True, stop=True)
            mk.op("act", "activation", reads=[pb], writes=[dstb], out=dst_ap, in_=p[:], func=AF.Copy)

        for hd in range(NH):
            for ti in T4:
                bcast(qf, qfb, hd, ti, fb[:, sl(ti)], fbb[ti])

            def ev_mul(dst, dstb):
                def evac(j, ti, p, pb):
                    mk.op("dve", "tensor_tensor", reads=[pb, fbb[ti]], writes=[dstb[ti]], out=dst[:, sl(ti)], in0=p[:], in1=fb[:, sl(ti)], op=ALU.mult)
                return evac
            self.linear_fm(win_d, [hd], DC, src, srcb, T4, ev_mul(qt, qtb))
            for ti in T4:
                bcast(kf, kfb, hd, ti, fb[:, sl(ti)], fbb[ti])
            self.linear_fm(win_d, [4 + hd], DC, src, srcb, T4, ev_mul(kt, ktb))
            p, pb = self.next_ps()
            mk.op("pe", "matmul", reads=[selb, dec4b], writes=[pb], out=p[:, 0:NCK], lhsT=sel[0:4, hd * 128:(hd + 1) * 128], rhs=dec4[0:4, :], start=True, stop=True)
            mk.op("act", "activation", reads=[pb], writes=[decbb], out=decb_[:], in_=p[:, 0:NCK], func=AF.Copy)
            for ti in T4:
                p, pb = self.next_ps()
                pv = p.bitcast(BF16)
                for c in range(8):
                    cs = slice(ti * 512 + c * 64, ti * 512 + (c + 1) * 64)
                    mk.op("pe", "transpose", reads=[ktb[ti], identb], writes=[pb], out=pv[0:64, c * 128:(c + 1) * 128], in_=kt[:, cs], identity=ident[:])
                mk.op("act", "activation", reads=[pb], writes=[ktmb[ti]], out=ktm[:, ti * 8:(ti + 1) * 8, :],
                      in_=pv[0:64, :].rearrange("p (c i) -> p c i", i=128), func=AF.Copy)
            for vc in range(2):
                def evac_v(j, ti, p, pb):
                    mk.op("act", "activation", reads=[pb], writes=[vfb[ti]], out=vf[:, sl(ti)], in_=p[:], func=AF.Copy)
                self.linear_fm(win_d, [8 + hd * 2 + vc], DC, src, srcb, T4, evac_v)
                for ti in T4:
                    p, pb = self.next_ps()
                    pv = p.bitcast(BF16)
                    for c in range(8):
                        cs = slice(ti * 512 + c * 64, ti * 512 + (c + 1) * 64)
                        mk.op("pe", "transpose", reads=[vfb[ti], identb], writes=[pb], out=pv[0:64, c * 128:(c + 1) * 128], in_=vf[:, cs], identity=ident[:])
                    mk.op("act", "activation", reads=[pb], writes=[vtmb[ti]], out=vtm[:, ti * 8:(ti + 1) * 8, vc * 128:(vc + 1) * 128],
                          in_=pv[0:64, :].rearrange("p (c i) -> p c i", i=128), func=AF.Copy)

                def evac_o(j, ti, p, pb, vc=vc):
                    mk.op("act", "activation", reads=[pb], writes=[ogb[vc][ti]], out=og[:, vc, sl(ti)], in_=p[:], func=AF.Sigmoid)
                self.linear_fm(win_d, [16 + hd * 2 + vc], DC, src, srcb, T4, evac_o)
            mk.op("dve", "memset", writes=[Cb[1]], ap=C[:, 1, :], constant=0.0)
            for ti in T4:
                p, pb = self.next_ps()
                for c in range(8):
                    cs = slice(ti * 512 + c * 64, ti * 512 + (c + 1) * 64)
                    mk.op("pe", "matmul", reads=[ktb[ti], qtb[ti]], writes=[pb], out=p[0:64, c * 64:(c + 1) * 64], lhsT=kt[:, cs], rhs=qt[:, cs], start=True, stop=True)
                si = self.rr("scs", 2)
                mbc = bass.AP(mask, mask[:, 0:1].offset, [[mask[:, 0:1].ap[0][0], 64], [0, 8], [1, 64]])
                mk.op("dve", "tensor_tensor", reads=[pb, maskb], writes=[scsb[si]], out=scs[:, si, :, :], in0=v3(p[0:64, :], 8, 64), in1=mbc, op=ALU.mult)
                pos = [self.next_ps_long() for _ in range(3)]
                pstq = {}

                def emit_pst(c):
                    cg = ti * 8 + c
                    pst, pstb = self.next_ps()
                    mk.op("pe", "matmul", reads=[ktmb[ti], vtmb[ti]], writes=[pstb], out=pst[:, 0:384], lhsT=ktm[:, cg, :], rhs=vtm[:, cg, :], start=True, stop=True)
                    pstq[c] = (pst, pstb)
                emit_pst(0)
                emit_pst(1)
                for c in range(8):
                    cg = ti * 8 + c
                    par = cg % 2
                    ci = c % 2
                    cs = slice(ti * 512 + c * 64, ti * 512 + (c + 1) * 64)
                    mk.op("act", "activation", reads=[Cb[1 - par], decbb], writes=[Cdb[ci]], out=Cd[:, ci, :], in_=C[:, 1 - par, :], func=AF.Copy,
                          scale=decb_[:, cg:cg + 1])
                    for oc in range(3):
                        po, pob = pos[oc]
                        mk.op("pe", "matmul", reads=[vtmb[ti], scsb[si]], writes=[pob], out=po[:, c * 64:(c + 1) * 64], lhsT=vtm[:, cg, oc * 128:(oc + 1) * 128],
                              rhs=scs[:, si, c, :], start=True, stop=False)
                        mk.op("pe", "matmul", reads=[Cdb[ci], qtb[ti]], writes=[pob], out=po[:, c * 64:(c + 1) * 64], lhsT=Cd[:, ci, oc * 128:(oc + 1) * 128],
                              rhs=qt[:, cs], start=False, stop=True)
                    if c + 2 < 8:
                        emit_pst(c + 2)
                    pst, pstb = pstq.pop(c)
                    mk.op("dve", "scalar_tensor_tensor", reads=[Cb[1 - par], decbb, pstb], writes=[Cb[par]], out=C[:, par, :], in0=C[:, 1 - par, :],
                          scalar=decb_[:, cg:cg + 1], in1=pst[:, 0:384], op0=ALU.mult, op1=ALU.add)
                pem, pemb = self.next_ps()
                mk.op("pe", "matmul", reads=[selb, emb], writes=[pemb], out=pem[:], lhsT=sel[0:4, hd * 128:(hd + 1) * 128], rhs=em[0:4, sl(ti)],
                      start=True, stop=True)
                mk.op("act", "activation", reads=[pos[2][1]], writes=[dnb], out=dn[:], in_=pos[2][0][:], func=AF.Copy)
                mk.op("dve", "scalar_tensor_tensor", reads=[dnb], writes=[dnb], out=dn[:], in0=dn[:], scalar=-1.0, in1=dn[:], op0=ALU.mult, op1=ALU.max)
                mk.op("dve", "tensor_tensor", reads=[dnb, pemb], writes=[dnb], out=dn[:], in0=dn[:], in1=pem[:], op=ALU.max)
                mk.op("dve", "reciprocal", reads=[dnb], writes=[dnb], out=dn[:], in_=dn[:])
                for oc in range(2):
                    mk.op("dve", "tensor_tensor", reads=[pos[oc][1], dnb], writes=[nmb[oc]], out=nm[:, oc, :], in0=pos[oc][0][:], in1=dn[:], op=ALU.mult)
                p2, p2b = self.sumsq_bcast([(nm[:, oc, :], nmb[oc]) for oc in range(2)], 512)
                r, rb = self.rstd_from(p2, p2b, 512, 256)
                for oc in range(2):
                    i = self.rr("ybuf", 2)
                    mk.op("dve", "scalar_tensor_tensor", reads=[nmb[oc], gnb, rb], writes=[nmb[oc]], out=nm[:, oc, :], in0=nm[:, oc, :], scalar=gn[:, oc:oc + 1],
                          in1=r, op0=ALU.mult, op1=ALU.mult)
                    mk.op("dve", "tensor_tensor", reads=[nmb[oc], ogb[oc][ti]], writes=[self.ybufb[i]], out=self.ybuf[:, i, :], in0=nm[:, oc, :], in1=og[:, oc, sl(ti)], op=ALU.mult)
                    self.store_y(hd * 2 + oc, ti, i)
        self.finish()


def mlstm_inputs(inp, idx, hh):
    W = inp["ml_w_in"][idx]
    wt = tile_w(W[:, :6144])
    h0 = hh * 4
    w_in = np.concatenate([wt[h0:h0 + 4], wt[8 + h0:8 + h0 + 4], wt[16 + h0 * 2:16 + h0 * 2 + 8], wt[32 + h0 * 2:32 + h0 * 2 + 8]], axis=0)
    gcols = np.concatenate([W[:, 6144 + h0:6144 + h0 + 4], W[:, 6152 + h0:6152 + h0 + 4]], axis=1)
    w_gate = np.ascontiguousarray(gcols.reshape(DC, 128, 8).transpose(1, 0, 2))
    b_if = np.ascontiguousarray(inp["ml_b_if"][idx][:, h0:h0 + 4].T)
    gn = fmv(inp["ml_g_norm"][idx])
    sel = np.zeros((4, 4 * 128), np.float32)
    for h in range(4):
        sel[h, h * 128:(h + 1) * 128] = 1.0
    return {"w_in": w_in, "w_gate": w_gate, "b_if": b_if, "gn": gn, "sel": sel}


def lru_inputs(inp, idx, hh):
    wt = tile_w(inp["lru_w_in"][idx])
    w_in = np.concatenate([wt[hh * 8:hh * 8 + 8], wt[16 + hh * 8:16 + hh * 8 + 8]], axis=0)
    wa = np.concatenate([tile_w(inp["lru_w_a"][idx, hh * 4 + b]) for b in range(4)], axis=0)
    wx = np.concatenate([tile_w(inp["lru_w_x"][idx, hh * 4 + b]) for b in range(4)], axis=0)
    sl = slice(hh * 1024, (hh + 1) * 1024)
    conv_w = np.ascontiguousarray(inp["lru_conv_w"][idx][:, sl].reshape(4, 8, 128).transpose(2, 0, 1))
    vec = np.stack([fmv(inp[k][idx][sl]) for k in ("lru_conv_b", "lru_b_a", "lru_b_x", "lru_lambda")], axis=1)
    return {"w_in": w_in, "w_ax": np.concatenate([wa, wx], axis=0), "conv_w": conv_w, "vec": np.ascontiguousarray(vec)}


def hgrn_inputs(inp, idx, hh):
    wt = tile_w(inp["hg_w_in"][idx])
    sel = np.concatenate([wt[kind * 16 + hh * 8: kind * 16 + hh * 8 + 8] for kind in range(4)], axis=0)
    lbp = np.ascontiguousarray(inp["hg_lb_param"][:, hh * 1024:(hh + 1) * 1024].reshape(4, 8, 128).transpose(2, 0, 1))
    return {"w_in": sel, "lbp": lbp, "gn": np.ascontiguousarray(inp["hg_g_norm"][idx].reshape(128, 1))}


class Fused(Prog):
    def __init__(self, depth=DEPTH, ncores=8):
        super().__init__()
        nc, mk = self.nc, self.mk
        rg = [[2 * i, 2 * i + 1] for i in range(ncores // 2)]

        def dbuf(name, shape, dt):
            return nc.dram_tensor(name, list(shape), dt).ap(), mk.buf(name)
        xs = [dbuf("xs%d" % l, [D, TA], F32) for l in range(depth)]
        hn_p = [[dbuf("hnp%d_%d" % (l, pc), [D, 512], BF16) for pc in range(2)] for l in range(depth)]
        hn_g = [[dbuf("hng%d_%d" % (l, pc), [2 * D, 512], BF16) for pc in range(2)] for l in range(depth)]
        y_p = [[dbuf("yp%d_%d" % (l, pc), [512, SEQ], BF16) for pc in range(2)] for l in range(depth)]
        y_g = [[dbuf("yg%d_%d" % (l, pc), [1024, SEQ], BF16) for pc in range(2)] for l in range(depth)]
        for l in range(depth + 1):
            io = {"rg": rg}
            if l > 0:
                io["x_src"] = xs[l - 1]
                io["y_g"] = y_g[l - 1]
            if l < depth:
                io["x_dst"] = xs[l]
                io["hn_p"] = hn_p[l]
                io["hn_g"] = hn_g[l]
            ProgA(first=(l == 0), last=(l == depth), shared=self, pfx="A%d_" % l, io=io)
            if l < depth:
                iom = {"rg": rg, "hn_g": hn_g[l], "y_p": y_p[l], "y_g": y_g[l]}
                kind = l % 3
                if kind == 0:
                    ProgHGRN(l, shared=self, pfx="M%d_" % l, io=iom)
                elif kind == 1:
                    ProgLRU(shared=self, pfx="M%d_" % l, io=iom)
                else:
                    ProgMLSTM(shared=self, pfx="M%d_" % l, io=iom)
        mk.finalize()


def fused_inputs(inp, depth=DEPTH, ncores=8, final_g=None):
    cst = consts_host()
    ng_all = inp["norm_g"]
    final_g = inp["final_norm_g"] if final_g is None else final_g
    mixer_wout = [inp["hg_w_out"][0], inp["lru_w_out"][0], inp["ml_w_out"][0], inp["hg_w_out"][1]]
    common = {"ident": cst["ident"], "mask64": cst["mask64"], "memg": fmv(inp["mem_norm_g"])}
    per_half = [dict(), dict()]
    for l in range(depth + 1):
        p = "A%d_" % l
        if l == 0:
            ng = np.stack([fmv(ng_all[0, i]) for i in (0, 0, 0, 1)], axis=1)
        elif l == depth:
            ng = np.stack([fmv(ng_all[l - 1, 2]), fmv(ng_all[l - 1, 3]), fmv(final_g), fmv(final_g)], axis=1)
        else:
            ng = np.stack([fmv(ng_all[l - 1, 2]), fmv(ng_all[l - 1, 3]), fmv(ng_all[l, 0]), fmv(ng_all[l, 1])], axis=1)
        common[p + "ng"] = np.ascontiguousarray(ng)
        if l > 0:
            common[p + "w_mo"] = tile_w(mixer_wout[l - 1])
            common[p + "w_q"] = tile_w(inp["xa_w_q"][l - 1])
            common[p + "w_kv"] = tile_w(inp["xa_w_kv"][l - 1])
            common[p + "w_o"] = tile_w(inp["xa_w_o"][l - 1])
            common[p + "w_gu2"] = tile_w(inp["ffn_w_gu"][l - 1, 1])
            common[p + "w_dn2"] = tile_w(inp["ffn_w_down"][l - 1, 1])
        if l < depth:
            common[p + "w_gu1"] = tile_w(inp["ffn_w_gu"][l, 0])
            common[p + "w_dn1"] = tile_w(inp["ffn_w_down"][l, 0])
            kind, idx = l % 3, l // 3
            for hh in range(2):
                if kind == 0:
                    m = hgrn_inputs(inp, idx, hh)
                elif kind == 1:
                    m = lru_inputs(inp, idx, hh)
                else:
                    m = mlstm_inputs(inp, idx, hh)
                for k, v in m.items():
                    per_half[hh]["M%d_%s" % (l, k)] = v
    in_maps = []
    for c in range(ncores):
        b, r = divmod(c, 2)
        m = dict(common)
        m.update(per_half[r])
        m["A0_xT"] = np.ascontiguousarray(inp["x"][b, r * TA:(r + 1) * TA].T)
        m["memT"] = np.ascontiguousarray(inp["mem"][b].T)
        in_maps.append(m)
    return in_maps


_PROGS = {}


def _prog(key, ctor):
    if key not in _PROGS:
        _PROGS[key] = ctor()
    return _PROGS[key]


def _run(prog, in_maps):
    for m in in_maps:
        for k, (shape, dt) in prog.in_specs.items():
            assert k in m, k
            assert tuple(m[k].shape) == shape, (k, m[k].shape, shape)
    in_maps = [{k: m[k] for k in prog.in_specs} for m in in_maps]
    res = run_bass_kernel_spmd(prog.nc, in_maps, core_ids=list(range(len(in_maps))))
    return res.results


def _run_timed(prog, in_maps):
    in_maps = [{k: m[k] for k in prog.in_specs} for m in in_maps]
    res = run_bass_kernel_spmd(prog.nc, in_maps, core_ids=list(range(len(in_maps))), trace=True)
    print("exec_time_ns", res.exec_time_ns)
    return res.results


def kernel(**inp):
    inp = {k: np.asarray(v) for k, v in inp.items()}
    prog = _prog("fused", Fused)
    res = _run(prog, fused_inputs(inp))
    out = np.empty((BATCH, SEQ, D), np.float32)
    for c in range(8):
        b, tc = divmod(c, 2)
        out[b, tc * TA:(tc + 1) * TA] = res[c]["A%d_outT" % DEPTH].T
    return out


def kernel_unfused(**inp):
    inp = {k: np.asarray(v) for k, v in inp.items()}
    cst = consts_host()
    NC = 8
    x = inp["x"]
    ng_all = inp["norm_g"]
    mixer_wout = [inp["hg_w_out"][0], inp["lru_w_out"][0], inp["ml_w_out"][0], inp["hg_w_out"][1]]

    def ffn_w(layer, f):
        return tile_w(inp["ffn_w_gu"][layer, f]), tile_w(inp["ffn_w_down"][layer, f])

    pa = _prog("A0", lambda: ProgA(first=True, last=False))
    ng = np.ascontiguousarray(np.stack([fmv(ng_all[0, i]) for i in (0, 0, 0, 1)], axis=1))
    wgu, wdn = ffn_w(0, 0)
    in_maps = []
    for c in range(NC):
        b, tc = divmod(c, 2)
        in_maps.append({"xT": np.ascontiguousarray(x[b, tc * TA:(tc + 1) * TA].T), "ng": ng, "w_gu1": wgu, "w_dn1": wdn})
    res = _run(pa, in_maps)
    xT = [r["xT_out"] for r in res]
    hnT = [r["hnT_out"] for r in res]
    del wgu, wdn
    out = None
    for layer in range(DEPTH):
        kind, idx = layer % 3, layer // 3
        in_maps = []
        for c in range(NC):
            b, hh = divmod(c, 2)
            hn_full = np.ascontiguousarray(np.concatenate([hnT[b * 2], hnT[b * 2 + 1]], axis=1))
            if kind == 0:
                m = dict(hgrn_inputs(inp, idx, hh), ident=cst["ident"], mask64=cst["mask64"])
            elif kind == 1:
                m = lru_inputs(inp, idx, hh)
            else:
                m = dict(mlstm_inputs(inp, idx, hh), ident=cst["ident"], mask64=cst["mask64"])
            m["hnT"] = hn_full
            in_maps.append(m)
        if kind == 0:
            pm = _prog("HGRN%d" % layer, lambda: ProgHGRN(layer))
        elif kind == 1:
            pm = _prog("LRU", ProgLRU)
        else:
            pm = _prog("MLSTM", ProgMLSTM)
        res = _run(pm, in_maps)
        yT = [r["yT_out"] for r in res]
        last = layer == DEPTH - 1
        pa = _prog("Alast" if last else "Amid", lambda: ProgA(first=False, last=last))
        if last:
            ng = np.stack([fmv(ng_all[layer, 2]), fmv(ng_all[layer, 3]), fmv(inp["final_norm_g"]), fmv(inp["final_norm_g"])], axis=1)
        else:
            ng = np.stack([fmv(ng_all[layer, 2]), fmv(ng_all[layer, 3]), fmv(ng_all[layer + 1, 0]), fmv(ng_all[layer + 1, 1])], axis=1)
        wgu2, wdn2 = ffn_w(layer, 1)
        common = {"ng": np.ascontiguousarray(ng), "memg": fmv(inp["mem_norm_g"]), "ident": cst["ident"],
                  "w_mo": tile_w(mixer_wout[layer]), "w_q": tile_w(inp["xa_w_q"][layer]), "w_kv": tile_w(inp["xa_w_kv"][layer]),
                  "w_o": tile_w(inp["xa_w_o"][layer]), "w_gu2": wgu2, "w_dn2": wdn2}
        if not last:
            wgu1, wdn1 = ffn_w(layer + 1, 0)
            common.update({"w_gu1": wgu1, "w_dn1": wdn1})
        in_maps = []
        for c in range(NC):
            b, tc = divmod(c, 2)
            y_c = np.ascontiguousarray(np.concatenate([yT[b * 2][:, tc * TA:(tc + 1) * TA], yT[b * 2 + 1][:, tc * TA:(tc + 1) * TA]], axis=0))
            in_maps.append(dict(common, xT=xT[c], yT=y_c, memT=np.ascontiguousarray(inp["mem"][b].T)))
        res = _run(pa, in_maps)
        if last:
            out = np.empty((BATCH, SEQ, D), np.float32)
            for c in range(NC):
                b, tc = divmod(c, 2)
                out[b, tc * TA:(tc + 1) * TA] = res[c]["outT"].T
        else:
            xT = [r["xT_out"] for r in res]
            hnT = [r["hnT_out"] for r in res]
    return out
```

```python
import numpy as np
import ml_dtypes
from contextlib import ExitStack
import concourse.bass as bass
import concourse.mybir as mybir
from concourse.bass_utils import run_bass_kernel_spmd

F32 = mybir.dt.float32
BF16 = mybir.dt.bfloat16
AF = mybir.ActivationFunctionType
ALU = mybir.AluOpType
AX = mybir.AxisListType
NPBF = ml_dtypes.bfloat16

SAME_ENGINE_SYNC = True

D = 2048
DC = 16
SEQ = 2048
BATCH = 4
NMEM = 256
DFF = 5504
FC = 43
EPS = 1e-6
DEPTH = 4
TA = 1024
GROUP = 8


class Buf:
    __slots__ = ("name", "last_w", "readers", "dma_readers", "dsem")

    def __init__(self, name):
        self.name = name
        self.last_w = None
        self.readers = {}
        self.dma_readers = []
        self.dsem = None


class Op:
    __slots__ = ("eng", "fn", "deps", "is_dma", "sem", "val", "milestone", "final_group", "inc")

    def __init__(self, eng, fn):
        self.eng = eng
        self.fn = fn
        self.deps = []
        self.is_dma = False
        self.sem = None
        self.val = 0
        self.milestone = False
        self.final_group = False
        self.inc = 16


class MK:
    ENGS = ("pe", "act", "dve", "pool", "sp")

    def __init__(self, nc):
        self.nc = nc
        self.es = ExitStack()
        self.ops = {e: [] for e in self.ENGS}
        self.esem = {}
        self.dsem_count = {}
        self.nsem = 0
        self.closers = []
        self.out_events = []
        self.nbuf = 0
        self.pes = None
        self.phase_sems = []
        self.free_sems = []
        self.live_async = {}
        self.nphase = 0
        self.rk = {}
        for e in self.ENGS:
            self.esem[e] = self.new_sem("eng_" + e, pooled=False)

    def new_sem(self, name, pooled=True):
        if pooled and self.free_sems:
            s = self.free_sems.pop()
        else:
            self.nsem += 1
            s = self.es.enter_context(self.nc.semaphore("%s_%d" % (name, self.nsem)))
            self.dsem_count[id(s)] = 0
        if pooled:
            self.phase_sems.append(s)
        return s

    def begin_phase(self):
        self.pes = ExitStack()
        self.nphase += 1

    def end_phase(self):
        frontier = []
        for e in self.ENGS:
            for op in reversed(self.ops[e]):
                if not op.is_dma and op.fn is not None:
                    frontier.append(op)
                    break
        frontier.extend(self.live_async.values())
        for e in self.ENGS:
            op = Op(e, None)
            op.deps = list(frontier)
            self.ops[e].append(op)
        self.live_async = {}
        self.free_sems.extend(self.phase_sems)
        self.phase_sems = []
        self.pes.close()
        self.pes = None

    def sbuf(self, name, shape, dtype):
        st = self.pes if self.pes is not None else self.es
        nm = name if self.pes is None else "%s_p%d" % (name, self.nphase)
        return st.enter_context(self.nc.sbuf_tensor(nm, list(shape), dtype))

    def psum(self, name, shape, dtype):
        return self.es.enter_context(self.nc.psum_tensor(name, list(shape), dtype))

    def buf(self, name=None):
        self.nbuf += 1
        return Buf(name or ("b%d" % self.nbuf))

    def bufs(self, name, *dims):
        if len(dims) == 1:
            return [self.buf("%s_%d" % (name, i)) for i in range(dims[0])]
        return [self.bufs("%s_%d" % (name, i), *dims[1:]) for i in range(dims[0])]

    def _record(self, op, reads, writes):
        deps = []
        for b in reads:
            if b.last_w is not None:
                deps.append(b.last_w)
        for b in writes:
            lw = b.last_w
            if lw is not None:
                if not (op.is_dma and lw.is_dma and lw.sem is op.sem and not b.readers and not b.dma_readers):
                    deps.append(lw)
            deps.extend(b.readers.values())
            deps.extend(b.dma_readers)
        op.deps = deps
        for b in reads:
            if op.is_dma:
                b.dma_readers.append(op)
            else:
                b.readers[op.eng] = op
        for b in writes:
            b.last_w = op
            b.readers = {}
            b.dma_readers = []
        self.ops[op.eng].append(op)
        return op

    def op(self, eng, meth, reads=(), writes=(), **kw):
        fn = (lambda e: getattr(e, meth)(**kw))
        return self._record(Op(eng, fn), list(reads), list(writes))

    def _async(self, op, reads, writes, sem, inc):
        if sem is None:
            b = writes[0] if writes else reads[0]
            if b.dsem is None:
                b.dsem = self.new_sem("d")
            sem = b.dsem
        op.is_dma = True
        op.sem = sem
        op.inc = inc
        self.dsem_count[id(sem)] += inc
        op.val = self.dsem_count[id(sem)]
        self.live_async[id(sem)] = op
        return self._record(op, reads, writes)

    def dma(self, eng, out_ap, in_ap, reads=(), writes=(), sem=None, final_group=False, in_fn=None):
        if in_fn is not None:
            op = Op(eng, lambda e: e.dma_start(out=out_ap, in_=in_fn(e)))
        else:
            op = Op(eng, lambda e: e.dma_start(out=out_ap, in_=in_ap))
        op.final_group = final_group
        return self._async(op, list(reads), list(writes), sem, 16)

    def cc_allgather(self, in_ap, out_ap, reads, writes, rg):
        op = Op("pool", lambda e: e.collective_compute("AllGather", ALU.bypass, replica_groups=rg, ins=[in_ap], outs=[out_ap]))
        sem = self.new_sem("cc", pooled=False)
        r = self._async(op, list(reads), list(writes), sem, 1)
        del self.live_async[id(sem)]
        return r

    def rank2(self, e, ename):
        if ename not in self.rk:
            self.rk[ename] = e.snap(e.partition_id() % 2)
        return self.rk[ename]

    def out_dma(self, eng, out_ap, in_ap, reads, sem):
        op = self.dma(eng, out_ap, in_ap, reads=reads, writes=(), sem=sem)
        self.out_events.append(op)
        return op

    def _skip(self, op, d):
        return (not d.is_dma) and d.eng == op.eng and (not op.is_dma) and (op.eng == "pe" or not SAME_ENGINE_SYNC)

    def finalize(self):
        for e in self.ENGS:
            for op in self.ops[e]:
                for d in op.deps:
                    if d.is_dma or self._skip(op, d):
                        continue
                    d.milestone = True
        for e in self.ENGS:
            n = 0
            for op in self.ops[e]:
                if op.is_dma or op.fn is None:
                    continue
                if op.milestone:
                    n += 1
                op.sem = self.esem[e]
                op.val = n
        nwaits = [0]
        with self.nc.Block() as block:
            def run(ename, eng):
                seen = {}
                for op in self.ops[ename]:
                    need = {}
                    for d in op.deps:
                        if self._skip(op, d):
                            continue
                        v = d.val
                        if d.is_dma and d.final_group:
                            v = self.dsem_count[id(d.sem)]
                        k = id(d.sem)
                        if seen.get(k, 0) >= v:
                            continue
                        if k not in need or need[k][1] < v:
                            need[k] = (d.sem, v)
                    for k, (s, v) in need.items():
                        eng.wait_ge(s, v)
                        seen[k] = v
                        nwaits[0] += 1
                    if op.fn is None:
                        continue
                    self.cur_ename = ename
                    ins = op.fn(eng)
                    if op.is_dma:
                        ins.then_inc(op.sem, op.inc)
                    elif op.milestone:
                        ins.then_inc(op.sem, 1)
                if ename == "sp":
                    fin = {}
                    for op in self.out_events:
                        fin[id(op.sem)] = (op.sem, self.dsem_count[id(op.sem)])
                    for s, v in fin.values():
                        eng.wait_ge(s, v)

            @block.sync
            def _(e):
                run("sp", e)

            @block.gpsimd
            def _(e):
                run("pool", e)

            @block.tensor
            def _(e):
                run("pe", e)

            @block.scalar
            def _(e):
                run("act", e)

            @block.vector
            def _(e):
                run("dve", e)
        self.stats = {e: len(self.ops[e]) for e in self.ENGS}
        self.stats["waits"] = nwaits[0]
        self.stats["sems"] = self.nsem
        for c in reversed(self.closers):
            c.close()
        self.es.close()


def tile_w(W):
    K, N = W.shape
    return np.ascontiguousarray(
        W.reshape(K // 128, 128, N // 128, 128).transpose(2, 1, 0, 3).reshape(N // 128, 128, K))


def fmv(v):
    n = v.shape[-1] // 128
    return np.ascontiguousarray(v.reshape(n, 128).T)


def consts_host():
    c = {}
    c["ident"] = np.eye(128, dtype=np.float32).astype(NPBF)
    s = np.arange(64)
    c["mask64"] = (s[:, None] <= s[None, :]).astype(np.float32)
    return c


class Prog:
    NSLOT = 6
    SLOT = 2048
    GLOBAL_INPUTS = ("ident", "mask64", "memg", "memT")

    def __init__(self, shared=None, pfx="", io=None):
        self.pfx = pfx
        self.io = io or {}
        if shared is None:
            self.root = self
            self.nc = bass.Bass("TRN2", target_bir_lowering=False)
            self.mk = MK(self.nc)
            mk = self.mk
            self.in_specs = {}
            self.gl_inputs = {}
            self.ps = [mk.psum("ps%d" % i, [128, 512], F32) for i in range(8)]
            self.out_sem = mk.new_sem("out", pooled=False)
            self.ones = mk.sbuf("ones", [128, 128], F32)
            self.eps_t = mk.sbuf("eps", [128, 1], F32)
            self.sq = mk.sbuf("sq", [128, 2, 512], F32)
            self.rstd = mk.sbuf("rstd", [128, 2, 512], F32)
        else:
            r = shared.root
            self.root = r
            for k in ("nc", "mk", "in_specs", "gl_inputs", "ps", "out_sem", "ones", "eps_t", "sq", "rstd"):
                setattr(self, k, getattr(r, k))

    def begin(self):
        mk = self.mk
        mk.begin_phase()
        self.wring = mk.sbuf("wring", [128, self.NSLOT, self.SLOT], BF16)
        self.wb = mk.bufs("w", self.NSLOT)
        self.psb = mk.bufs("ps", 8)
        self.onesb = mk.buf("ones")
        self.epsb = mk.buf("eps")
        self.sqb = mk.bufs("sq", 2)
        self.rstdb = mk.bufs("rstd", 2)
        self._slot = 0
        self._ps = 0
        self._rr = {}
        mk.op("dve", "memset", writes=[self.onesb], ap=self.ones[:], constant=1.0)
        mk.op("dve", "memset", writes=[self.epsb], ap=self.eps_t[:], constant=EPS)

    def din(self, name, shape, dtype=F32):
        if name in self.GLOBAL_INPUTS:
            if name not in self.gl_inputs:
                self.in_specs[name] = (tuple(shape), dtype)
                self.gl_inputs[name] = self.nc.dram_tensor(name, list(shape), dtype, kind="ExternalInput").ap()
            return self.gl_inputs[name]
        name = self.pfx + name
        self.in_specs[name] = (tuple(shape), dtype)
        return self.nc.dram_tensor(name, list(shape), dtype, kind="ExternalInput").ap()

    def dout(self, name, shape, dtype=F32):
        return self.nc.dram_tensor(self.pfx + name, list(shape), dtype, kind="ExternalOutput").ap()

    def rr(self, key, n):
        i = self._rr.get(key, 0)
        self._rr[key] = (i + 1) % n
        return i

    def next_ps(self):
        i = self._ps
        self._ps = (i + 1) % 5
        return self.ps[i], self.psb[i]

    def next_ps_long(self):
        i = 5 + self.rr("pslong", 3)
        return self.ps[i], self.psb[i]

    def wload(self, src_ap, n):
        i = self._slot
        self._slot = (i + 1) % self.NSLOT
        self.mk.dma("pool", self.wring[:, i, 0:n], src_ap, writes=[self.wb[i]])
        return self.wring[:, i, :], self.wb[i]

    def const(self, name, shape, dtype=F32, dram=None):
        t = self.mk.sbuf("c_" + name, shape, dtype)
        b = self.mk.buf("c_" + name)
        d = dram if dram is not None else self.din(name, shape, dtype)
        self.mk.dma("sp", t[:], d, writes=[b])
        return t, b

    def linear_fm(self, w_d, jlist, nk, src, srcb, ttiles, evac, k0=0):
        mk = self.mk
        for j in jlist:
            w, wbuf = self.wload(w_d[j, :, k0 * 128:(k0 + nk) * 128], nk * 128)
            for ti in ttiles:
                p, pb = self.next_ps()
                rhs0 = src(0, ti)
                ntok = rhs0.shape[-1]
                for k in range(nk):
                    mk.op("pe", "matmul", reads=[wbuf, srcb(k, ti)], writes=[pb],
                          out=p[:, 0:ntok], lhsT=w[:, k * 128:(k + 1) * 128], rhs=src(k, ti),
                          start=(k == 0), stop=(k == nk - 1))
                evac(j, ti, p, pb)

    def sumsq_bcast(self, srcs, ntok):
        mk = self.mk
        p, pb = self.next_ps()
        n = len(srcs)
        for c, (ap, b) in enumerate(srcs):
            i = self.rr("sq", 2)
            mk.op("act", "activation", reads=[b], writes=[self.sqb[i]],
                  out=self.sq[:, i, 0:ntok], in_=ap, func=AF.Square)
            mk.op("pe", "matmul", reads=[self.onesb, self.sqb[i]], writes=[pb],
                  out=p[:, 0:ntok], lhsT=self.ones[:], rhs=self.sq[:, i, 0:ntok], start=(c == 0), stop=(c == n - 1))
        return p, pb

    def rstd_from(self, p, pb, ntok, n_feat):
        mk = self.mk
        i = self.rr("rstd", 2)
        if getattr(self, "rstd_lnexp", False):
            mk.op("act", "activation", reads=[pb, self.epsb], writes=[self.rstdb[i]],
                  out=self.rstd[:, i, 0:ntok], in_=p[:, 0:ntok], func=AF.Ln, bias=self.eps_t[:, 0:1], scale=1.0 / n_feat)
            mk.op("act", "activation", reads=[self.rstdb[i]], writes=[self.rstdb[i]],
                  out=self.rstd[:, i, 0:ntok], in_=self.rstd[:, i, 0:ntok], func=AF.Exp, scale=-0.5)
            return self.rstd[:, i, 0:ntok], self.rstdb[i]
        mk.op("act", "activation", reads=[pb, self.epsb], writes=[self.rstdb[i]],
              out=self.rstd[:, i, 0:ntok], in_=p[:, 0:ntok], func=AF.Sqrt, bias=self.eps_t[:, 0:1], scale=1.0 / n_feat)
        mk.op("dve", "reciprocal", reads=[self.rstdb[i]], writes=[self.rstdb[i]],
              out=self.rstd[:, i, 0:ntok], in_=self.rstd[:, i, 0:ntok])
        return self.rstd[:, i, 0:ntok], self.rstdb[i]

    def make_eps(self):
        pass

    def rmsnorm(self, src, srcb, dst, dstb, g_ap, nchunk, ntok):
        mk = self.mk
        p, pb = self.sumsq_bcast([(src(c), srcb(c)) for c in range(nchunk)], ntok)
        r, rb = self.rstd_from(p, pb, ntok, nchunk * 128)
        for c in range(nchunk):
            mk.op("dve", "scalar_tensor_tensor", reads=[srcb(c), self.gb, rb], writes=[dstb(c)],
                  out=dst(c), in0=src(c), scalar=g_ap(c), in1=r, op0=ALU.mult, op1=ALU.mult)

    def finish(self):
        self.mk.end_phase()
        if self.root is self:
            self.mk.finalize()
        return self.nc


class ProgA(Prog):
    def __init__(self, first, last, shared=None, pfx="", io=None):
        super().__init__(shared, pfx, io)
        self.begin()
        self.first, self.last = first, last
        mk = self.mk
        io = self.io
        T = TA
        NTH = T // 512
        if "x_src" in io:
            xT_d, xsrcb = io["x_src"]
            xT_d = xT_d.rearrange("(c p) t -> p c t", p=128)
            xsrc_reads = [xsrcb]
        else:
            xT_d = self.din("xT", [D, T]).rearrange("(c p) t -> p c t", p=128)
            xsrc_reads = []
        ng_t, self.gb = self.const("ng", [128, 4, DC])
        x = mk.sbuf("x", [128, DC, T], F32)
        xb = mk.bufs("x", DC, NTH)
        xn = mk.sbuf("xn", [128, DC, T], BF16)
        xnb = mk.bufs("xn", DC, NTH)
        bigB = mk.sbuf("bigB", [128, DC, 512], BF16)
        bigBb = mk.bufs("bigB", DC)
        bigC = mk.sbuf("bigC", [128, DC, 512], BF16)
        bigCb = mk.bufs("bigC", DC)
        sg = mk.sbuf("sg", [128, 2, 512], F32)
        sgb = mk.bufs("sg", 2)
        self.x, self.xb, self.xn, self.xnb = x, xb, xn, xnb

        def load_x():
            for c in range(DC):
                for th in range(NTH):
                    mk.dma("sp", x[:, c, th * 512:(th + 1) * 512], xT_d[:, c, th * 512:(th + 1) * 512], reads=xsrc_reads, writes=[xb[c][th]])
        if first:
            load_x()

        def norm_x(gi):
            for th in range(NTH):
                tsl = slice(th * 512, (th + 1) * 512)
                self.rmsnorm(lambda c: x[:, c, tsl], lambda c: xb[c][th], lambda c: xn[:, c, tsl], lambda c: xnb[c][th],
                             lambda c: ng_t[:, gi, c:c + 1], DC, 512)

        def ffn(wgu_d, wdn_d):
            groups = [(g0, min(g0 + GROUP, FC)) for g0 in range(0, FC, GROUP)]
            for (g0, g1) in groups:
                for c in range(g0, g1):
                    cl = c - g0
                    wg, wgb = self.wload(wgu_d[c, :, :], DC * 128)
                    wu, wub = self.wload(wgu_d[FC + c, :, :], DC * 128)
                    for th in range(NTH):
                        tsl = slice(th * 512, (th + 1) * 512)
                        pg, pgb = self.next_ps()
                        for k in range(DC):
                            mk.op("pe", "matmul", reads=[wgb, xnb[k][th]], writes=[pgb], out=pg[:], lhsT=wg[:, k * 128:(k + 1) * 128],
                                  rhs=xn[:, k, tsl], start=(k == 0), stop=(k == DC - 1))
                        pu, pub = self.next_ps()
                        for k in range(DC):
                            mk.op("pe", "matmul", reads=[wub, xnb[k][th]], writes=[pub], out=pu[:], lhsT=wu[:, k * 128:(k + 1) * 128],
                                  rhs=xn[:, k, tsl], start=(k == 0), stop=(k == DC - 1))
                        i = self.rr("sg", 2)
                        mk.op("act", "activation", reads=[pgb], writes=[sgb[i]], out=sg[:, i, :], in_=pg[:], func=AF.Silu)
                        mk.op("dve", "tensor_tensor", reads=[sgb[i], pub], writes=[bigCb[cl * 2 + th]],
                              out=bigC[:, cl * 2 + th, :], in0=sg[:, i, :], in1=pu[:], op=ALU.mult)
                nk = g1 - g0
                for j in range(DC):
                    wd, wdb = self.wload(wdn_d[j, :, g0 * 128:g1 * 128], nk * 128)
                    for th in range(NTH):
                        tsl = slice(th * 512, (th + 1) * 512)
                        po, pob = self.next_ps()
                        for kk in range(nk):
                            mk.op("pe", "matmul", reads=[wdb, bigCb[kk * 2 + th]], writes=[pob], out=po[:],
                                  lhsT=wd[:, kk * 128:(kk + 1) * 128], rhs=bigC[:, kk * 2 + th, :], start=(kk == 0), stop=(kk == nk - 1))
                        mk.op("dve", "scalar_tensor_tensor", reads=[pob, xb[j][th]], writes=[xb[j][th]],
                              out=x[:, j, tsl], in0=po[:], scalar=0.5, in1=x[:, j, tsl], op0=ALU.mult, op1=ALU.add)

        def add_into_x(th):
            tsl = slice(th * 512, (th + 1) * 512)

            def evac(j, ti, p, pb):
                mk.op("dve", "tensor_tensor", reads=[pb, xb[j][th]], writes=[xb[j][th]],
                      out=x[:, j, tsl], in0=p[:], in1=x[:, j, tsl], op=ALU.add)
            return evac

        if not first:
            if "y_g" not in io:
                yT_d = self.din("yT", [D, T], BF16).rearrange("(c p) t -> p c t", p=128)
            memT_d = self.din("memT", [D, NMEM]).rearrange("(c p) t -> p c t", p=128)
            mg_t, mgb = self.const("memg", [128, DC])
            ident, identb = self.const("ident", [128, 128], BF16)
            wmo_d = self.din("w_mo", [DC, 128, D])
            wq_d = self.din("w_q", [DC, 128, D])
            wkv_d = self.din("w_kv", [2 * DC, 128, D])
            wo_d = self.din("w_o", [DC, 128, D])
            wgu2_d = self.din("w_gu2", [2 * FC, 128, D])
            wdn2_d = self.din("w_dn2", [DC, 128, DFF])
            kT = mk.sbuf("kT", [128, DC, NMEM], BF16)
            kTb = mk.bufs("kT", DC)
            vtm = mk.sbuf("vtm", [128, 2, D], BF16)
            vtmb = mk.bufs("vtm", 2, 4)
            mst = mk.sbuf("mst", [128, 2, NMEM], F32)
            mstb = mk.bufs("mst", 2)
            pT = mk.sbuf("pT", [128, 2, 2, 512], BF16)
            pTb = mk.bufs("pT", 2)
            sm = mk.sbuf("sm", [128, 4, NMEM], F32)
            smb = mk.bufs("sm", 4)
            pbf = mk.sbuf("pbf", [128, 4, NMEM], BF16)
            pbfb = mk.bufs("pbf", 4)
            st1 = mk.sbuf("st1", [128, 16], F32)
            st1b = mk.bufs("st1", 4)
            p, pb = self.next_ps()
            for c in range(DC):
                i = self.rr("mst", 2)
                mk.dma("sp", mst[:, i, :], memT_d[:, c, :], writes=[mstb[i]])
                k = self.rr("sq", 2)
                mk.op("act", "activation", reads=[mstb[i]], writes=[self.sqb[k]], out=self.sq[:, k, 0:NMEM], in_=mst[:, i, :], func=AF.Square)
                mk.op("pe", "matmul", reads=[self.onesb, self.sqb[k]], writes=[pb], out=p[:, 0:NMEM], lhsT=self.ones[:],
                      rhs=self.sq[:, k, 0:NMEM], start=(c == 0), stop=(c == DC - 1))
            r, rb = self.rstd_from(p, pb, NMEM, D)
            for c in range(DC):
                i = self.rr("mst", 2)
                mk.dma("sp", mst[:, i, :], memT_d[:, c, :], writes=[mstb[i]])
                mk.op("dve", "scalar_tensor_tensor", reads=[mstb[i], mgb, rb], writes=[bigBb[c]], out=bigB[:, c, 0:NMEM],
                      in0=mst[:, i, :], scalar=mg_t[:, c:c + 1], in1=r, op0=ALU.mult, op1=ALU.mult)

            load_x()

            def k_item(j):
                w, wbuf = self.wload(wkv_d[j, :, :], D)
                p, pb = self.next_ps()
                for k in range(DC):
                    mk.op("pe", "matmul", reads=[wbuf, bigBb[k]], writes=[pb], out=p[:, 0:NMEM], lhsT=w[:, k * 128:(k + 1) * 128],
                          rhs=bigB[:, k, 0:NMEM], start=(k == 0), stop=(k == DC - 1))
                mk.op("act", "activation", reads=[pb], writes=[kTb[j]], out=kT[:, j, :], in_=p[:, 0:NMEM], func=AF.Copy)

            def v_item(jg):
                ws = [self.wload(wkv_d[DC + jg * 4 + jj, :, :], D) for jj in range(4)]
                for mc in range(2):
                    p, pb = self.next_ps()
                    for jj in range(4):
                        w, wbuf = ws[jj]
                        for k in range(DC):
                            mk.op("pe", "matmul", reads=[wbuf, bigBb[k]], writes=[pb], out=p[:, jj * 128:(jj + 1) * 128],
                                  lhsT=bigB[:, k, mc * 128:(mc + 1) * 128], rhs=w[:, k * 128:(k + 1) * 128], start=(k == 0), stop=(k == DC - 1))
                    mk.op("act", "activation", reads=[pb], writes=[vtmb[mc][jg]], out=vtm[:, mc, jg * 512:(jg + 1) * 512], in_=p[:], func=AF.Copy)

            scale = 512.0 ** -0.5
            THS = list(range(NTH))

            def tsl_(th):
                return slice(th * 512, (th + 1) * 512)
            for th in THS:
                tsl = tsl_(th)
                for c in range(DC):
                    if "y_g" in io:
                        r_, rem = divmod(c, 8)
                        pc, i_ = divmod(rem, 4)
                        yg_ap, ygb = io["y_g"][pc]
                        row0 = r_ * 512 + i_ * 128

                        def in_fn(e, yg_ap=yg_ap, row0=row0, th=th):
                            rk = mk.rank2(e, "sp")
                            return yg_ap[row0:row0 + 128, bass.ds(rk * TA + th * 512, 512)]
                        mk.dma("sp", xn[:, c, tsl], None, reads=[ygb], writes=[xnb[c][th]], in_fn=in_fn)
                    else:
                        mk.dma("sp", xn[:, c, tsl], yT_d[:, c, tsl], writes=[xnb[c][th]])

            def evac_add(j, th, p, pb):
                mk.op("dve", "tensor_tensor", reads=[pb, xb[j][th]], writes=[xb[j][th]],
                      out=x[:, j, tsl_(th)], in0=p[:], in1=x[:, j, tsl_(th)], op=ALU.add)
            kv_items = [lambda jg=jg: v_item(jg) for jg in range(2)] + [lambda j=j: k_item(j) for j in range(DC)] + \
                       [lambda jg=jg: v_item(jg) for jg in range(2, 4)]
            for it in kv_items[:6]:
                it()
            rest = kv_items[6:]
            for j in range(DC):
                self.linear_fm(wmo_d, [j], DC, lambda k, th: xn[:, k, tsl_(th)], lambda k, th: xnb[k][th], THS, evac_add)
                if rest:
                    rest.pop(0)()
            for it in rest:
                it()
            for th in THS:
                tsl = tsl_(th)
                self.rmsnorm(lambda c: x[:, c, tsl], lambda c: xb[c][th], lambda c: xn[:, c, tsl], lambda c: xnb[c][th],
                             lambda c: ng_t[:, 0, c:c + 1], DC, 512)
            qbuf = [(bigB, bigBb), (bigC, bigCb)]

            def evac_q(j, th, p, pb):
                qd, qdb = qbuf[th]
                mk.op("act", "activation", reads=[pb], writes=[qdb[j]], out=qd[:, j, :], in_=p[:], func=AF.Copy)
            self.linear_fm(wq_d, range(DC), DC, lambda k, th: xn[:, k, tsl_(th)], lambda k, th: xnb[k][th], THS, evac_q)
            for th in THS:
                tsl = tsl_(th)
                qd, qdb = qbuf[th]
                for hd in range(4):
                    pi = self.rr("pT", 2)
                    def chain(tt, hd=hd, qd=qd, qdb=qdb, pi=pi):
                        ps_, psb_ = self.next_ps()
                        for kc in range(4):
                            ch = hd * 4 + kc
                            mk.op("pe", "matmul", reads=[qdb[ch], kTb[ch]], writes=[psb_], out=ps_[:, 0:NMEM],
                                  lhsT=qd[:, ch, tt * 128:(tt + 1) * 128], rhs=kT[:, ch, :], start=(kc == 0), stop=(kc == 3))
                        si = tt
                        mi = tt
                        yield
                        mk.op("dve", "reduce_max", reads=[psb_], writes=[st1b[si]], out=st1[:, si * 4:si * 4 + 1], in_=ps_[:, 0:NMEM], axis=AX.X)
                        yield
                        mk.op("dve", "tensor_scalar", reads=[st1b[si]], writes=[st1b[si]], out=st1[:, si * 4 + 1:si * 4 + 2],
                              in0=st1[:, si * 4:si * 4 + 1], scalar1=-scale, scalar2=None, op0=ALU.mult)
                        yield
                        mk.op("act", "activation", reads=[psb_, st1b[si]], writes=[smb[mi], st1b[si]], out=sm[:, mi, :], in_=ps_[:, 0:NMEM],
                              func=AF.Exp, bias=st1[:, si * 4 + 1:si * 4 + 2], scale=scale, accum_out=st1[:, si * 4 + 2:si * 4 + 3])
                        yield
                        mk.op("dve", "reciprocal", reads=[st1b[si]], writes=[st1b[si]], out=st1[:, si * 4 + 3:si * 4 + 4],
                              in_=st1[:, si * 4 + 2:si * 4 + 3])
                        yield
                        mk.op("dve", "tensor_scalar", reads=[smb[mi], st1b[si]], writes=[pbfb[mi]], out=pbf[:, mi, :], in0=sm[:, mi, :],
                              scalar1=st1[:, si * 4 + 3:si * 4 + 4], scalar2=None, op0=ALU.mult)
                        yield
                        pt_, ptb_ = self.next_ps()
                        ptv = pt_.bitcast(BF16)
                        for mc in range(2):
                            mk.op("pe", "transpose", reads=[pbfb[mi], identb], writes=[ptb_], out=ptv[:, mc * 128:(mc + 1) * 128],
                                  in_=pbf[:, mi, mc * 128:(mc + 1) * 128], identity=ident[:])
                        yield
                        for mc in range(2):
                            mk.op("act", "activation", reads=[ptb_], writes=[pTb[pi]], out=pT[:, pi, mc, tt * 128:(tt + 1) * 128],
                                  in_=ptv[:, mc * 128:(mc + 1) * 128], func=AF.Copy)
                    gens = [chain(tt) for tt in range(4)]
                    while gens:
                        for g in list(gens):
                            try:
                                next(g)
                            except StopIteration:
                                gens.remove(g)
                    for dc in range(4):
                        ch = hd * 4 + dc
                        po, pob = self.next_ps()
                        for mc in range(2):
                            mk.op("pe", "matmul", reads=[vtmb[mc][hd], pTb[pi]], writes=[pob], out=po[:],
                                  lhsT=vtm[:, mc, ch * 128:(ch + 1) * 128], rhs=pT[:, pi, mc, :], start=(mc == 0), stop=(mc == 1))
                        mk.op("act", "activation", reads=[pob], writes=[xnb[ch][th]], out=xn[:, ch, tsl], in_=po[:], func=AF.Copy)
            self.linear_fm(wo_d, range(DC), DC, lambda k, th: xn[:, k, tsl_(th)], lambda k, th: xnb[k][th], THS, evac_add)
            norm_x(1)
            ffn(wgu2_d, wdn2_d)

        if not last:
            wgu1_d = self.din("w_gu1", [2 * FC, 128, D])
            wdn1_d = self.din("w_dn1", [DC, 128, DFF])
            norm_x(2)
            ffn(wgu1_d, wdn1_d)
            norm_x(3)
            if "x_dst" in io:
                for pc in range(2):
                    (hp, hpb), (hg, hgb) = io["hn_p"][pc], io["hn_g"][pc]
                    hpv = hp.rearrange("(c p) t -> p c t", p=128)
                    for c in range(DC):
                        mk.dma("sp", hpv[:, c, :], xn[:, c, pc * 512:(pc + 1) * 512], reads=[xnb[c][pc]], writes=[hpb])
                    mk.cc_allgather(hp, hg, reads=[hpb], writes=[hgb], rg=io["rg"])
                xd, xdb = io["x_dst"]
                xd = xd.rearrange("(c p) t -> p c t", p=128)
                for c in range(DC):
                    mk.dma("sp", xd[:, c, :], x[:, c, :], reads=xb[c], writes=[xdb])
            else:
                xo_d = self.dout("xT_out", [D, T]).rearrange("(c p) t -> p c t", p=128)
                hn_d = self.dout("hnT_out", [D, T], BF16).rearrange("(c p) t -> p c t", p=128)
                for c in range(DC):
                    mk.out_dma("sp", xo_d[:, c, :], x[:, c, :], reads=xb[c], sem=self.out_sem)
                    mk.out_dma("sp", hn_d[:, c, :], xn[:, c, :], reads=xnb[c], sem=self.out_sem)
        else:
            fo_d = self.dout("outT", [D, T]).rearrange("(c p) t -> p c t", p=128)
            for th in range(NTH):
                tsl = slice(th * 512, (th + 1) * 512)
                p, pb = self.sumsq_bcast([(x[:, c, tsl], xb[c][th]) for c in range(DC)], 512)
                r, rb = self.rstd_from(p, pb, 512, D)
                for c in range(DC):
                    mk.op("dve", "scalar_tensor_tensor", reads=[xb[c][th], self.gb, rb], writes=[xb[c][th]],
                          out=x[:, c, tsl], in0=x[:, c, tsl], scalar=ng_t[:, 2, c:c + 1], in1=r, op0=ALU.mult, op1=ALU.mult)
            for c in range(DC):
                mk.out_dma("sp", fo_d[:, c, :], x[:, c, :], reads=xb[c], sem=self.out_sem)
        self.finish()


def bc_chunks(t, col, nch, clen, pstep=None):
    base = t[:, 0:1]
    ps = base.ap[0][0]
    return bass.AP(t, base.offset + col, [[ps, 128], [clen, nch], [0, clen]])


def v3(ap2, nch, clen):
    return ap2.rearrange("p (c i) -> p c i", i=clen)


class ProgM(Prog):
    NSLOT = 4

    def __init__(self, shared=None, pfx="", io=None):
        super().__init__(shared, pfx, io)
        self.begin()
        mk = self.mk
        io = self.io
        self.hn = mk.sbuf("hn", [128, DC, SEQ], BF16)
        self.hnb = mk.bufs("hn", DC, 4)
        if "hn_g" in io:
            for ti in range(4):
                r_, pc = divmod(ti, 2)
                hg, hgb = io["hn_g"][pc]
                hgv = hg[r_ * D:(r_ + 1) * D, :].rearrange("(c p) t -> p c t", p=128)
                for c in range(DC):
                    mk.dma("sp", self.hn[:, c, ti * 512:(ti + 1) * 512], hgv[:, c, :], reads=[hgb], writes=[self.hnb[c][ti]])
            self.y_cnt = [0, 0]
        else:
            hnT_d = self.din("hnT", [D, SEQ], BF16).rearrange("(c p) t -> p c t", p=128)
            for c in range(DC):
                for ti in range(4):
                    mk.dma("sp", self.hn[:, c, ti * 512:(ti + 1) * 512], hnT_d[:, c, ti * 512:(ti + 1) * 512], writes=[self.hnb[c][ti]])
            self.y_d = self.dout("yT_out", [D // 2, SEQ], BF16).rearrange("(c p) t -> p c t", p=128)
        self.ybuf = mk.sbuf("ybuf", [128, 2, 512], BF16)
        self.ybufb = mk.bufs("ybuf", 2)

    def hsrc(self):
        return (lambda k, ti: self.hn[:, k, ti * 512:(ti + 1) * 512]), (lambda k, ti: self.hnb[k][ti])

    def store_y(self, ch, ti, i):
        io = self.io
        if "y_p" in io:
            pc, cl = divmod(ch, 4)
            (yp, ypb), (yg, ygb) = io["y_p"][pc], io["y_g"][pc]
            self.mk.dma("sp", yp[cl * 128:(cl + 1) * 128, ti * 512:(ti + 1) * 512], self.ybuf[:, i, :], reads=[self.ybufb[i]], writes=[ypb])
            self.y_cnt[pc] += 1
            if self.y_cnt[pc] == 16:
                self.mk.cc_allgather(yp, yg, reads=[ypb], writes=[ygb], rg=io["rg"])
        else:
            self.mk.out_dma("sp", self.y_d[:, ch, ti * 512:(ti + 1) * 512], self.ybuf[:, i, :], reads=[self.ybufb[i]], sem=self.out_sem)


class ProgHGRN(ProgM):
    NSLOT = 8

    def __init__(self, layer, shared=None, pfx="", io=None):
        super().__init__(shared, pfx, io)
        mk = self.mk
        NH = 8
        win_d = self.din("w_in", [4 * NH, 128, D])
        lbp, lbpb = self.const("lbp", [128, 4, NH])
        gn, gnb = self.const("gn", [128, 1])
        self.gb = gnb
        ident, identb = self.const("ident", [128, 128], BF16)
        mask, maskb = self.const("mask64", [64, 64])
        sm = mk.sbuf("lbs", [128, 8, NH], F32)
        smb = mk.buf("lbs")
        R_ = [lbpb, smb]

        def tt(o, a, b, op):
            mk.op("dve", "tensor_tensor", reads=R_, writes=[smb], out=o, in0=a, in1=b, op=op)
        tt(sm[:, 0, :], lbp[:, 0, :], lbp[:, 1, :], ALU.max)
        tt(sm[:, 0, :], sm[:, 0, :], lbp[:, 2, :], ALU.max)
        tt(sm[:, 0, :], sm[:, 0, :], lbp[:, 3, :], ALU.max)
        for i in range(4):
            tt(sm[:, 1 + i, :], lbp[:, i, :], sm[:, 0, :], ALU.subtract)
            mk.op("act", "activation", reads=[smb], writes=[smb], out=sm[:, 1 + i, :], in_=sm[:, 1 + i, :], func=AF.Exp)
        tt(sm[:, 5, :], sm[:, 1, :], sm[:, 2, :], ALU.add)
        tt(sm[:, 5, :], sm[:, 5, :], sm[:, 3, :], ALU.add)
        tt(sm[:, 5, :], sm[:, 5, :], sm[:, 4, :], ALU.add)
        mk.op("dve", "reciprocal", reads=[smb], writes=[smb], out=sm[:, 5, :], in_=sm[:, 5, :])
        mk.op("dve", "memset", reads=[smb], writes=[smb], ap=sm[:, 6, :], constant=0.0)
        for i in range(1, layer + 1):
            tt(sm[:, 6, :], sm[:, 6, :], sm[:, 1 + i, :], ALU.add)
        tt(sm[:, 6, :], sm[:, 6, :], sm[:, 5, :], ALU.mult)
        mk.op("dve", "tensor_scalar", reads=[smb], writes=[smb], out=sm[:, 7, :], in0=sm[:, 6, :], scalar1=-1.0, scalar2=1.0,
              op0=ALU.mult, op1=ALU.add)
        mk.op("dve", "tensor_scalar", reads=[smb], writes=[smb], out=sm[:, 0, :], in0=sm[:, 7, :], scalar1=-1.0, scalar2=None,
              op0=ALU.mult)
        LB, OML, NOML = 6, 7, 0

        def tbuf(name, dt=F32):
            return mk.sbuf(name, [128, 2, 512], dt), mk.bufs(name, 2)

        def pbuf(name, dt=BF16):
            return mk.sbuf(name, [128, SEQ], dt), mk.bufs(name, 4)
        qs, qsb = tbuf("qs")
        sg_, sgb_ = tbuf("sgm")
        kk, kkb = tbuf("kk")
        bb, bbb = tbuf("bb")
        t1, t1b = tbuf("t1")
        ex, exb = tbuf("ex")
        oo, oob = tbuf("oo")
        vf, vfb = tbuf("vf", BF16)
        qt, qtb = pbuf("qt")
        kt, ktb = pbuf("kt")
        qh, qhb = pbuf("qh")
        kh, khb = pbuf("kh")
        gs, gsb = pbuf("gs")
        vtm = mk.sbuf("vtm", [64, 32, 128], BF16)
        vtmb = mk.bufs("vtm", 4)
        ktm = mk.sbuf("ktm", [64, 32, 128], BF16)
        ktmb = mk.bufs("ktm", 4)
        scs = mk.sbuf("scs", [64, 2, 8, 64], BF16)
        scsb = mk.bufs("scs", 2)
        dec = mk.sbuf("dec", [128, 32], F32)
        decb = mk.bufs("dec", 4)
        S = mk.sbuf("S", [128, 2, 128], F32)
        Sb = mk.bufs("S", 2)
        Sall = mk.sbuf("Sall", [128, 33, 128], BF16)
        Sallb = mk.bufs("Sall", 4)
        S0b = mk.buf("Sall0")
        mk.op("dve", "memset", writes=[S0b], ap=Sall[:, 0, :], constant=0.0)
        onesr = mk.sbuf("onesr", [128, 64], F32)
        onesrb = mk.buf("onesr")
        mk.op("dve", "memset", writes=[onesrb], ap=onesr[:], constant=1.0)
        tmp = mk.sbuf("tmpy", [128, 2, 512], F32)
        tmpb = mk.bufs("tmpy", 2)
        wt = {}
        st1 = {}

        def sl(ti):
            return slice(ti * 512, (ti + 1) * 512)

        def stage1(hd, ti):
            if ti == 0:
                wt[hd] = [self.wload(win_d[kind * NH + hd, :, :], D) for kind in range(4)]
            i = self.rr("tl", 2)
            s_ = sl(ti)

            def proj(kind, evac):
                w, wbuf = wt[hd][kind]
                p, pb = self.next_ps()
                for k in range(DC):
                    mk.op("pe", "matmul", reads=[wbuf, self.hnb[k][ti]], writes=[pb], out=p[:], lhsT=w[:, k * 128:(k + 1) * 128],
                          rhs=self.hn[:, k, s_], start=(k == 0), stop=(k == DC - 1))
                evac(p, pb)
            proj(1, lambda p, pb: mk.op("act", "activation", reads=[pb], writes=[sgb_[i]], out=sg_[:, i, :], in_=p[:], func=AF.Sigmoid))
            yield
            proj(0, lambda p, pb: mk.op("act", "activation", reads=[pb], writes=[qsb[i]], out=qs[:, i, :], in_=p[:], func=AF.Silu))
            yield
            mk.op("dve", "tensor_scalar", reads=[sgb_[i], smb], writes=[kkb[i]], out=kk[:, i, :], in0=sg_[:, i, :],
                  scalar1=sm[:, NOML, hd:hd + 1], scalar2=sm[:, OML, hd:hd + 1], op0=ALU.mult, op1=ALU.add)
            yield
            mk.op("dve", "tensor_scalar", reads=[sgb_[i], smb], writes=[sgb_[i]], out=sg_[:, i, :], in0=sg_[:, i, :],
                  scalar1=sm[:, OML, hd:hd + 1], scalar2=sm[:, LB, hd:hd + 1], op0=ALU.mult, op1=ALU.add)
            yield
            mk.op("dve", "tensor_scalar", reads=[sgb_[i]], writes=[sgb_[i]], out=sg_[:, i, :], in0=sg_[:, i, :],
                  scalar1=1e-12, scalar2=None, op0=ALU.max)
            yield
            mk.op("act", "activation", reads=[sgb_[i]], writes=[sgb_[i]], out=sg_[:, i, :], in_=sg_[:, i, :], func=AF.Ln)
            yield
            proj(2, lambda p, pb: mk.op("act", "activation", reads=[pb], writes=[vfb[i]], out=vf[:, i, :], in_=p[:], func=AF.Copy))
            yield
            for c in range(8):
                cs = slice(c * 64, (c + 1) * 64)
                mk.op("dve", "tensor_tensor_scan", reads=[sgb_[i], onesrb], writes=[bbb[i]], out=bb[:, i, cs], data0=onesr[:, :],
                      data1=sg_[:, i, cs], initial=0.0, op0=ALU.mult, op1=ALU.add)
            yield
            pstep = bb[:, 0, 0:1].ap[0][0]
            b3 = v3(bb[:, i, :], 8, 64)
            bref = bass.AP(bb, bb[:, i, 31:32].offset, [[pstep, 128], [64, 8], [0, 64]])
            blast = bass.AP(bb, bb[:, i, 63:64].offset, [[pstep, 128], [64, 8], [0, 64]])
            bl2 = bass.AP(bb, bb[:, i, 63:64].offset, [[pstep, 128], [64, 8]])
            mk.op("dve", "tensor_tensor", reads=[bbb[i]], writes=[t1b[i]], out=v3(t1[:, i, :], 8, 64), in0=b3, in1=bref, op=ALU.subtract)
            yield
            mk.op("act", "activation", reads=[t1b[i]], writes=[exb[i]], out=ex[:, i, :], in_=t1[:, i, :], func=AF.Exp)
            yield
            mk.op("dve", "tensor_tensor", reads=[exb[i], qsb[i]], writes=[qtb[ti]], out=qt[:, s_], in0=ex[:, i, :], in1=qs[:, i, :], op=ALU.mult)
            yield
            mk.op("act", "activation", reads=[t1b[i]], writes=[exb[i]], out=ex[:, i, :], in_=t1[:, i, :], func=AF.Exp, scale=-1.0)
            yield
            mk.op("dve", "tensor_tensor", reads=[exb[i], kkb[i]], writes=[ktb[ti]], out=kt[:, s_], in0=ex[:, i, :], in1=kk[:, i, :], op=ALU.mult)
            yield
            proj(3, lambda p, pb: mk.op("act", "activation", reads=[pb], writes=[gsb[ti]], out=gs[:, s_], in_=p[:], func=AF.Silu))
            yield
            mk.op("act", "activation", reads=[bbb[i]], writes=[exb[i]], out=ex[:, i, :], in_=bb[:, i, :], func=AF.Exp)
            yield
            mk.op("dve", "tensor_tensor", reads=[exb[i], qsb[i]], writes=[qhb[ti]], out=qh[:, s_], in0=ex[:, i, :], in1=qs[:, i, :], op=ALU.mult)
            yield
            mk.op("dve", "tensor_tensor", reads=[bbb[i]], writes=[t1b[i]], out=v3(t1[:, i, :], 8, 64), in0=blast, in1=b3, op=ALU.subtract)
            yield
            mk.op("act", "activation", reads=[t1b[i]], writes=[exb[i]], out=ex[:, i, :], in_=t1[:, i, :], func=AF.Exp)
            yield
            mk.op("dve", "tensor_tensor", reads=[exb[i], kkb[i]], writes=[khb[ti]], out=kh[:, s_], in0=ex[:, i, :], in1=kk[:, i, :], op=ALU.mult)
            yield
            mk.op("act", "activation", reads=[bbb[i]], writes=[decb[ti]], out=dec[:, ti * 8:(ti + 1) * 8], in_=bl2, func=AF.Exp)
            yield
            st1[(hd, ti)] = i

        st2 = {}

        def stage2a(hd, ti):
            s_ = sl(ti)
            if ti == 0:
                mk.op("dve", "memset", writes=[Sb[1]], ap=S[:, 1, :], constant=0.0)
            i1 = st1.pop((hd, ti))
            for (srcf, srcfb, dstt, dsttb) in ((kh[:, s_], khb[ti], ktm, ktmb), (vf[:, i1, :], vfb[i1], vtm, vtmb)):
                p, pb = self.next_ps()
                pv = p.bitcast(BF16)
                for c in range(8):
                    mk.op("pe", "transpose", reads=[srcfb, identb], writes=[pb], out=pv[0:64, c * 128:(c + 1) * 128],
                          in_=srcf[:, c * 64:(c + 1) * 64], identity=ident[:])
                mk.op("act", "activation", reads=[pb], writes=[dsttb[ti]], out=dstt[:, ti * 8:(ti + 1) * 8, :],
                      in_=pv[0:64, :].rearrange("p (c i) -> p c i", i=128), func=AF.Copy)

            p, pb = self.next_ps()
            for c in range(8):
                cs = slice(ti * 512 + c * 64, ti * 512 + (c + 1) * 64)
                mk.op("pe", "matmul", reads=[ktb[ti], qtb[ti]], writes=[pb], out=p[0:64, c * 64:(c + 1) * 64], lhsT=kt[:, cs], rhs=qt[:, cs],
                      start=True, stop=True)
            si = self.rr("scs", 2)
            mbc = bass.AP(mask, mask[:, 0:1].offset, [[mask[:, 0:1].ap[0][0], 64], [0, 8], [1, 64]])
            mk.op("dve", "tensor_tensor", reads=[pb, maskb], writes=[scsb[si]], out=scs[:, si, :, :], in0=v3(p[0:64, :], 8, 64), in1=mbc, op=ALU.mult)
            pst = [self.next_ps(), self.next_ps()]
            for c in range(8):
                cg = ti * 8 + c
                pp, ppb = pst[c // 4]
                mk.op("pe", "matmul", reads=[ktmb[ti], vtmb[ti]], writes=[ppb], out=pp[:, (c % 4) * 128:(c % 4 + 1) * 128], lhsT=ktm[:, cg, :],
                      rhs=vtm[:, cg, :], start=True, stop=True)
            for c in range(8):
                cg = ti * 8 + c
                par = cg % 2
                pp, ppb = pst[c // 4]
                mk.op("dve", "scalar_tensor_tensor", reads=[Sb[1 - par], decb[ti], ppb], writes=[Sb[par]], out=S[:, par, :], in0=S[:, 1 - par, :],
                      scalar=dec[:, cg:cg + 1], in1=pp[:, (c % 4) * 128:(c % 4 + 1) * 128], op0=ALU.mult, op1=ALU.add)
                mk.op("act", "activation", reads=[Sb[par]], writes=[Sallb[ti]], out=Sall[:, cg + 1, :], in_=S[:, par, :], func=AF.Copy)
            st2[(hd, ti)] = si

        def stage2b(hd, ti):
            s_ = sl(ti)
            si = st2.pop((hd, ti))
            po, pob = self.next_ps_long()
            for c in range(8):
                cg = ti * 8 + c
                cs = slice(ti * 512 + c * 64, ti * 512 + (c + 1) * 64)
                prevb = S0b if cg == 0 else (Sallb[ti - 1] if c == 0 else Sallb[ti])
                mk.op("pe", "matmul", reads=[vtmb[ti], scsb[si]], writes=[pob], out=po[:, c * 64:(c + 1) * 64], lhsT=vtm[:, cg, :], rhs=scs[:, si, c, :],
                      start=True, stop=False)
                mk.op("pe", "matmul", reads=[prevb, qhb[ti]], writes=[pob], out=po[:, c * 64:(c + 1) * 64], lhsT=Sall[:, cg, :], rhs=qh[:, cs],
                      start=False, stop=True)
            yield
            oi = self.rr("oo", 2)
            mk.op("act", "activation", reads=[pob], writes=[oob[oi]], out=oo[:, oi, :], in_=po[:], func=AF.Copy)
            yield
            p2, p2b = self.sumsq_bcast([(oo[:, oi, :], oob[oi])], 512)
            yield
            r, rb = self.rstd_from(p2, p2b, 512, 128)
            yield
            i = self.rr("ybuf", 2)
            mk.op("dve", "scalar_tensor_tensor", reads=[oob[oi], gnb, rb], writes=[tmpb[i]], out=tmp[:, i, :], in0=oo[:, oi, :], scalar=gn[:, 0:1],
                  in1=r, op0=ALU.mult, op1=ALU.mult)
            yield
            mk.op("dve", "tensor_tensor", reads=[tmpb[i], gsb[ti]], writes=[self.ybufb[i]], out=self.ybuf[:, i, :], in0=tmp[:, i, :], in1=gs[:, s_], op=ALU.mult)
            self.store_y(hd, ti, i)

        units = [(hd, ti) for hd in range(NH) for ti in range(4)]
        pairs = [(units[2 * k], units[2 * k + 1]) for k in range(len(units) // 2)]
        prev = None
        for pr in pairs + [None]:
            if prev is not None:
                stage2a(*prev[0])
                stage2a(*prev[1])
            if pr is not None:
                gens = [stage1(*pr[0]), stage1(*pr[1])]
                while gens:
                    for g in list(gens):
                        try:
                            next(g)
                        except StopIteration:
                            gens.remove(g)
            if prev is not None:
                gens = [stage2b(*prev[0]), stage2b(*prev[1])]
                while gens:
                    for g in list(gens):
                        try:
                            next(g)
                        except StopIteration:
                            gens.remove(g)
            prev = pr
        self.finish()


class ProgLRU(ProgM):
    def __init__(self, shared=None, pfx="", io=None):
        super().__init__(shared, pfx, io)
        mk = self.mk
        NCH = 8
        win_d = self.din("w_in", [2 * NCH, 128, D])
        wax_d = self.din("w_ax", [2 * NCH, 128, 256])
        cw, cwb = self.const("conv_w", [128, 4, NCH])
        vec, vecb = self.const("vec", [128, 4, NCH])
        cl = mk.sbuf("clam", [128, 2, NCH], F32)
        clb = mk.buf("clam")
        one_t = mk.sbuf("one_t", [128, 1], F32)
        oneb = mk.buf("one_t")
        mk.op("dve", "memset", writes=[oneb], ap=one_t[:], constant=1.0)
        mk.op("act", "activation", reads=[vecb], writes=[clb], out=cl[:, 0, :], in_=vec[:, 3, :], func=AF.Exp, scale=-1.0)
        mk.op("act", "activation", reads=[clb, oneb], writes=[clb], out=cl[:, 0, :], in_=cl[:, 0, :], func=AF.Ln, bias=one_t[:, 0:1])
        mk.op("dve", "tensor_scalar", reads=[clb], writes=[clb], out=cl[:, 1, :], in0=cl[:, 0, :], scalar1=-16.0, scalar2=None, op0=ALU.mult)
        mk.op("dve", "tensor_scalar", reads=[clb], writes=[clb], out=cl[:, 0, :], in0=cl[:, 0, :], scalar1=-8.0, scalar2=None, op0=ALU.mult)

        def fbuf(name, dt=F32, n=4):
            return mk.sbuf(name, [128, SEQ], dt), mk.bufs(name, n)
        u, ub = fbuf("u")
        uc = [fbuf("uc%d" % i, F32, 1) for i in range(2)]
        u16 = [fbuf("u16_%d" % i, BF16, 1) for i in range(2)]
        r_, rb_ = fbuf("r")
        ig, igb = fbuf("ig")
        a_, ab_ = fbuf("a")
        t_, tb_ = fbuf("t")
        h_, hb_ = fbuf("h")
        gx, gxb = fbuf("gx")
        src, srcb = self.hsrc()
        T4 = range(4)

        def sl(ti):
            return slice(ti * 512, (ti + 1) * 512)

        def ev_act(dst, dstb, func, bias=None, extra=()):
            def evac(j, ti, p, pb):
                kw = {}
                if bias is not None:
                    kw["bias"] = bias
                mk.op("act", "activation", reads=[pb] + list(extra), writes=[dstb[ti]], out=dst[:, sl(ti)], in_=p[:], func=func, **kw)
            return evac

        for bl in range(4):
            for q in range(2):
                cc = bl * 2 + q
                ucq, ucqb = uc[q]
                self.linear_fm(win_d, [NCH + cc], DC, src, srcb, T4, ev_act(u, ub, AF.Copy))
                mk.op("dve", "tensor_scalar", reads=ub + [cwb, vecb], writes=[ucqb[0]], out=ucq[:, :], in0=u[:, :], scalar1=cw[:, 3, cc:cc + 1],
                      scalar2=vec[:, 0, cc:cc + 1], op0=ALU.mult, op1=ALU.add)
                for sh in (1, 2, 3):
                    mk.op("dve", "scalar_tensor_tensor", reads=ub + [cwb, ucqb[0]], writes=[ucqb[0]], out=ucq[:, sh:], in0=u[:, 0:SEQ - sh],
                          scalar=cw[:, 3 - sh, cc:cc + 1], in1=ucq[:, sh:], op0=ALU.mult, op1=ALU.add)
                mk.op("act", "activation", reads=[ucqb[0]], writes=[u16[q][1][0]], out=u16[q][0][:, :], in_=ucq[:, :], func=AF.Copy)
            for q in range(2):
                cc = bl * 2 + q
                ucq, ucqb = uc[q]

                def usrc(k, ti):
                    return u16[k][0][:, sl(ti)]

                def usrcb(k, ti):
                    return u16[k][1][0]
                self.linear_fm(wax_d, [cc], 2, usrc, usrcb, T4, ev_act(r_, rb_, AF.Sigmoid, bias=vec[:, 1, cc:cc + 1], extra=[vecb]))
                self.linear_fm(wax_d, [NCH + cc], 2, usrc, usrcb, T4, ev_act(ig, igb, AF.Sigmoid, bias=vec[:, 2, cc:cc + 1], extra=[vecb]))
                self.linear_fm(win_d, [cc], DC, src, srcb, T4, ev_act(gx, gxb, AF.Copy))
                def chain(ti, cc=cc, ucq=ucq, ucqb=ucqb):
                    s_ = sl(ti)
                    mk.op("act", "activation", reads=[rb_[ti], clb], writes=[ab_[ti]], out=a_[:, s_], in_=r_[:, s_], func=AF.Exp, scale=cl[:, 0, cc:cc + 1])
                    yield
                    mk.op("act", "activation", reads=[rb_[ti], clb], writes=[tb_[ti]], out=t_[:, s_], in_=r_[:, s_], func=AF.Exp, scale=cl[:, 1, cc:cc + 1])
                    yield
                    mk.op("dve", "tensor_scalar", reads=[tb_[ti]], writes=[tb_[ti]], out=t_[:, s_], in0=t_[:, s_], scalar1=-1.0, scalar2=1.0,
                          op0=ALU.mult, op1=ALU.add)
                    yield
                    mk.op("dve", "tensor_scalar", reads=[tb_[ti]], writes=[tb_[ti]], out=t_[:, s_], in0=t_[:, s_], scalar1=0.0, scalar2=None, op0=ALU.max)
                    yield
                    mk.op("act", "activation", reads=[tb_[ti]], writes=[tb_[ti]], out=t_[:, s_], in_=t_[:, s_], func=AF.Sqrt)
                    yield
                    mk.op("dve", "tensor_tensor", reads=[tb_[ti], igb[ti]], writes=[tb_[ti]], out=t_[:, s_], in0=t_[:, s_], in1=ig[:, s_], op=ALU.mult)
                    yield
                    mk.op("dve", "tensor_tensor", reads=[tb_[ti], ucqb[0]], writes=[tb_[ti]], out=t_[:, s_], in0=t_[:, s_], in1=ucq[:, s_], op=ALU.mult)
                    yield
                    init = 0.0 if ti == 0 else h_[:, ti * 512 - 1:ti * 512]
                    rd = [ab_[ti], tb_[ti]] + ([hb_[ti - 1]] if ti > 0 else [])
                    yield
                    mk.op("dve", "tensor_tensor_scan", reads=rd, writes=[hb_[ti]], out=h_[:, s_], data0=a_[:, s_], data1=t_[:, s_], initial=init,
                          op0=ALU.mult, op1=ALU.add)
                    yield
                    yield
                    mk.op("dve", "tensor_tensor", reads=[gxb[ti]], writes=[ab_[ti]], out=a_[:, s_], in0=gx[:, s_], in1=gx[:, s_], op=ALU.mult)
                    yield
                    mk.op("dve", "tensor_scalar", reads=[ab_[ti]], writes=[ab_[ti]], out=a_[:, s_], in0=a_[:, s_], scalar1=0.044715, scalar2=1.0,
                          op0=ALU.mult, op1=ALU.add)
                    yield
                    mk.op("dve", "tensor_tensor", reads=[ab_[ti], gxb[ti]], writes=[ab_[ti]], out=a_[:, s_], in0=a_[:, s_], in1=gx[:, s_], op=ALU.mult)
                    yield
                    mk.op("act", "activation", reads=[ab_[ti]], writes=[ab_[ti]], out=a_[:, s_], in_=a_[:, s_], func=AF.Sigmoid, scale=2.0 * 0.7978845608028654)
                    yield
                    mk.op("dve", "tensor_tensor", reads=[ab_[ti], gxb[ti]], writes=[ab_[ti]], out=a_[:, s_], in0=a_[:, s_], in1=gx[:, s_], op=ALU.mult)
                    yield
                    i = self.rr("ybuf", 2)
                    yield
                    mk.op("dve", "tensor_tensor", reads=[ab_[ti], hb_[ti]], writes=[self.ybufb[i]], out=self.ybuf[:, i, :], in0=a_[:, s_], in1=h_[:, s_], op=ALU.mult)
                    self.store_y(cc, ti, i)
                    yield
                gens = [chain(ti) for ti in T4]
                while gens:
                    for g in list(gens):
                        try:
                            next(g)
                        except StopIteration:
                            gens.remove(g)
        self.finish()


class ProgMLSTM(ProgM):
    def __init__(self, shared=None, pfx="", io=None):
        super().__init__(shared, pfx, io)
        mk = self.mk
        NH = 4
        NCK = 32
        win_d = self.din("w_in", [24, 128, D])
        wg, wgb = self.const("w_gate", [128, DC, 8])
        bif, bifb = self.const("b_if", [4, 2])
        gn, gnb = self.const("gn", [128, 2])
        self.gb = gnb
        sel, selb = self.const("sel", [4, 4 * 128])
        ident, identb = self.const("ident", [128, 128], BF16)
        mask, maskb = self.const("mask64", [64, 64])
        wg16 = mk.sbuf("wg16", [128, DC, 8], BF16)
        wg16b = mk.buf("wg16")
        mk.op("act", "activation", reads=[wgb], writes=[wg16b], out=wg16[:], in_=wg[:], func=AF.Copy)
        one4 = mk.sbuf("one4", [4, 1], F32)
        one4b = mk.buf("one4")
        mk.op("dve", "memset", writes=[one4b], ap=one4[:], constant=1.0)
        b15 = mk.sbuf("b15", [4, 2], F32)
        b15b = mk.buf("b15")
        mk.op("dve", "tensor_scalar", reads=[bifb], writes=[b15b], out=b15[:], in0=bif[:], scalar1=1.0 / 15.0, scalar2=None, op0=ALU.mult)
        src, srcb = self.hsrc()
        T4 = range(4)

        def sl(ti):
            return slice(ti * 512, (ti + 1) * 512)

        def row(name):
            return mk.sbuf(name, [4, SEQ], F32), mk.buf(name)
        it, itb = row("g_it")
        bn, bnb = row("g_bn")
        aa, aab = row("g_a")
        AA, AAb = row("g_A")
        tr, trb = row("g_tr")
        qf, qfb = it, itb
        kf, kfb = aa, aab
        em, emb = bn, bnb
        zr4 = mk.sbuf("g_zero", [4, 1], F32)
        zrb = mk.buf("g_zero")
        mk.op("dve", "memset", writes=[zrb], ap=zr4[:], constant=0.0)
        zr_bc = bass.AP(zr4, zr4[:, 0:1].offset, [[zr4[:, 0:1].ap[0][0], 4], [0, SEQ]])
        R33 = mk.sbuf("g_R33", [4, NCK + 1], F32)
        R33b = mk.buf("g_R33")
        dec4 = mk.sbuf("g_dec", [4, NCK], F32)
        dec4b = mk.buf("g_dec")
        for gi, dst, dstb in ((0, it, itb), (1, bn, bnb)):
            for ti in T4:
                p, pb = self.next_ps()
                for k in range(DC):
                    mk.op("pe", "matmul", reads=[wg16b, self.hnb[k][ti]], writes=[pb], out=p[0:4, :], lhsT=wg16[:, k, gi * 4:(gi + 1) * 4],
                          rhs=self.hn[:, k, sl(ti)], start=(k == 0), stop=(k == DC - 1))
                mk.op("act", "activation", reads=[pb, b15b], writes=[dstb], out=dst[:, sl(ti)], in_=p[0:4, :], func=AF.Tanh,
                      bias=b15[:, gi:gi + 1], scale=1.0 / 15.0)
        mk.op("dve", "tensor_scalar", reads=[itb], writes=[itb], out=it[:], in0=it[:], scalar1=15.0, scalar2=None, op0=ALU.mult)
        mk.op("act", "activation", reads=[bnb], writes=[bnb], out=bn[:], in_=bn[:], func=AF.Exp, scale=-15.0)
        mk.op("act", "activation", reads=[bnb, one4b], writes=[bnb], out=bn[:], in_=bn[:], func=AF.Ln, bias=one4[:, 0:1])
        mk.op("dve", "tensor_tensor_scan", reads=[bnb, zrb], writes=[trb], out=tr[:], data0=bn[:], data1=zr_bc, initial=0.0, op0=ALU.add, op1=ALU.add)
        mk.op("dve", "tensor_tensor", reads=[trb, itb], writes=[aab], out=aa[:], in0=tr[:], in1=it[:], op=ALU.add)
        mk.op("dve", "tensor_tensor_scan", reads=[aab], writes=[AAb], out=AA[:], data0=aa[:], data1=aa[:], initial=0.0, op0=ALU.max, op1=ALU.max)
        pstep = AA[:, 0:1].ap[0][0]
        Rbc = bass.AP(AA, AA[:, 63:64].offset, [[pstep, 4], [64, NCK], [0, 64]])
        Rv = bass.AP(AA, AA[:, 63:64].offset, [[pstep, 4], [64, NCK]])
        mk.op("dve", "tensor_tensor", reads=[trb, AAb], writes=[emb], out=em[:], in0=tr[:], in1=AA[:], op=ALU.subtract)
        mk.op("act", "activation", reads=[emb], writes=[emb], out=em[:], in_=em[:], func=AF.Exp)
        mk.op("dve", "tensor_tensor", reads=[AAb], writes=[qfb], out=v3(qf[:], NCK, 64), in0=Rbc, in1=v3(AA[:], NCK, 64), op=ALU.subtract)
        mk.op("act", "activation", reads=[qfb], writes=[qfb], out=qf[:], in_=qf[:], func=AF.Exp)
        mk.op("dve", "tensor_tensor", reads=[AAb, aab], writes=[kfb], out=v3(kf[:], NCK, 64), in0=v3(aa[:], NCK, 64), in1=Rbc, op=ALU.subtract)
        mk.op("act", "activation", reads=[kfb], writes=[kfb], out=kf[:], in_=kf[:], func=AF.Exp)
        mk.op("dve", "tensor_scalar", reads=[kfb], writes=[kfb], out=kf[:], in0=kf[:], scalar1=128.0 ** -0.5, scalar2=None, op0=ALU.mult)
        mk.op("dve", "memset", writes=[R33b], ap=R33[:, 0:1], constant=0.0)
        mk.op("dve", "tensor_copy", reads=[AAb, R33b], writes=[R33b], out=R33[:, 1:NCK + 1], in_=Rv)
        mk.op("dve", "tensor_tensor", reads=[R33b], writes=[dec4b], out=dec4[:], in0=R33[:, 0:NCK], in1=R33[:, 1:NCK + 1], op=ALU.subtract)
        mk.op("act", "activation", reads=[dec4b], writes=[dec4b], out=dec4[:], in_=dec4[:], func=AF.Exp)

        def fbuf(name, dt=F32, n=4):
            return mk.sbuf(name, [128, SEQ], dt), mk.bufs(name, n)
        fb, fbb = fbuf("fb")
        qt, qtb = fbuf("qt", BF16)
        kt, ktb = fbuf("kt", BF16)
        vf, vfb = fbuf("vf", BF16)
        og = mk.sbuf("og", [128, 2, SEQ], BF16)
        ogb = mk.bufs("og", 2, 4)
        ktm = mk.sbuf("ktm", [64, NCK, 128], BF16)
        ktmb = mk.bufs("ktm", 4)
        vtm = mk.sbuf("vtm", [64, NCK, 384], BF16)
        vtmb = mk.bufs("vtm", 4)
        mk.op("dve", "memset", writes=vtmb, ap=vtm[:, :, 256:384], constant=1.0)
        scs = mk.sbuf("scs", [64, 2, 8, 64], BF16)
        scsb = mk.bufs("scs", 2)
        decb_ = mk.sbuf("decb", [128, NCK], F32)
        decbb = mk.buf("decb")
        C = mk.sbuf("C", [128, 2, 384], F32)
        Cb = mk.bufs("C", 2)
        Cd = mk.sbuf("Cd", [128, 2, 384], BF16)
        Cdb = mk.bufs("Cd", 2)
        nm = mk.sbuf("nm", [128, 2, 512], F32)
        nmb = mk.bufs("nm", 2)
        dn = mk.sbuf("dn", [128, 512], F32)
        dnb = mk.buf("dn")

        def bcast(rowt, rowb, hd, ti, dst_ap, dstb):
            p, pb = self.next_ps()
            mk.op("pe", "matmul", reads=[selb, rowb], writes=[pb], out=p[:], lhsT=sel[0:4, hd * 128:(hd + 1) * 128], rhs=rowt[0:4, sl(ti)],
                  start=True, stop=True)
            mk.op("act", "activation", reads=[pb], writes=[dstb], out=dst_ap, in_=p[:], func=AF.Copy)

        for hd in range(NH):
            for ti in T4:
                bcast(qf, qfb, hd, ti, fb[:, sl(ti)], fbb[ti])

            def ev_mul(dst, dstb):
                def evac(j, ti, p, pb):
                    mk.op("dve", "tensor_tensor", reads=[pb, fbb[ti]], writes=[dstb[ti]], out=dst[:, sl(ti)], in0=p[:], in1=fb[:, sl(ti)], op=ALU.mult)
                return evac
            self.linear_fm(win_d, [hd], DC, src, srcb, T4, ev_mul(qt, qtb))
            for ti in T4:
                bcast(kf, kfb, hd, ti, fb[:, sl(ti)], fbb[ti])
            self.linear_fm(win_d, [4 + hd], DC, src, srcb, T4, ev_mul(kt, ktb))
            p, pb = self.next_ps()
            mk.op("pe", "matmul", reads=[selb, dec4b], writes=[pb], out=p[:, 0:NCK], lhsT=sel[0:4, hd * 128:(hd + 1) * 128], rhs=dec4[0:4, :], start=True, stop=True)
            mk.op("act", "activation", reads=[pb], writes=[decbb], out=decb_[:], in_=p[:, 0:NCK], func=AF.Copy)
            for ti in T4:
                p, pb = self.next_ps()
                pv = p.bitcast(BF16)
                for c in range(8):
                    cs = slice(ti * 512 + c * 64, ti * 512 + (c + 1) * 64)
                    mk.op("pe", "transpose", reads=[ktb[ti], identb], writes=[pb], out=pv[0:64, c * 128:(c + 1) * 128], in_=kt[:, cs], identity=ident[:])
                mk.op("act", "activation", reads=[pb], writes=[ktmb[ti]], out=ktm[:, ti * 8:(ti + 1) * 8, :],
                      in_=pv[0:64, :].rearrange("p (c i) -> p c i", i=128), func=AF.Copy)
            for vc in range(2):
                def evac_v(j, ti, p, pb):
                    mk.op("act", "activation", reads=[pb], writes=[vfb[ti]], out=vf[:, sl(ti)], in_=p[:], func=AF.Copy)
                self.linear_fm(win_d, [8 + hd * 2 + vc], DC, src, srcb, T4, evac_v)
                for ti in T4:
                    p, pb = self.next_ps()
                    pv = p.bitcast(BF16)
                    for c in range(8):
                        cs = slice(ti * 512 + c * 64, ti * 512 + (c + 1) * 64)
                        mk.op("pe", "transpose", reads=[vfb[ti], identb], writes=[pb], out=pv[0:64, c * 128:(c + 1) * 128], in_=vf[:, cs], identity=ident[:])
                    mk.op("act", "activation", reads=[pb], writes=[vtmb[ti]], out=vtm[:, ti * 8:(ti + 1) * 8, vc * 128:(vc + 1) * 128],
                          in_=pv[0:64, :].rearrange("p (c i) -> p c i", i=128), func=AF.Copy)

                def evac_o(j, ti, p, pb, vc=vc):
                    mk.op("act", "activation", reads=[pb], writes=[ogb[vc][ti]], out=og[:, vc, sl(ti)], in_=p[:], func=AF.Sigmoid)
                self.linear_fm(win_d, [16 + hd * 2 + vc], DC, src, srcb, T4, evac_o)
            mk.op("dve", "memset", writes=[Cb[1]], ap=C[:, 1, :], constant=0.0)
            for ti in T4:
                p, pb = self.next_ps()
                for c in range(8):
                    cs = slice(ti * 512 + c * 64, ti * 512 + (c + 1) * 64)
                    mk.op("pe", "matmul", reads=[ktb[ti], qtb[ti]], writes=[pb], out=p[0:64, c * 64:(c + 1) * 64], lhsT=kt[:, cs], rhs=qt[:, cs], start=True, stop=True)
                si = self.rr("scs", 2)
                mbc = bass.AP(mask, mask[:, 0:1].offset, [[mask[:, 0:1].ap[0][0], 64], [0, 8], [1, 64]])
                mk.op("dve", "tensor_tensor", reads=[pb, maskb], writes=[scsb[si]], out=scs[:, si, :, :], in0=v3(p[0:64, :], 8, 64), in1=mbc, op=ALU.mult)
                pos = [self.next_ps_long() for _ in range(3)]
                pstq = {}

                def emit_pst(c):
                    cg = ti * 8 + c
                    pst, pstb = self.next_ps()
                    mk.op("pe", "matmul", reads=[ktmb[ti], vtmb[ti]], writes=[pstb], out=pst[:, 0:384], lhsT=ktm[:, cg, :], rhs=vtm[:, cg, :], start=True, stop=True)
                    pstq[c] = (pst, pstb)
                emit_pst(0)
                emit_pst(1)
                for c in range(8):
                    cg = ti * 8 + c
                    par = cg % 2
                    ci = c % 2
                    cs = slice(ti * 512 + c * 64, ti * 512 + (c + 1) * 64)
                    mk.op("act", "activation", reads=[Cb[1 - par], decbb], writes=[Cdb[ci]], out=Cd[:, ci, :], in_=C[:, 1 - par, :], func=AF.Copy,
                          scale=decb_[:, cg:cg + 1])
                    for oc in range(3):
                        po, pob = pos[oc]
                        mk.op("pe", "matmul", reads=[vtmb[ti], scsb[si]], writes=[pob], out=po[:, c * 64:(c + 1) * 64], lhsT=vtm[:, cg, oc * 128:(oc + 1) * 128],
                              rhs=scs[:, si, c, :], start=True, stop=False)
                        mk.op("pe", "matmul", reads=[Cdb[ci], qtb[ti]], writes=[pob], out=po[:, c * 64:(c + 1) * 64], lhsT=Cd[:, ci, oc * 128:(oc + 1) * 128],
                              rhs=qt[:, cs], start=False, stop=True)
                    if c + 2 < 8:
                        emit_pst(c + 2)
                    pst, pstb = pstq.pop(c)
                    mk.op("dve", "scalar_tensor_tensor", reads=[Cb[1 - par], decbb, pstb], writes=[Cb[par]], out=C[:, par, :], in0=C[:, 1 - par, :],
                          scalar=decb_[:, cg:cg + 1], in1=pst[:, 0:384], op0=ALU.mult, op1=ALU.add)
                pem, pemb = self.next_ps()
                mk.op("pe", "matmul", reads=[selb, emb], writes=[pemb], out=pem[:], lhsT=sel[0:4, hd * 128:(hd + 1) * 128], rhs=em[0:4, sl(ti)],
                      start=True, stop=True)
                mk.op("act", "activation", reads=[pos[2][1]], writes=[dnb], out=dn[:], in_=pos[2][0][:], func=AF.Copy)
                mk.op("dve", "scalar_tensor_tensor", reads=[dnb], writes=[dnb], out=dn[:], in0=dn[:], scalar=-1.0, in1=dn[:], op0=ALU.mult, op1=ALU.max)
                mk.op("dve", "tensor_tensor", reads=[dnb, pemb], writes=[dnb], out=dn[:], in0=dn[:], in1=pem[:], op=ALU.max)
                mk.op("dve", "reciprocal", reads=[dnb], writes=[dnb], out=dn[:], in_=dn[:])
                for oc in range(2):
                    mk.op("dve", "tensor_tensor", reads=[pos[oc][1], dnb], writes=[nmb[oc]], out=nm[:, oc, :], in0=pos[oc][0][:], in1=dn[:], op=ALU.mult)
                p2, p2b = self.sumsq_bcast([(nm[:, oc, :], nmb[oc]) for oc in range(2)], 512)
                r, rb = self.rstd_from(p2, p2b, 512, 256)
                for oc in range(2):
                    i = self.rr("ybuf", 2)
                    mk.op("dve", "scalar_tensor_tensor", reads=[nmb[oc], gnb, rb], writes=[nmb[oc]], out=nm[:, oc, :], in0=nm[:, oc, :], scalar=gn[:, oc:oc + 1],
                          in1=r, op0=ALU.mult, op1=ALU.mult)
                    mk.op("dve", "tensor_tensor", reads=[nmb[oc], ogb[oc][ti]], writes=[self.ybufb[i]], out=self.ybuf[:, i, :], in0=nm[:, oc, :], in1=og[:, oc, sl(ti)], op=ALU.mult)
                    self.store_y(hd * 2 + oc, ti, i)
        self.finish()


def mlstm_inputs(inp, idx, hh):
    W = inp["ml_w_in"][idx]
    wt = tile_w(W[:, :6144])
    h0 = hh * 4
    w_in = np.concatenate([wt[h0:h0 + 4], wt[8 + h0:8 + h0 + 4], wt[16 + h0 * 2:16 + h0 * 2 + 8], wt[32 + h0 * 2:32 + h0 * 2 + 8]], axis=0)
    gcols = np.concatenate([W[:, 6144 + h0:6144 + h0 + 4], W[:, 6152 + h0:6152 + h0 + 4]], axis=1)
    w_gate = np.ascontiguousarray(gcols.reshape(DC, 128, 8).transpose(1, 0, 2))
    b_if = np.ascontiguousarray(inp["ml_b_if"][idx][:, h0:h0 + 4].T)
    gn = fmv(inp["ml_g_norm"][idx])
    sel = np.zeros((4, 4 * 128), np.float32)
    for h in range(4):
        sel[h, h * 128:(h + 1) * 128] = 1.0
    return {"w_in": w_in, "w_gate": w_gate, "b_if": b_if, "gn": gn, "sel": sel}


def lru_inputs(inp, idx, hh):
    wt = tile_w(inp["lru_w_in"][idx])
    w_in = np.concatenate([wt[hh * 8:hh * 8 + 8], wt[16 + hh * 8:16 + hh * 8 + 8]], axis=0)
    wa = np.concatenate([tile_w(inp["lru_w_a"][idx, hh * 4 + b]) for b in range(4)], axis=0)
    wx = np.concatenate([tile_w(inp["lru_w_x"][idx, hh * 4 + b]) for b in range(4)], axis=0)
    sl = slice(hh * 1024, (hh + 1) * 1024)
    conv_w = np.ascontiguousarray(inp["lru_conv_w"][idx][:, sl].reshape(4, 8, 128).transpose(2, 0, 1))
    vec = np.stack([fmv(inp[k][idx][sl]) for k in ("lru_conv_b", "lru_b_a", "lru_b_x", "lru_lambda")], axis=1)
    return {"w_in": w_in, "w_ax": np.concatenate([wa, wx], axis=0), "conv_w": conv_w, "vec": np.ascontiguousarray(vec)}


def hgrn_inputs(inp, idx, hh):
    wt = tile_w(inp["hg_w_in"][idx])
    sel = np.concatenate([wt[kind * 16 + hh * 8: kind * 16 + hh * 8 + 8] for kind in range(4)], axis=0)
    lbp = np.ascontiguousarray(inp["hg_lb_param"][:, hh * 1024:(hh + 1) * 1024].reshape(4, 8, 128).transpose(2, 0, 1))
    return {"w_in": sel, "lbp": lbp, "gn": np.ascontiguousarray(inp["hg_g_norm"][idx].reshape(128, 1))}


class Fused(Prog):
    def __init__(self, depth=DEPTH, ncores=8):
        super().__init__()
        nc, mk = self.nc, self.mk
        rg = [[2 * i, 2 * i + 1] for i in range(ncores // 2)]

        def dbuf(name, shape, dt):
            return nc.dram_tensor(name, list(shape), dt).ap(), mk.buf(name)
        xs = [dbuf("xs%d" % l, [D, TA], F32) for l in range(depth)]
        hn_p = [[dbuf("hnp%d_%d" % (l, pc), [D, 512], BF16) for pc in range(2)] for l in range(depth)]
        hn_g = [[dbuf("hng%d_%d" % (l, pc), [2 * D, 512], BF16) for pc in range(2)] for l in range(depth)]
        y_p = [[dbuf("yp%d_%d" % (l, pc), [512, SEQ], BF16) for pc in range(2)] for l in range(depth)]
        y_g = [[dbuf("yg%d_%d" % (l, pc), [1024, SEQ], BF16) for pc in range(2)] for l in range(depth)]
        for l in range(depth + 1):
            io = {"rg": rg}
            if l > 0:
                io["x_src"] = xs[l - 1]
                io["y_g"] = y_g[l - 1]
            if l < depth:
                io["x_dst"] = xs[l]
                io["hn_p"] = hn_p[l]
                io["hn_g"] = hn_g[l]
            ProgA(first=(l == 0), last=(l == depth), shared=self, pfx="A%d_" % l, io=io)
            if l < depth:
                iom = {"rg": rg, "hn_g": hn_g[l], "y_p": y_p[l], "y_g": y_g[l]}
                kind = l % 3
                if kind == 0:
                    ProgHGRN(l, shared=self, pfx="M%d_" % l, io=iom)
                elif kind == 1:
                    ProgLRU(shared=self, pfx="M%d_" % l, io=iom)
                else:
                    ProgMLSTM(shared=self, pfx="M%d_" % l, io=iom)
        mk.finalize()


def fused_inputs(inp, depth=DEPTH, ncores=8, final_g=None):
    cst = consts_host()
    ng_all = inp["norm_g"]
    final_g = inp["final_norm_g"] if final_g is None else final_g
    mixer_wout = [inp["hg_w_out"][0], inp["lru_w_out"][0], inp["ml_w_out"][0], inp["hg_w_out"][1]]
    common = {"ident": cst["ident"], "mask64": cst["mask64"], "memg": fmv(inp["mem_norm_g"])}
    per_half = [dict(), dict()]
    for l in range(depth + 1):
        p = "A%d_" % l
        if l == 0:
            ng = np.stack([fmv(ng_all[0, i]) for i in (0, 0, 0, 1)], axis=1)
        elif l == depth:
            ng = np.stack([fmv(ng_all[l - 1, 2]), fmv(ng_all[l - 1, 3]), fmv(final_g), fmv(final_g)], axis=1)
        else:
            ng = np.stack([fmv(ng_all[l - 1, 2]), fmv(ng_all[l - 1, 3]), fmv(ng_all[l, 0]), fmv(ng_all[l, 1])], axis=1)
        common[p + "ng"] = np.ascontiguousarray(ng)
        if l > 0:
            common[p + "w_mo"] = tile_w(mixer_wout[l - 1])
            common[p + "w_q"] = tile_w(inp["xa_w_q"][l - 1])
            common[p + "w_kv"] = tile_w(inp["xa_w_kv"][l - 1])
            common[p + "w_o"] = tile_w(inp["xa_w_o"][l - 1])
            common[p + "w_gu2"] = tile_w(inp["ffn_w_gu"][l - 1, 1])
            common[p + "w_dn2"] = tile_w(inp["ffn_w_down"][l - 1, 1])
        if l < depth:
            common[p + "w_gu1"] = tile_w(inp["ffn_w_gu"][l, 0])
            common[p + "w_dn1"] = tile_w(inp["ffn_w_down"][l, 0])
            kind, idx = l % 3, l // 3
            for hh in range(2):
                if kind == 0:
                    m = hgrn_inputs(inp, idx, hh)
                elif kind == 1:
                    m = lru_inputs(inp, idx, hh)
                else:
                    m = mlstm_inputs(inp, idx, hh)
                for k, v in m.items():
                    per_half[hh]["M%d_%s" % (l, k)] = v
    in_maps = []
    for c in range(ncores):
        b, r = divmod(c, 2)
        m = dict(common)
        m.update(per_half[r])
        m["A0_xT"] = np.ascontiguousarray(inp["x"][b, r * TA:(r + 1) * TA].T)
        m["memT"] = np.ascontiguousarray(inp["mem"][b].T)
        in_maps.append(m)
    return in_maps


_PROGS = {}


def _prog(key, ctor):
    if key not in _PROGS:
        _PROGS[key] = ctor()
    return _PROGS[key]


def _run(prog, in_maps):
    for m in in_maps:
        for k, (shape, dt) in prog.in_specs.items():
            assert k in m, k
            assert tuple(m[k].shape) == shape, (k, m[k].shape, shape)
    in_maps = [{k: m[k] for k in prog.in_specs} for m in in_maps]
    res = run_bass_kernel_spmd(prog.nc, in_maps, core_ids=list(range(len(in_maps))))
    return res.results


def _run_timed(prog, in_maps):
    in_maps = [{k: m[k] for k in prog.in_specs} for m in in_maps]
    res = run_bass_kernel_spmd(prog.nc, in_maps, core_ids=list(range(len(in_maps))), trace=True)
    print("exec_time_ns", res.exec_time_ns)
    return res.results


def kernel(**inp):
    inp = {k: np.asarray(v) for k, v in inp.items()}
    prog = _prog("fused", Fused)
    res = _run(prog, fused_inputs(inp))
    out = np.empty((BATCH, SEQ, D), np.float32)
    for c in range(8):
        b, tc = divmod(c, 2)
        out[b, tc * TA:(tc + 1) * TA] = res[c]["A%d_outT" % DEPTH].T
    return out


def kernel_unfused(**inp):
    inp = {k: np.asarray(v) for k, v in inp.items()}
    cst = consts_host()
    NC = 8
    x = inp["x"]
    ng_all = inp["norm_g"]
    mixer_wout = [inp["hg_w_out"][0], inp["lru_w_out"][0], inp["ml_w_out"][0], inp["hg_w_out"][1]]

    def ffn_w(layer, f):
        return tile_w(inp["ffn_w_gu"][layer, f]), tile_w(inp["ffn_w_down"][layer, f])

    pa = _prog("A0", lambda: ProgA(first=True, last=False))
    ng = np.ascontiguousarray(np.stack([fmv(ng_all[0, i]) for i in (0, 0, 0, 1)], axis=1))
    wgu, wdn = ffn_w(0, 0)
    in_maps = []
    for c in range(NC):
        b, tc = divmod(c, 2)
        in_maps.append({"xT": np.ascontiguousarray(x[b, tc * TA:(tc + 1) * TA].T), "ng": ng, "w_gu1": wgu, "w_dn1": wdn})
    res = _run(pa, in_maps)
    xT = [r["xT_out"] for r in res]
    hnT = [r["hnT_out"] for r in res]
    del wgu, wdn
    out = None
    for layer in range(DEPTH):
        kind, idx = layer % 3, layer // 3
        in_maps = []
        for c in range(NC):
            b, hh = divmod(c, 2)
            hn_full = np.ascontiguousarray(np.concatenate([hnT[b * 2], hnT[b * 2 + 1]], axis=1))
            if kind == 0:
                m = dict(hgrn_inputs(inp, idx, hh), ident=cst["ident"], mask64=cst["mask64"])
            elif kind == 1:
                m = lru_inputs(inp, idx, hh)
            else:
                m = dict(mlstm_inputs(inp, idx, hh), ident=cst["ident"], mask64=cst["mask64"])
            m["hnT"] = hn_full
            in_maps.append(m)
        if kind == 0:
            pm = _prog("HGRN%d" % layer, lambda: ProgHGRN(layer))
        elif kind == 1:
            pm = _prog("LRU", ProgLRU)
        else:
            pm = _prog("MLSTM", ProgMLSTM)
        res = _run(pm, in_maps)
        yT = [r["yT_out"] for r in res]
        last = layer == DEPTH - 1
        pa = _prog("Alast" if last else "Amid", lambda: ProgA(first=False, last=last))
        if last:
            ng = np.stack([fmv(ng_all[layer, 2]), fmv(ng_all[layer, 3]), fmv(inp["final_norm_g"]), fmv(inp["final_norm_g"])], axis=1)
        else:
            ng = np.stack([fmv(ng_all[layer, 2]), fmv(ng_all[layer, 3]), fmv(ng_all[layer + 1, 0]), fmv(ng_all[layer + 1, 1])], axis=1)
        wgu2, wdn2 = ffn_w(layer, 1)
        common = {"ng": np.ascontiguousarray(ng), "memg": fmv(inp["mem_norm_g"]), "ident": cst["ident"],
                  "w_mo": tile_w(mixer_wout[layer]), "w_q": tile_w(inp["xa_w_q"][layer]), "w_kv": tile_w(inp["xa_w_kv"][layer]),
                  "w_o": tile_w(inp["xa_w_o"][layer]), "w_gu2": wgu2, "w_dn2": wdn2}
        if not last:
            wgu1, wdn1 = ffn_w(layer + 1, 0)
            common.update({"w_gu1": wgu1, "w_dn1": wdn1})
        in_maps = []
        for c in range(NC):
            b, tc = divmod(c, 2)
            y_c = np.ascontiguousarray(np.concatenate([yT[b * 2][:, tc * TA:(tc + 1) * TA], yT[b * 2 + 1][:, tc * TA:(tc + 1) * TA]], axis=0))
            in_maps.append(dict(common, xT=xT[c], yT=y_c, memT=np.ascontiguousarray(inp["mem"][b].T)))
        res = _run(pa, in_maps)
        if last:
            out = np.empty((BATCH, SEQ, D), np.float32)
            for c in range(NC):
                b, tc = divmod(c, 2)
                out[b, tc * TA:(tc + 1) * TA] = res[c]["outT"].T
        else:
            xT = [r["xT_out"] for r in res]
            hnT = [r["hnT_out"] for r in res]
    return out
```
